# Optimizing a Trainium2 kernel written in Bass

```python
import jax
import jax.numpy as jnp
from jax import lax
import numpy as np

D_MODEL = 4096
BATCH = 4
SEQ = 4096
DEPTH = 1

CTX_LEN = 256
GRID_W = 64
HEAD_DIM = 128
N_MLP_GROUPS = (D_MODEL // 2) // HEAD_DIM
MLP_WIDTH = N_MLP_GROUPS * HEAD_DIM
N_Q_HEADS = (D_MODEL // 2) // HEAD_DIM
N_KV_HEADS = N_Q_HEADS // 4
Q_PER_KV = N_Q_HEADS // N_KV_HEADS
ATTN_WIDTH = N_Q_HEADS * HEAD_DIM
KV_WIDTH = N_KV_HEADS * HEAD_DIM
MIX_WIDTH = MLP_WIDTH + ATTN_WIDTH
IN_WIDTH = 2 * MLP_WIDTH + ATTN_WIDTH + 2 * KV_WIDTH
SPLITS = (MLP_WIDTH, 2 * MLP_WIDTH, 2 * MLP_WIDTH + ATTN_WIDTH, 2 * MLP_WIDTH + ATTN_WIDTH + KV_WIDTH)
CHUNK = 128
WINDOW = 128
BLOCK = 128
D_FF = ((8 * D_MODEL // 3 + 255) // 256) * 256
N_SUB = 3
N_MOD = 3
ROPE_BASE = 10000.0
EPS = 1e-6
ALPHA = (2.0 * DEPTH) ** 0.25
BETA = (8.0 * DEPTH) ** -0.25

kernel_name = 'hymba_gmlp_swa_deepnorm_dit_block'


def standardize(x):
    xf = x.astype(jnp.float32)
    mu = jnp.mean(xf, -1, keepdims=True)
    var = jnp.mean(jnp.square(xf - mu), -1, keepdims=True)
    return (xf - mu) * lax.rsqrt(var + EPS)


def layer_norm(x, gain, bias):
    y = standardize(x) * gain.astype(jnp.float32) + bias.astype(jnp.float32)
    return y.astype(x.dtype)


def rms_normalize(y):
    yf = y.astype(jnp.float32)
    return yf * lax.rsqrt(jnp.mean(jnp.square(yf), -1, keepdims=True) + EPS)


def modulate(h, shift, scale):
    return h * (1.0 + scale) + shift


def mod_terms(m, s):
    return m[:, s, 0][:, None, :], m[:, s, 1][:, None, :], m[:, s, 2][:, None, :]


def swiglu(h, w_gate, w_up, w_down):
    return (jax.nn.silu(h @ w_gate) * (h @ w_up)) @ w_down


def ffn_sublayer(h, m, s, w_gate, w_up, w_down, gain, bias):
    shift, scale, gate = mod_terms(m, s)
    y = swiglu(modulate(h, shift, scale), w_gate, w_up, w_down)
    return layer_norm(ALPHA * h + 0.5 * gate * y, gain, bias)


def split_heads(t, n_heads):
    return t.reshape(t.shape[:-1] + (n_heads, HEAD_DIM))


def project(h, w):
    u, v, q, k, va = jnp.split(h @ w, SPLITS, axis=-1)
    return (jax.nn.gelu(u, approximate=False), jax.nn.gelu(v, approximate=False),
            split_heads(q, N_Q_HEADS), split_heads(k, N_KV_HEADS), split_heads(va, N_KV_HEADS))


def axial_rope_tables(n):
    rows = n // GRID_W
    row = jnp.broadcast_to(jnp.arange(rows, dtype=jnp.float32)[:, None], (rows, GRID_W)).reshape(n)
    col = jnp.broadcast_to(jnp.arange(GRID_W, dtype=jnp.float32)[None, :], (rows, GRID_W)).reshape(n)
    n_freq = HEAD_DIM // 4
    inv_freq = ROPE_BASE ** (-jnp.arange(n_freq, dtype=jnp.float32) / n_freq)
    ang = jnp.concatenate([row[:, None] * inv_freq, col[:, None] * inv_freq], axis=-1)
    return jnp.cos(ang), jnp.sin(ang)


def apply_rope(t, cos, sin):
    tf = t.astype(jnp.float32)
    t1, t2 = jnp.split(tf, 2, axis=-1)
    cs, sn = cos[None, :, None, :], sin[None, :, None, :]
    return jnp.concatenate([t1 * cs - t2 * sn, t2 * cs + t1 * sn], axis=-1).astype(t.dtype)


def chunk_spatial_gating(u, v, w_s, b_s):
    b, n, _ = u.shape
    vn = standardize(v).astype(v.dtype).reshape(b, n // CHUNK, CHUNK, N_MLP_GROUPS, HEAD_DIM)
    mixed = jnp.einsum('gpq,bcqgd->bcpgd', w_s, vn) + b_s.T[:, :, None]
    return u * mixed.reshape(u.shape)


def sink_attention(q, k, v, sink, mask):
    s = jnp.einsum('bqhgd,bkhd->bhgqk', q, k).astype(jnp.float32) * (HEAD_DIM ** -0.5)
    if mask is not None:
        s = jnp.where(mask, s, -jnp.inf)
    sink_col = jnp.broadcast_to(sink.astype(jnp.float32)[None, :, :, None, None], s.shape[:-1] + (1,))
    p = jax.nn.softmax(jnp.concatenate([sink_col, s], axis=-1), axis=-1)[..., 1:]
    return jnp.einsum('bhgqk,bkhd->bqhgd', p.astype(v.dtype), v)


def windowed_context_attention(q, k, v, kc, vc, sink):
    b, n = q.shape[:2]
    nb = n // BLOCK
    c_len = kc.shape[1]
    qb = jnp.moveaxis(q.reshape(b, nb, BLOCK, N_KV_HEADS, Q_PER_KV, HEAD_DIM), 1, 0)
    pad = ((0, 0), (BLOCK, BLOCK), (0, 0), (0, 0))
    kp, vp = jnp.pad(k, pad), jnp.pad(v, pad)
    q_off = jnp.arange(BLOCK)[:, None]
    k_off = jnp.arange(3 * BLOCK)[None, :] - BLOCK
    in_band = jnp.abs(k_off - q_off) <= WINDOW
    ctx_mask = jnp.ones((BLOCK, c_len), dtype=bool)

    def one_block(args):
        qi, i = args
        kw = lax.dynamic_slice_in_dim(kp, i * BLOCK, 3 * BLOCK, axis=1)
        vw = lax.dynamic_slice_in_dim(vp, i * BLOCK, 3 * BLOCK, axis=1)
        kpos = i * BLOCK + k_off
        mask = jnp.concatenate([ctx_mask, in_band & (kpos >= 0) & (kpos < n)], axis=1)
        return sink_attention(qi, jnp.concatenate([kc, kw], axis=1), jnp.concatenate([vc, vw], axis=1), sink, mask)

    out = lax.map(one_block, (qb, jnp.arange(nb)))
    return jnp.moveaxis(out, 0, 1).reshape(b, n, ATTN_WIDTH)


def merge_mixers(y_mlp, y_attn, g_mix, w_out):
    y = jnp.concatenate([rms_normalize(y_mlp), rms_normalize(y_attn)], axis=-1) * g_mix.astype(jnp.float32)
    return y.astype(w_out.dtype) @ w_out


def setup_inputs(seed: int = 0) -> dict:
    key = jax.random.key(seed)
    ks = jax.random.split(key, 17)

    def nrm(k, shape, scale):
        return jax.random.normal(k, shape, jnp.float32) * scale

    return {
        'x': nrm(ks[0], (BATCH, SEQ, D_MODEL), 1.0),
        'c': nrm(ks[1], (BATCH, D_MODEL), 1.0),
        'ctx': nrm(ks[2], (BATCH, CTX_LEN, D_MODEL), 1.0),
        'c_ctx': nrm(ks[3], (D_MODEL,), 1.0),
        'w_ada': nrm(ks[4], (DEPTH, D_MODEL, N_SUB * N_MOD * D_MODEL), 0.5 * D_MODEL ** -0.5),
        'b_ada': nrm(ks[5], (DEPTH, N_SUB * N_MOD * D_MODEL), 0.02),
        'w_ffn_gate': nrm(ks[6], (DEPTH, 2, D_MODEL, D_FF), D_MODEL ** -0.5),
        'w_ffn_up': nrm(ks[7], (DEPTH, 2, D_MODEL, D_FF), D_MODEL ** -0.5),
        'w_ffn_down': nrm(ks[8], (DEPTH, 2, D_FF, D_MODEL), BETA * D_FF ** -0.5),
        'w_in': nrm(ks[9], (DEPTH, D_MODEL, IN_WIDTH), D_MODEL ** -0.5),
        'w_spatial': nrm(ks[10], (DEPTH, N_MLP_GROUPS, CHUNK, CHUNK), CHUNK ** -0.5),
        'b_spatial': 1.0 + nrm(ks[11], (DEPTH, N_MLP_GROUPS, CHUNK), 0.02),
        'sink_logit': nrm(ks[12], (DEPTH, N_Q_HEADS), 0.5),
        'g_mix': 1.0 + nrm(ks[13], (DEPTH, MIX_WIDTH), 0.02),
        'w_out': nrm(ks[14], (DEPTH, MIX_WIDTH, D_MODEL), BETA * MIX_WIDTH ** -0.5),
        'ln_gain': 1.0 + nrm(ks[15], (DEPTH, N_SUB, D_MODEL), 0.02),
        'ln_bias': nrm(ks[16], (DEPTH, N_SUB, D_MODEL), 0.02),
    }


def reference(x, c, ctx, c_ctx, w_ada, b_ada, w_ffn_gate, w_ffn_up, w_ffn_down, w_in, w_spatial,
              b_spatial, sink_logit, g_mix, w_out, ln_gain, ln_bias):
    b, n, d = x.shape
    c_len = ctx.shape[1]
    cos, sin = axial_rope_tables(n)
    for layer in range(DEPTH):
        update_ctx = layer < DEPTH - 1
        m_x = (jax.nn.silu(c) @ w_ada[layer] + b_ada[layer]).reshape(b, N_SUB, N_MOD, d)
        m_c = (jax.nn.silu(c_ctx) @ w_ada[layer] + b_ada[layer]).reshape(1, N_SUB, N_MOD, d)
        sink = sink_logit[layer].reshape(N_KV_HEADS, Q_PER_KV)
        ffn_a = (w_ffn_gate[layer, 0], w_ffn_up[layer, 0], w_ffn_down[layer, 0], ln_gain[layer, 0], ln_bias[layer, 0])
        ffn_b = (w_ffn_gate[layer, 1], w_ffn_up[layer, 1], w_ffn_down[layer, 1], ln_gain[layer, 2], ln_bias[layer, 2])

        x = ffn_sublayer(x, m_x, 0, *ffn_a)
        ctx = ffn_sublayer(ctx, m_c, 0, *ffn_a)

        sx, scx, gx = mod_terms(m_x, 1)
        sc, scc, gc = mod_terms(m_c, 1)
        hx = modulate(x, sx, scx)
        hc = modulate(ctx, sc, scc)
        ux, vgx, qx, kx, vx = project(hx, w_in[layer])
        qx = apply_rope(qx, cos, sin)
        kx = apply_rope(kx, cos, sin)
        if update_ctx:
            uc, vgc, qc, kc, vc = project(hc, w_in[layer])
        else:
            kc, vc = jnp.split(hc @ w_in[layer, :, SPLITS[2]:], [KV_WIDTH], axis=-1)
            kc, vc = split_heads(kc, N_KV_HEADS), split_heads(vc, N_KV_HEADS)
        y_attn = windowed_context_attention(qx, kx, vx, kc, vc, sink)
        y_mlp = chunk_spatial_gating(ux, vgx, w_spatial[layer], b_spatial[layer])
        mix = merge_mixers(y_mlp, y_attn, g_mix[layer], w_out[layer])
        x_mid = layer_norm(ALPHA * x + gx * mix, ln_gain[layer, 1], ln_bias[layer, 1])

        if update_ctx:
            qc5 = qc.reshape(b, c_len, N_KV_HEADS, Q_PER_KV, HEAD_DIM)
            yc_attn = sink_attention(qc5, kc, vc, sink, None).reshape(b, c_len, ATTN_WIDTH)
            yc_mlp = chunk_spatial_gating(uc, vgc, w_spatial[layer], b_spatial[layer])
            mix_c = merge_mixers(yc_mlp, yc_attn, g_mix[layer], w_out[layer])
            ctx = layer_norm(ALPHA * ctx + gc * mix_c, ln_gain[layer, 1], ln_bias[layer, 1])
            ctx = ffn_sublayer(ctx, m_c, 2, *ffn_b)

        x = ffn_sublayer(x_mid, m_x, 2, *ffn_b)
    return x
```

```python
import numpy as np
import ml_dtypes
from contextlib import ExitStack
import concourse.bass as bass
import concourse.mybir as mybir
from concourse.bass_utils import run_bass_kernel_spmd

F32 = mybir.dt.float32
BF16 = mybir.dt.bfloat16
AF = mybir.ActivationFunctionType
ALU = mybir.AluOpType
EPS = 1e-6


class Cfg:
    def __init__(self, D=4096, SEQ=4096, BATCH=4, CTX=256, GRID_W=64, n_cores=8):
        self.D = D
        self.SEQ = SEQ
        self.BATCH = BATCH
        self.CTX = CTX
        self.GRID_W = GRID_W
        self.n_cores = n_cores
        self.HD = 128
        self.NG = (D // 2) // 128
        self.NQ = (D // 2) // 128
        self.NKV = self.NQ // 4
        self.DFF = ((8 * D // 3 + 255) // 256) * 256
        self.KC = D // 128
        self.FC = self.DFF // 128
        self.NIN = 2 * self.NG + self.NQ + 2 * self.NKV
        self.cores_per_seq = n_cores // BATCH
        self.OWN = SEQ // self.cores_per_seq
        self.HALO = 128
        self.NTA = self.OWN + self.HALO + CTX
        self.NSO = self.OWN // 128
        self.NSA = self.NTA // 128
        self.T = 512
        self.DPW = 1024
        self.NDP = D // self.DPW
        self.ALPHA = (2.0 * 1) ** 0.25


class _Stop(Exception):
    pass


class Sem:
    def __init__(self, h, name):
        self.h = h
        self.name = name
        self.n = 0


class Arena:
    def __init__(self, nc, es, nbytes):
        self.t = es.enter_context(nc.sbuf_tensor("arena", [128, nbytes // 2], BF16))
        self.nbytes = nbytes
        self.off = 0

    def alloc(self, shape, dt):
        n = 1
        for d in shape[1:]:
            n *= d
        size = n * (4 if dt == F32 else 2)
        size = (size + 63) // 64 * 64
        assert self.off + size <= self.nbytes, ("SBUF arena overflow", self.off, size, self.nbytes)
        ap = self.t[0:shape[0], self.off // 2:self.off // 2 + n * (2 if dt == F32 else 1)]
        self.off += size
        if dt == F32:
            ap = ap.bitcast(F32)
        if len(shape) == 3:
            ap = ap.rearrange("p (a b) -> p a b", b=shape[2])
        elif len(shape) == 4:
            ap = ap.rearrange("p (a b c) -> p a b c", b=shape[2], c=shape[3])
        return ap


class Prog:
    ENG = ("pe", "act", "dve", "pool", "sp")

    def __init__(self, nc, es):
        self.nc = nc
        self.es = es
        self.ops = {e: [] for e in self.ENG}
        self.waited = {e: {} for e in self.ENG}
        self.nsem = 0
        self.prog = {e: self.newsem("prog_" + e) for e in ("pe", "act", "dve", "pool")}

    def newsem(self, name):
        self.nsem += 1
        h = self.es.enter_context(self.nc.semaphore(name + "_%d" % self.nsem))
        return Sem(h, name + "_%d" % self.nsem)

    def _deps(self, eng, deps):
        for d in deps or ():
            if d is None:
                continue
            if isinstance(d, list):
                self._deps(eng, d)
                continue
            sem, val = d
            if self.waited[eng].get(sem.name, 0) >= val:
                continue
            self.waited[eng][sem.name] = val
            self.ops[eng].append(("wait", sem, val))

    def op(self, eng, fn, deps=None, sig=True):
        self._deps(eng, deps)
        if sig:
            sem = self.prog[eng]
            sem.n += 1
            self.ops[eng].append(("op", fn, sem, 1))
            return (sem, sem.n)
        self.ops[eng].append(("op", fn, None, 0))
        return None

    def dma(self, eng, fn, sem, deps=None):
        self._deps(eng, deps)
        sem.n += 16
        self.ops[eng].append(("op", fn, sem, 16))
        return (sem, sem.n)

    def dma_group(self, eng, fns, sem, deps=None):
        self._deps(eng, deps)
        for fn in fns:
            sem.n += 16
            self.ops[eng].append(("op", fn, sem, 16))
        return (sem, sem.n)

    def wait(self, eng, deps):
        self._deps(eng, deps)

    def emit(self, block):
        def run(e, lst):
            for it in lst:
                if it[0] == "wait":
                    e.wait_ge(it[1].h, it[2])
                else:
                    ins = it[1](e)
                    if it[2] is not None:
                        ins.then_inc(it[2].h, it[3])

        ops = self.ops

        @block.tensor
        def _(e):
            run(e, ops["pe"])

        @block.scalar
        def _(e):
            run(e, ops["act"])

        @block.vector
        def _(e):
            run(e, ops["dve"])

        @block.gpsimd
        def _(e):
            run(e, ops["pool"])

        @block.sync
        def _(e):
            run(e, ops["sp"])


def build(cfg):
    nc = bass.Bass("TRN2", target_bir_lowering=False)
    D, KC, FC, T, NTA, OWN = cfg.D, cfg.KC, cfg.FC, cfg.T, cfg.NTA, cfg.OWN
    NG, NQ, NKV, NIN = cfg.NG, cfg.NQ, cfg.NKV, cfg.NIN
    NDP, DPW = cfg.NDP, cfg.DPW
    NSA, NSO = cfg.NSA, cfg.NSO
    ALPHA = cfg.ALPHA
    NR = OWN + cfg.HALO

    def din(name, shape, dt=F32):
        return nc.dram_tensor(name, list(shape), dt, kind="ExternalInput").ap()

    def dscr(name, shape, dt=F32):
        return nc.dram_tensor(name, list(shape), dt, kind="Internal").ap()

    xin = din("xin", [NTA, D])
    cT_in = din("cT", [128, KC, 2])
    wada = din("wada", [9 * KC, 128, KC * 128])
    badaT_in = din("badaT", [128, 9 * KC])
    wg = [din("wg%d" % i, [FC, 128, KC * 128]) for i in range(2)]
    wu = [din("wu%d" % i, [FC, 128, KC * 128]) for i in range(2)]
    wd = [din("wd%d" % i, [NDP, FC, 128, DPW]) for i in range(2)]
    win = din("win", [NIN, 128, KC * 128])
    wout = din("wout", [NDP, KC, 128, DPW])
    wsT_in = din("wsT", [128, NG * 128])
    bsp_in = din("bsp", [128, NG * 128])
    sink_in = din("sinkb", [128, NQ])
    gmixT_in = din("gmixT", [128, NG + NQ])
    lng_in = din("lng", [3, 128, D])
    lnb_in = din("lnb", [3, 128, D])
    cos_in = din("cos2", [128, NR])
    sin_in = din("sin2", [128, NR])
    masks_in = din("masks", [4, 128, 512], BF16)
    ident_in = din("ident", [128, 128])
    out = nc.dram_tensor("out", [OWN, D], F32, kind="ExternalOutput").ap()

    hTs = [dscr("hT%d" % i, [KC, 128, NTA], BF16) for i in range(3)]
    Ys = dscr("Yscr", [NTA, D])
    xa = dscr("xa", [NTA, D])
    xmid = dscr("xmid", [OWN, D])
    gbc = dscr("gbc", [4, 128, D])

    import os as _os
    es = ExitStack()
    P = Prog(nc, es)

    arena = Arena(nc, es, 212800)

    def gsb(name, shape, dt):
        return arena.alloc(list(shape), dt)

    ps = es.enter_context(nc.psum_tensor("ps", [128, 8, 512], F32))
    ident = gsb("ident", [128, 128], F32)
    modT = gsb("modT", [128, 9 * KC, 2], F32)
    csT = gsb("csT", [128, KC, 2], BF16)
    badaT = gsb("badaT", [128, 9 * KC], F32)
    cT_f = gsb("cT_f", [128, KC, 2], F32)
    arena_base = [arena.off]
    s_misc = P.newsem("misc")
    t_ident = P.dma("sp", lambda e: e.dma_start(out=ident[:], in_=ident_in), s_misc)
    s_c0 = P.newsem("c0")
    t_c0 = P.dma_group("sp", [lambda e: e.dma_start(out=cT_f[:], in_=cT_in),
                              lambda e: e.dma_start(out=badaT[:], in_=badaT_in)], s_c0)
    t_cs = P.op("act", lambda e: e.activation(out=csT[:], in_=cT_f[:], func=AF.Silu), deps=[t_c0])

    bank_free = [None] * 8
    state = {"all": []}

    arena_peak = [0]

    def phase_barrier():
        arena_peak[0] = max(arena_peak[0], arena.off)
        if _os.environ.get("KVERBOSE"):
            print("phase end: arena off", arena.off, "ops", {k: len(v) for k, v in P.ops.items()})
        toks = [(P.prog[e], P.prog[e].n) for e in ("pe", "act", "dve", "pool") if P.prog[e].n > 0]
        best = {}
        for (sem, val) in state["all"]:
            if sem.name not in best or best[sem.name][1] < val:
                best[sem.name] = (sem, val)
        toks += list(best.values())
        for e in Prog.ENG:
            P.wait(e, toks)
        state["all"] = []

    def rsqrt_chain(out_ap, in_ap, scale, deps):
        t1 = P.op("dve", lambda e: e.tensor_scalar(out=out_ap, in0=in_ap, scalar1=float(scale), scalar2=EPS,
                                                   op0=ALU.mult, op1=ALU.add), deps=deps)
        t2 = P.op("act", lambda e: e.activation(out=out_ap, in_=out_ap, func=AF.Sqrt), deps=[t1])
        t3 = P.op("dve", lambda e: e.reciprocal(out=out_ap, in_=out_ap), deps=[t2])
        return t3

    def track(tok):
        state["all"].append(tok)
        return tok

    class AdaStream:
        def __init__(self):
            self.grp_tok = {}

        def chunk(self, ring, grp, c):
            bank = 4 + (grp % 2)
            cc = grp * KC + c
            slot, t_w = ring.load(wada[cc])
            tk = None
            for k in range(KC):
                last = k == KC - 1
                tk = P.op("pe", lambda e, bank=bank, c=c, slot=slot, k=k, last=last: e.matmul(
                    ps[:, bank, 2 * c:2 * c + 2], lhsT=ring.buf[:, slot, k * 128:(k + 1) * 128],
                    rhs=csT[:, k, :], start=(k == 0), stop=last),
                    deps=[t_w, t_cs, bank_free[bank] if c == 0 else None] if k == 0 else None, sig=last)
            ring.free[slot] = tk
            if c == KC - 1:
                psv = ps[:, bank, 0:2 * KC].rearrange("p (c v) -> p c v", v=2)
                t_mod = None
                for v in range(2):
                    t_mod = P.op("dve", lambda e, grp=grp, v=v, psv=psv: e.tensor_tensor(
                        out=modT[:, grp * KC:(grp + 1) * KC, v], in0=psv[:, :, v],
                        in1=badaT[:, grp * KC:(grp + 1) * KC], op=ALU.add), deps=[tk, t_cs])
                bank_free[bank] = t_mod
                if grp % 3 == 1:
                    g0 = grp * KC
                    t_mod = P.op("dve", lambda e, g0=g0: e.tensor_scalar(
                        out=modT[:, g0:g0 + KC, :], in0=modT[:, g0:g0 + KC, :], scalar1=1.0, scalar2=None,
                        op0=ALU.add), deps=[t_mod])
                self.grp_tok[grp] = t_mod

    ada = AdaStream()

    def ada_plan(groups, ntiles):
        per = [[] for _ in range(ntiles)]
        for i, g in enumerate(groups):
            per[min(ntiles - 1, i * ntiles // max(1, len(groups)))] += [(g, c) for c in range(KC)]
        return per

    def phase0():
        arena.off = arena_base[0]
        ring = GRing(None, "adaring", 4)
        for grp in (0, 1):
            for c in range(KC):
                ada.chunk(ring, grp, c)
        phase_barrier()

    def make_gates(sb, gis):
        gates = [(0, 0, 0.5), (0, 1, 0.5), (1, 0, 1.0), (2, 0, 0.5)]
        ones_f = sb("ones_f", [128, 128], F32)
        dm = sb("dm", [128, 8, 128], F32)
        gs = sb("gs", [128, D], F32)
        s_st = P.newsem("gst")
        t_ones = P.op("dve", lambda e: e.memset(ones_f[:], 1.0))
        dm_free = [None] * 8
        gs_free = None
        di = 0
        toks = []
        for gi in gis:
            s_, v, fac = gates[gi]
            t_ev = None
            for c in range(KC):
                q = c % 4
                bank = 2 + ((c // 4) % 2)
                col = (s_ * 3 + 2) * KC + c
                dmi = di % 8
                di += 1
                t_dm = P.op("dve", lambda e, dmi=dmi, col=col, v=v: e.tensor_scalar(
                    out=dm[:, dmi, :], in0=ident[:], scalar1=modT[:, col, v:v + 1], scalar2=None,
                    op0=ALU.mult), deps=[t_ident, dm_free[dmi]])
                t_mm = P.op("pe", lambda e, bank=bank, q=q, dmi=dmi: e.matmul(
                    ps[:, bank, q * 128:(q + 1) * 128], lhsT=ones_f[:], rhs=dm[:, dmi, :],
                    start=True, stop=True), deps=[t_dm, t_ones, bank_free[bank] if q == 0 else None])
                dm_free[dmi] = t_mm
                if q == 3 or c == KC - 1:
                    c0 = c - q
                    t_ev = P.op("act", lambda e, bank=bank, c0=c0, c=c, q=q, fac=fac: e.activation(
                        out=gs[:, c0 * 128:(c + 1) * 128], in_=ps[:, bank, 0:(q + 1) * 128],
                        func=AF.Identity, scale=fac), deps=[t_mm, gs_free])
                    bank_free[bank] = t_ev
            gs_free = track(P.dma("sp", lambda e, gi=gi: e.dma_start(out=gbc[gi], in_=gs[:]), s_st, deps=[t_ev]))
            toks.append(gs_free)
        return toks

    def lpass(name, nsub, src_y, src_res, gate_of, ln_idx, x_dst, hT_dst, mod_s, v_of, gate_make=()):
        with ExitStack() as ph:
            arena.off = arena_base[0]

            def sb(nm, shape, dt):
                return arena.alloc(list(shape), dt)
            do_ln = src_y is not None
            t_gates = make_gates(sb, list(gate_make)) if gate_make else []
            rb = sb("rb", [128, 2, D], F32)
            s_ld = P.newsem(name + "ld")
            s_rb = [P.newsem(name + "rb") for _ in range(2)]
            s_yb = [P.newsem(name + "yb") for _ in range(2)]
            s_xs = [P.newsem(name + "xs") for _ in range(2)]
            s_hs = [P.newsem(name + "hs") for _ in range(2)]
            NCH = D // 512
            if do_ln:
                yb = sb("yb", [128, 2, D], F32)
                gidx = sorted(set(gate_of(st) for st in range(nsub)))
                gt = sb("gt", [128, len(gidx), D], F32)
                gn = sb("gn", [128, D], F32)
                bs = sb("bs", [128, D], F32)
                stats = sb("stats", [128, 2, NCH * 6], F32)
                mv = sb("mv", [128, 2, 2], F32)
                rstd = sb("rstd", [128, 2, 1], F32)
                fns = [(lambda e, i=i, g=g: e.dma_start(out=gt[:, i, :], in_=gbc[g])) for i, g in enumerate(gidx)]
                fns.append(lambda e: e.dma_start(out=gn[:], in_=lng_in[ln_idx]))
                fns.append(lambda e: e.dma_start(out=bs[:], in_=lnb_in[ln_idx]))
                t_consts = [P.dma_group("sp", fns, s_ld, deps=t_gates)]
            if hT_dst is not None:
                ho = sb("ho", [128, 2, KC, 128], BF16)
            rb_free = [None, None]
            yb_free = [None, None]
            ho_free = [None, None]
            loads = {}

            def issue_load(st):
                b = st % 2
                t1 = P.dma("sp", lambda e, st=st, b=b: e.dma_start(out=rb[:, b, :], in_=src_res[st * 128:(st + 1) * 128, :]),
                           s_rb[b], deps=rb_free[b])
                t2 = None
                if do_ln:
                    t2 = P.dma("sp", lambda e, st=st, b=b: e.dma_start(out=yb[:, b, :], in_=src_y[st * 128:(st + 1) * 128, :]),
                               s_yb[b], deps=yb_free[b])
                loads[st] = (t1, t2)

            issue_load(0)
            for st in range(nsub):
                b = st % 2
                if st + 1 < nsub:
                    issue_load(st + 1)
                t1, t2 = loads[st]
                t_x = t1
                if do_ln:
                    gi = gidx.index(gate_of(st))
                    ta = P.op("dve", lambda e, b=b, gi=gi: e.tensor_tensor(
                        out=yb[:, b, :], in0=yb[:, b, :], in1=gt[:, gi, :], op=ALU.mult), deps=[t2] + t_consts)
                    tb = P.op("dve", lambda e, b=b: e.scalar_tensor_tensor(
                        out=rb[:, b, :], in0=rb[:, b, :], scalar=ALPHA, in1=yb[:, b, :], op0=ALU.mult, op1=ALU.add),
                        deps=[ta, t1])
                    yb_free[b] = [tb]
                    tc = None
                    for ch in range(NCH):
                        tc = P.op("dve", lambda e, b=b, ch=ch: e.bn_stats(
                            stats[:, b, ch * 6:(ch + 1) * 6], rb[:, b, ch * 512:(ch + 1) * 512]), deps=[tb])
                    td = P.op("dve", lambda e, b=b: e.bn_aggr(mv[:, b, :], stats[:, b, :]), deps=[tc])
                    te = rsqrt_chain(rstd[:, b, :], mv[:, b, 1:2], 1.0, [td])
                    tf = P.op("dve", lambda e, b=b: e.tensor_scalar(
                        out=rb[:, b, :], in0=rb[:, b, :], scalar1=mv[:, b, 0:1], scalar2=rstd[:, b, 0:1],
                        op0=ALU.subtract, op1=ALU.mult), deps=[te])
                    tg = P.op("pool", lambda e, b=b: e.tensor_tensor(
                        out=rb[:, b, :], in0=rb[:, b, :], in1=gn[:], op=ALU.mult), deps=[tf] + t_consts)
                    t_x = P.op("pool", lambda e, b=b: e.tensor_tensor(
                        out=rb[:, b, :], in0=rb[:, b, :], in1=bs[:], op=ALU.add), deps=[tg])
                frees = []
                if x_dst is not None:
                    frees.append(track(P.dma("sp", lambda e, st=st, b=b: e.dma_start(
                        out=x_dst[st * 128:(st + 1) * 128, :], in_=rb[:, b, :]), s_xs[b], deps=[t_x])))
                if hT_dst is not None:
                    v = v_of(st)
                    t_tp = None
                    t_ev = None
                    for c in range(KC):
                        q = c % 4
                        bank = (c // 4) % 8
                        t_tp = P.op("pe", lambda e, b=b, c=c, q=q, bank=bank: e.transpose(
                            out=ps[:, bank, q * 128:(q + 1) * 128], in_=rb[:, b, c * 128:(c + 1) * 128],
                            identity=ident[:]), deps=[t_x, t_ident, bank_free[bank] if q == 0 else None],
                            sig=(q == 3))
                        if q == 3:
                            for qq in range(4):
                                cc = c - 3 + qq
                                t_ev = P.op("act", lambda e, b=b, cc=cc, qq=qq, bank=bank, v=v: e.activation(
                                    out=ho[:, b, cc, :], in_=ps[:, bank, qq * 128:(qq + 1) * 128], func=AF.Identity,
                                    bias=modT[:, (mod_s * 3 + 0) * KC + cc, v:v + 1],
                                    scale=modT[:, (mod_s * 3 + 1) * KC + cc, v:v + 1]),
                                    deps=[t_tp] + (ho_free[b] or []))
                            bank_free[bank] = t_ev
                    frees.append(t_tp)
                    t_hs = track(P.dma("sp", lambda e, st=st, b=b: e.dma_start(
                        out=hT_dst[:, :, st * 128:(st + 1) * 128].rearrange("c p t -> p c t"), in_=ho[:, b, :, :]),
                        s_hs[b], deps=[t_ev]))
                    ho_free[b] = [t_hs]
                rb_free[b] = frees
            phase_barrier()

    class GRing:
        def __init__(self, ph, name, nslots):
            self.n = nslots
            self.buf = arena.alloc([128, nslots, KC * 128], BF16)
            self.free = [None] * nslots
            self.i = 0
            self.sem = [P.newsem(name) for _ in range(nslots)]

        def load(self, src):
            slot = self.i % self.n
            self.i += 1
            tok = P.dma("pool", lambda e, slot=slot, src=src: e.dma_start(out=self.buf[:, slot, :], in_=src),
                        self.sem[slot], deps=[self.free[slot]])
            return slot, tok

    def gemm_g(ring, slot, t_w, hT, tw, bank, extra_deps):
        tk = None
        for k in range(KC):
            last = k == KC - 1
            tk = P.op("pe", lambda e, slot=slot, k=k, bank=bank, last=last: e.matmul(
                ps[:, bank, 0:tw], lhsT=ring.buf[:, slot, k * 128:(k + 1) * 128], rhs=hT[:, k, 0:tw],
                start=(k == 0), stop=last), deps=([t_w, bank_free[bank]] + extra_deps) if k == 0 else None, sig=last)
        ring.free[slot] = tk
        return tk

    def gemm_d(ph, name, aT, nchunks, wsrc, tw, y_dst_rows, dring, ybuf, yb_state, s_y, a_ready):
        S = tw // 128
        GC = 2
        for dp in range(NDP):
            tk = None
            for c0 in range(0, nchunks, GC):
                gc = min(GC, nchunks - c0)
                slot = dring["i"] % dring["n"]
                dring["i"] += 1
                t_w = P.dma("pool", lambda e, slot=slot, dp=dp, c0=c0, gc=gc: e.dma_start(
                    out=dring["buf"][:, slot, 0:gc, :], in_=wsrc[dp, c0:c0 + gc].rearrange("g p n -> p g n")),
                    dring["sem"][slot], deps=[dring["free"][slot]])
                for cl in range(gc):
                    c = c0 + cl
                    for s in range(S):
                        for hf in range(2):
                            bank = s * 2 + hf
                            first = c == 0
                            last = c == nchunks - 1
                            deps = None
                            if cl == 0 and s == 0 and hf == 0:
                                deps = [t_w] + a_ready
                            if first:
                                deps = (deps or []) + [bank_free[bank]]
                            sig = (cl == gc - 1 and s == S - 1 and hf == 1)
                            tk = P.op("pe", lambda e, bank=bank, c=c, s=s, slot=slot, cl=cl, hf=hf, first=first, last=last: e.matmul(
                                ps[:, bank, :], lhsT=aT[:, c, s * 128:(s + 1) * 128],
                                rhs=dring["buf"][:, slot, cl, hf * 512:(hf + 1) * 512], start=first, stop=last),
                                deps=deps, sig=sig)
                dring["free"][slot] = tk
            for s in range(S):
                yi = yb_state["i"] % len(yb_state["free"])
                yb_state["i"] += 1
                eng = "act" if s % 2 == 0 else "dve"
                if eng == "act":
                    t_ev = P.op("act", lambda e, yi=yi, s=s: e.activation(
                        out=ybuf[:, yi, :], in_=ps[:, 2 * s:2 * s + 2, :].rearrange("p a b -> p (a b)"), func=AF.Copy),
                        deps=[tk, yb_state["free"][yi]])
                else:
                    t_ev = P.op("dve", lambda e, yi=yi, s=s: e.tensor_copy(
                        out=ybuf[:, yi, :], in_=ps[:, 2 * s:2 * s + 2, :].rearrange("p a b -> p (a b)")),
                        deps=[tk, yb_state["free"][yi]])
                bank_free[2 * s] = t_ev
                bank_free[2 * s + 1] = t_ev
                r0 = y_dst_rows + s * 128
                yb_state["free"][yi] = track(P.dma("sp", lambda e, yi=yi, r0=r0, dp=dp: e.dma_start(
                    out=Ys[r0:r0 + 128, dp * DPW:(dp + 1) * DPW], in_=ybuf[:, yi, :]), s_y[yi], deps=[t_ev]))

    def make_dring(ph, name, n=4):
        return {"buf": arena.alloc([128, n, 2, DPW], BF16), "n": n, "i": 0,
                "free": [None] * n, "sem": [P.newsem(name) for _ in range(n)]}

    def ffn(name, tiles, hT_src, wg_, wu_, wd_, ada_groups=()):
        with ExitStack() as ph:
            arena.off = arena_base[0]

            def sb(nm, shape, dt):
                return arena.alloc(list(shape), dt)
            hT = sb("hT", [128, KC, T], BF16)
            actT = sb("actT", [128, FC, T], BF16)
            ring = GRing(ph, name + "ring", 5)
            dring = make_dring(ph, name + "dring")
            sgt = sb("sgt", [128, 2, T], F32)
            ybuf = sb("ybuf", [128, 4, DPW], F32)
            yb_state = {"i": 0, "free": [None] * 4}
            s_h = P.newsem(name + "h")
            s_y = [P.newsem(name + "y") for _ in range(4)]
            hT_free = None
            sgt_free = [None, None]
            t_h = None

            def load_h(t0, tw):
                return P.dma("sp", lambda e, t0=t0, tw=tw: e.dma_start(
                    out=hT[:, :, 0:tw], in_=hT_src[:, :, t0:t0 + tw].rearrange("c p t -> p c t")), s_h,
                    deps=[hT_free])

            t_h = load_h(*tiles[0])
            plan = ada_plan(list(ada_groups), len(tiles))
            for ti, (t0, tw) in enumerate(tiles):
                ev_toks = []
                tk = None
                pend = list(plan[ti])
                for j in range(FC):
                    pb = j % 2
                    npump = -(-len(pend) // (FC - j)) if j % 2 == 0 or len(pend) > (FC - j) else 0
                    for _ in range(npump):
                        ada.chunk(ring, *pend.pop(0))
                    slot_g, tw_g = ring.load(wg_[j])
                    slot_u, tw_u = ring.load(wu_[j])
                    tkg = gemm_g(ring, slot_g, tw_g, hT, tw, 2 * pb, [t_h])
                    tk = gemm_g(ring, slot_u, tw_u, hT, tw, 2 * pb + 1, [t_h])
                    t_s = P.op("act", lambda e, pb=pb, tw=tw: e.activation(
                        out=sgt[:, pb, 0:tw], in_=ps[:, 2 * pb, 0:tw], func=AF.Silu), deps=[tkg, sgt_free[pb]])
                    t_m = P.op("dve", lambda e, pb=pb, j=j, tw=tw: e.tensor_tensor(
                        out=actT[:, j, 0:tw], in0=sgt[:, pb, 0:tw], in1=ps[:, 2 * pb + 1, 0:tw], op=ALU.mult),
                        deps=[t_s, tk])
                    sgt_free[pb] = t_m
                    bank_free[2 * pb] = t_m
                    bank_free[2 * pb + 1] = t_m
                    ev_toks = [t_m] if j == FC - 1 else ev_toks
                hT_free = tk
                if ti + 1 < len(tiles):
                    t_h = load_h(*tiles[ti + 1])
                gemm_d(ph, name, actT, FC, wd_, tw, t0, dring, ybuf, yb_state, s_y, ev_toks + [bank_free[2], bank_free[0]])
            phase_barrier()

    def mixer():
        with ExitStack() as ph:
            arena.off = arena_base[0]

            def sb(nm, shape, dt):
                return arena.alloc(list(shape), dt)
            KT = sb("KT", [128, NKV, NTA], BF16)
            V = sb("V", [128, NSA, NKV * 128], BF16)
            hT = sb("hT", [128, KC, T], BF16)
            uT = sb("uT", [128, NG, T], BF16)
            vn = sb("vn", [128, 4, NG * 128], BF16)
            qT = sb("qT", [128, NQ, T], BF16)
            ring = GRing(ph, "mxring", 3)
            dring = make_dring(ph, "mxdring", 3)
            ybuf = sb("ybuf", [128, 2, DPW], F32)
            yb_state = {"i": 0, "free": [None] * 2}
            tmp = sb("tmp", [128, 4, 512], F32)
            PT = sb("PT", [128, 3, 512], BF16)
            sq = sb("sq", [128, 2, 512], BF16)
            cosb = sb("cosb", [128, 512], F32)
            sinb = sb("sinb", [128, 512], F32)
            masks = sb("masks", [128, 4, 512], BF16)
            bsp = sb("bsp", [128, NG * 128], F32)
            wsT = sb("wsT", [128, NG * 128], BF16)
            esink = sb("esink", [128, NQ], F32)
            gmixT = sb("gmixT", [128, NG + NQ], F32)
            ones_b = sb("ones_b", [128, 128], BF16)
            lsum = sb("lsum", [128, 512], F32)
            ssum = sb("ssum", [128, 2, 128], F32)
            rs = sb("rs", [128, 2, 128], F32)
            NVC = max(1, NG * 128 // 512)
            vst = sb("vst", [128, 4, NVC * 6], F32)
            vmv = sb("vmv", [128, 4, 2], F32)
            vrs = sb("vrs", [128, 4, 1], F32)
            s_c = P.newsem("mxc")
            s_h = P.newsem("mxh")
            s_y = [P.newsem("mxy") for _ in range(4)]
            s_t = P.newsem("mxt")
            fns = [(lambda e, k_=k_: e.dma_start(out=masks[:, k_, :], in_=masks_in[k_])) for k_ in range(4)]
            fns.append(lambda e: e.dma_start(out=bsp[:], in_=bsp_in))
            fns.append(lambda e: e.dma_start(out=esink[:], in_=sink_in))
            fns.append(lambda e: e.dma_start(out=gmixT[:], in_=gmixT_in))
            tc_ = [P.dma_group("sp", fns, s_c)]
            t_k1 = P.op("act", lambda e: e.activation(out=esink[:], in_=esink[:], func=AF.Exp), deps=tc_)
            s_ws = P.newsem("mxws")
            t_k2 = P.dma("pool", lambda e: e.dma_start(out=wsT[:], in_=wsT_in), s_ws)
            t_k3 = P.op("dve", lambda e: e.memset(ones_b[:], 1.0))
            consts = tc_ + [t_k1, t_k2, t_k3]
            chk(10)
            SC = 1.0 / float(np.sqrt(128.0))
            tmp_free = [None] * 4
            tmp_i = [0]

            def tmp_slot():
                i = tmp_i[0] % 4
                tmp_i[0] += 1
                return i

            hT_free = [None]

            def load_h(t0, tw):
                return P.dma("sp", lambda e, t0=t0, tw=tw: e.dma_start(
                    out=hT[:, :, 0:tw], in_=hTs[1][:, :, t0:t0 + tw].rearrange("c p t -> p c t")), s_h,
                    deps=hT_free[0])

            rope_free = [None]

            def load_rope(t0, tw):
                d = rope_free[0]
                a = P.dma_group("sp", [lambda e, t0=t0, tw=tw: e.dma_start(out=cosb[:, 0:tw], in_=cos_in[:, t0:t0 + tw]),
                                       lambda e, t0=t0, tw=tw: e.dma_start(out=sinb[:, 0:tw], in_=sin_in[:, t0:t0 + tw])],
                                s_t, deps=d)
                return [a]

            MXVAR = int(_os.environ.get("MXVAR", "0"))

            def rope_evac(bank, tw_r, dst, t_mm, t_rope):
                if MXVAR == 1:
                    t = P.op("act", lambda e: e.activation(out=dst, in_=ps[:, bank, 0:tw_r], func=AF.Copy), deps=[t_mm])
                    return t, [t]
                if MXVAR == 2:
                    i0 = tmp_slot()
                    ta = P.op("act", lambda e: e.activation(out=tmp[0:64, i0, 0:tw_r], in_=ps[64:128, bank, 0:tw_r], func=AF.Copy),
                              deps=[t_mm, tmp_free[i0]])
                    tb = P.op("act", lambda e: e.activation(out=tmp[64:128, i0, 0:tw_r], in_=ps[0:64, bank, 0:tw_r], func=AF.Copy),
                              deps=[t_mm])
                    t = P.op("act", lambda e: e.activation(out=dst, in_=tmp[:, i0, 0:tw_r], func=AF.Copy), deps=[ta, tb])
                    tmp_free[i0] = t
                    return t, [t]
                if MXVAR == 3:
                    i0 = tmp_slot()
                    ta = P.op("dve", lambda e: e.tensor_tensor(out=tmp[:, i0, 0:tw_r], in0=ps[:, bank, 0:tw_r], in1=cosb[:, 0:tw_r],
                                                             op=ALU.mult), deps=[t_mm, tmp_free[i0]] + t_rope)
                    t = P.op("pool", lambda e: e.tensor_tensor(out=tmp[:, i0, 0:tw_r], in0=tmp[:, i0, 0:tw_r], in1=sinb[:, 0:tw_r],
                                                            op=ALU.mult), deps=[ta] + t_rope)
                    t2 = P.op("dve", lambda e: e.tensor_copy(out=dst, in_=tmp[:, i0, 0:tw_r]), deps=[t])
                    tmp_free[i0] = t2
                    return t2, [ta]
                i0 = tmp_slot()
                i1 = tmp_slot()
                ta = P.op("act", lambda e: e.activation(out=tmp[0:64, i0, 0:tw_r], in_=ps[64:128, bank, 0:tw_r], func=AF.Copy),
                          deps=[t_mm, tmp_free[i0]])
                tb = P.op("act", lambda e: e.activation(out=tmp[64:128, i0, 0:tw_r], in_=ps[0:64, bank, 0:tw_r], func=AF.Copy),
                          deps=[t_mm])
                tc2 = P.op("dve", lambda e: e.tensor_tensor(out=tmp[:, i1, 0:tw_r], in0=ps[:, bank, 0:tw_r], in1=cosb[:, 0:tw_r],
                                                             op=ALU.mult), deps=[t_mm, tb, tmp_free[i1]] + t_rope)
                td = P.op("dve" if MXVAR == 4 else "pool", lambda e: e.tensor_tensor(out=tmp[:, i0, 0:tw_r], in0=tmp[:, i0, 0:tw_r], in1=sinb[:, 0:tw_r],
                                                            op=ALU.mult), deps=[ta, tb] + t_rope + ([tc2] if MXVAR == 5 else []))
                if MXVAR == 6:
                    te0 = P.op("dve", lambda e: e.tensor_tensor(out=tmp[:, i1, 0:tw_r], in0=tmp[:, i1, 0:tw_r], in1=tmp[:, i0, 0:tw_r], op=ALU.add),
                               deps=[tc2, td])
                    te = P.op("dve", lambda e: e.tensor_copy(out=dst, in_=tmp[:, i1, 0:tw_r]), deps=[te0])
                elif MXVAR == 7:
                    te = P.op("dve", lambda e: e.tensor_copy(out=dst, in_=tmp[:, i1, 0:tw_r]), deps=[tc2, td])
                else:
                    te = P.op("dve", lambda e: e.tensor_tensor(out=dst, in0=tmp[:, i1, 0:tw_r], in1=tmp[:, i0, 0:tw_r], op=ALU.add),
                              deps=[tc2, td])
                tmp_free[i0] = te
                tmp_free[i1] = te
                return te, [tb, tc2]

            def transpose_to(src_f32, nsub_, dsts, t_src, bank):
                t_tp = None
                for s in range(nsub_):
                    t_tp = P.op("pe", lambda e, s=s: e.transpose(out=ps[:, bank, s * 128:(s + 1) * 128],
                                                                 in_=src_f32[:, s * 128:(s + 1) * 128], identity=ident[:]),
                                deps=[t_src, t_ident, bank_free[bank] if s == 0 else None], sig=(s == nsub_ - 1))
                t_ev = None
                for s in range(nsub_):
                    t_ev = P.op("dve", lambda e, s=s: e.tensor_copy(out=dsts[s], in_=ps[:, bank, s * 128:(s + 1) * 128]),
                                deps=[t_tp])
                bank_free[bank] = t_ev
                return t_tp, t_ev

            tiles_all = [(t0, min(T, NTA - t0)) for t0 in range(0, NTA, T)]
            t_h = load_h(*tiles_all[0])
            kv_ready = []
            for ti, (t0, tw) in enumerate(tiles_all):
                tw_r = max(0, min(tw, NR - t0))
                t_rope = load_rope(t0, tw_r) if tw_r > 0 else []
                tk = None
                readers = []
                for jj in range(2 * NKV):
                    j = 2 * NG + NQ + jj
                    bank = jj % 4
                    slot, t_w = ring.load(win[j])
                    tk = gemm_g(ring, slot, t_w, hT, tw, bank, [t_h])
                    if jj == 1:
                        chk(11)
                    if jj < NKV:
                        toks = []
                        if tw_r < tw:
                            tcp = P.op("act", lambda e, jj=jj, t0=t0, tw=tw, tw_r=tw_r, bank=bank: e.activation(
                                out=KT[:, jj, t0 + tw_r:t0 + tw], in_=ps[:, bank, tw_r:tw], func=AF.Copy), deps=[tk])
                            toks.append(tcp)
                        if tw_r > 0:
                            te, rd = rope_evac(bank, tw_r, KT[:, jj, t0:t0 + tw_r], tk, t_rope)
                            toks += [te] + rd
                            readers += rd
                        bank_free[bank] = toks
                        kv_ready.append(toks)
                    else:
                        h = jj - NKV
                        i0 = tmp_slot()
                        t_c = P.op("act", lambda e, i0=i0, bank=bank, tw=tw: e.activation(
                            out=tmp[:, i0, 0:tw], in_=ps[:, bank, 0:tw], func=AF.Copy), deps=[tk, tmp_free[i0]])
                        bank_free[bank] = t_c
                        nsub_ = tw // 128
                        dsts = [V[:, t0 // 128 + s, h * 128:(h + 1) * 128] for s in range(nsub_)]
                        t_tp, t_ev = transpose_to(tmp[:, i0, :], nsub_, dsts, t_c, 6 + (jj % 2))
                        tmp_free[i0] = t_tp
                        kv_ready.append(t_ev)
                chk(12)
                rope_free[0] = readers + [bank_free[b_] for b_ in range(4)]
                hT_free[0] = [tk]
                if ti + 1 < len(tiles_all):
                    t_h = load_h(*tiles_all[ti + 1])

            chk(0)
            tiles_own = [(t0, min(T, OWN - t0)) for t0 in range(0, OWN, T)]
            t_h = load_h(*tiles_own[0])
            yT = hT
            plan_mx = ada_plan([5, 6, 7, 8], len(tiles_own))
            for ti, (t0, tw) in enumerate(tiles_own):
                S = tw // 128
                t_rope = load_rope(t0, tw)
                pend = list(plan_mx[ti])
                readers = []
                vn_ready = []
                u_ready = []
                q_ready = []
                tk = None
                NJ = 2 * NG + NQ
                for j in range(NJ):
                    bank = j % 4
                    for _ in range(-(-len(pend) // (NJ - j))):
                        ada.chunk(ring, *pend.pop(0))
                    slot, t_w = ring.load(win[j])
                    tk = gemm_g(ring, slot, t_w, hT, tw, bank, [t_h])
                    if pend_tp[0] is not None:
                        pend_tp[0]()
                        pend_tp[0] = None
                    if j < NG:
                        t_e = P.op("act", lambda e, j=j, bank=bank, tw=tw: e.activation(
                            out=uT[:, j, 0:tw], in_=ps[:, bank, 0:tw], func=AF.Gelu), deps=[tk])
                        bank_free[bank] = t_e
                        u_ready.append(t_e)
                    elif j < 2 * NG:
                        g = j - NG
                        i0 = tmp_slot()
                        t_c = P.op("act", lambda e, i0=i0, bank=bank, tw=tw: e.activation(
                            out=tmp[:, i0, 0:tw], in_=ps[:, bank, 0:tw], func=AF.Gelu), deps=[tk, tmp_free[i0]])
                        bank_free[bank] = t_c
                        dsts = [vn[:, s, g * 128:(g + 1) * 128] for s in range(S)]

                        def do_tp(i0=i0, dsts=dsts, t_c=t_c, j=j, S=S):
                            t_tp, t_ev = transpose_to(tmp[:, i0, :], S, dsts, t_c, 6 + (j % 2))
                            tmp_free[i0] = t_tp
                            vn_ready.append(t_ev)
                        pend_tp[0] = do_tp
                    else:
                        hq = j - 2 * NG
                        te, rd = rope_evac(bank, tw, qT[:, hq, 0:tw], tk, t_rope)
                        readers += rd
                        bank_free[bank] = [te] + rd
                        q_ready.append(te)
                if pend_tp[0] is not None:
                    pend_tp[0]()
                    pend_tp[0] = None
                rope_free[0] = readers + [bank_free[b_] for b_ in range(4)]
                chk(1)
                for s in range(S):
                    t1 = None
                    VW = NG * 128 // NVC
                    for ch in range(NVC):
                        t1 = P.op("dve", lambda e, s=s, ch=ch: e.bn_stats(
                            vst[:, s, ch * 6:(ch + 1) * 6], vn[:, s, ch * VW:(ch + 1) * VW]), deps=vn_ready)
                    t2 = P.op("dve", lambda e, s=s: e.bn_aggr(vmv[:, s, :], vst[:, s, :]), deps=[t1])
                    t3 = rsqrt_chain(vrs[:, s, :], vmv[:, s, 1:2], 1.0, [t2])
                    t4 = P.op("dve", lambda e, s=s: e.tensor_scalar(
                        out=vn[:, s, :], in0=vn[:, s, :], scalar1=vmv[:, s, 0:1], scalar2=vrs[:, s, 0:1],
                        op0=ALU.subtract, op1=ALU.mult), deps=[t3])
                    vn_ready.append(t4)
                chk(2)
                y_wr = [tk]
                SSB = 6
                for s in range(S):
                    n4 = (NG + 3) // 4
                    t_ssm = None
                    for g4 in range(n4):
                        g0 = g4 * 4
                        ng_ = min(4, NG - g0)
                        bank = 2 + (g4 % 2)
                        t_mm = None
                        for gg in range(ng_):
                            g = g0 + gg
                            t_mm = P.op("pe", lambda e, g=g, gg=gg, s=s, bank=bank: e.matmul(
                                ps[:, bank, gg * 128:(gg + 1) * 128], lhsT=vn[:, s, g * 128:(g + 1) * 128],
                                rhs=wsT[:, g * 128:(g + 1) * 128], start=True, stop=True),
                                deps=(vn_ready + consts + [bank_free[bank]]) if gg == 0 else None, sig=(gg == ng_ - 1))
                        i0 = tmp_slot()
                        w_ = ng_ * 128
                        ta = P.op("dve", lambda e, i0=i0, bank=bank, g0=g0, w_=w_: e.tensor_tensor(
                            out=tmp[:, i0, 0:w_], in0=ps[:, bank, 0:w_], in1=bsp[:, g0 * 128:g0 * 128 + w_], op=ALU.add),
                            deps=[t_mm, tmp_free[i0]])
                        bank_free[bank] = ta
                        tb = P.op("dve", lambda e, i0=i0, g0=g0, ng_=ng_, s=s: e.tensor_tensor(
                            out=tmp[:, i0, 0:ng_ * 128].rearrange("p (a b) -> p a b", b=128),
                            in0=tmp[:, i0, 0:ng_ * 128].rearrange("p (a b) -> p a b", b=128),
                            in1=uT[:, g0:g0 + ng_, s * 128:(s + 1) * 128], op=ALU.mult), deps=[ta] + u_ready)
                        sqi = g4 % 2
                        tcq = P.op("act", lambda e, i0=i0, sqi=sqi, w_=w_: e.activation(
                            out=sq[:, sqi, 0:w_], in_=tmp[:, i0, 0:w_], func=AF.Square), deps=[tb, sq_free[sqi]])
                        tdq = P.op("pool", lambda e, i0=i0, g0=g0, ng_=ng_, s=s: e.tensor_copy(
                            out=yT[:, g0:g0 + ng_, s * 128:(s + 1) * 128],
                            in_=tmp[:, i0, 0:ng_ * 128].rearrange("p (a b) -> p a b", b=128)), deps=[tb] + y_wr)
                        tmp_free[i0] = tdq
                        y_parts.append(tdq)
                        t_ssm = P.op("pe", lambda e, sqi=sqi, w_=w_, g4=g4, n4=n4: e.matmul(
                            ps[:, SSB, 0:w_], lhsT=ones_b[:], rhs=sq[:, sqi, 0:w_], start=(g4 == 0), stop=(g4 == n4 - 1)),
                            deps=[tcq, consts[-1], bank_free[SSB] if g4 == 0 else None])
                        sq_free[sqi] = t_ssm
                    wl = min(4, NG)
                    tr = P.op("dve", lambda e, wl=wl: e.tensor_reduce(
                        out=ssum[:, 0, :], in_=ps[:, SSB, 0:wl * 128].rearrange("p (a b) -> p b a", b=128),
                        axis=mybir.AxisListType.X, op=ALU.add), deps=[t_ssm, ss_free[0]])
                    bank_free[SSB] = tr
                    tr3 = rsqrt_chain(rs[:, 0, :], ssum[:, 0, :], 1.0 / (NG * 128), [tr, rs_free[0]])
                    ss_free[0] = tr3
                    tl = None
                    for g in range(NG):
                        tl = P.op("dve", lambda e, g=g, s=s: e.scalar_tensor_tensor(
                            out=yT[:, g, s * 128:(s + 1) * 128], in0=yT[:, g, s * 128:(s + 1) * 128],
                            scalar=gmixT[:, g:g + 1], in1=rs[:, 0, :], op0=ALU.mult, op1=ALU.mult),
                            deps=[tr3] + y_parts + consts)
                    rs_free[0] = tl
                    y_done.append(tl)
                    del y_parts[:]
                chk(3)
                for s in range(S):
                    qb = t0 // 128 + s
                    kbs = []
                    for cb in range(cfg.CTX // 128):
                        kbs.append((NR // 128 + cb, None))
                    if qb == 0:
                        kbs.append((OWN // 128, 2))
                    else:
                        kbs.append((qb - 1, 0))
                    kbs.append((qb, None))
                    if qb == NSO - 1:
                        kbs.append((OWN // 128, 3))
                    else:
                        kbs.append((qb + 1, 1))
                    t_ssa = None
                    for h in range(NKV):
                        OB, LB = (4, 5) if h % 2 == 0 else (2, 3)
                        t_o = None
                        t_l = None
                        for ki, (kb, mk) in enumerate(kbs):
                            bank = ki % 2
                            t_s = P.op("pe", lambda e, h=h, kb=kb, s=s, bank=bank: e.matmul(
                                ps[:, bank, :].rearrange("p (a b) -> p a b", b=128),
                                lhsT=KT[:, h, kb * 128:(kb + 1) * 128], rhs=qT[:, 4 * h:4 * h + 4, s * 128:(s + 1) * 128],
                                start=True, stop=True), deps=kv_ready + q_ready + [bank_free[bank]])
                            pi = pt_i[0] % 3
                            pt_i[0] += 1
                            t_e = P.op("act", lambda e, pi=pi, bank=bank: e.activation(
                                out=PT[:, pi, :], in_=ps[:, bank, :], func=AF.Exp, scale=SC), deps=[t_s, pt_free[pi]])
                            bank_free[bank] = t_e
                            if mk is not None:
                                t_e = P.op("pool", lambda e, pi=pi, mk=mk: e.tensor_tensor(
                                    out=PT[:, pi, :], in0=PT[:, pi, :], in1=masks[:, mk, :], op=ALU.mult), deps=[t_e] + consts)
                            first = ki == 0
                            last = ki == len(kbs) - 1
                            t_o = P.op("pe", lambda e, pi=pi, kb=kb, h=h, first=first, last=last: e.matmul(
                                ps[:, OB, :], lhsT=V[:, kb, h * 128:(h + 1) * 128], rhs=PT[:, pi, :], start=first, stop=last),
                                deps=[t_e, bank_free[OB] if first else None], sig=False)
                            t_l = P.op("pe", lambda e, pi=pi, first=first, last=last: e.matmul(
                                ps[:, LB, :], lhsT=ones_b[:], rhs=PT[:, pi, :], start=first, stop=last),
                                deps=[bank_free[LB] if first else None])
                            pt_free[pi] = t_l
                        tn = None
                        for g in range(4):
                            tn = P.op("dve", lambda e, g=g, h=h: e.tensor_scalar(
                                out=lsum[:, g * 128:(g + 1) * 128], in0=ps[:, LB, g * 128:(g + 1) * 128],
                                scalar1=esink[:, 4 * h + g:4 * h + g + 1], scalar2=None, op0=ALU.add),
                                deps=[t_l, ls_free[0]] + consts)
                        bank_free[LB] = tn
                        tn2 = P.op("dve", lambda e: e.reciprocal(out=lsum[:], in_=lsum[:]), deps=[tn])
                        i0 = tmp_slot()
                        tn3 = P.op("dve", lambda e, i0=i0: e.tensor_tensor(
                            out=tmp[:, i0, :], in0=ps[:, OB, :], in1=lsum[:], op=ALU.mult), deps=[tn2, tmp_free[i0]])
                        bank_free[OB] = tn3
                        ls_free[0] = tn3
                        sqi = h % 2
                        tcq = P.op("act", lambda e, i0=i0, sqi=sqi: e.activation(
                            out=sq[:, sqi, :], in_=tmp[:, i0, :], func=AF.Square), deps=[tn3, sq_free[sqi]])
                        tdq = P.op("pool", lambda e, i0=i0, h=h, s=s: e.tensor_copy(
                            out=yT[:, NG + 4 * h:NG + 4 * h + 4, s * 128:(s + 1) * 128],
                            in_=tmp[:, i0, :].rearrange("p (a b) -> p a b", b=128)), deps=[tn3] + y_wr)
                        tmp_free[i0] = tdq
                        y_parts.append(tdq)
                        t_ssa = P.op("pe", lambda e, sqi=sqi, h=h: e.matmul(
                            ps[:, 7, :], lhsT=ones_b[:], rhs=sq[:, sqi, :], start=(h == 0), stop=(h == NKV - 1)),
                            deps=[tcq, bank_free[7] if h == 0 else None])
                        sq_free[sqi] = t_ssa
                    tr = P.op("dve", lambda e: e.tensor_reduce(
                        out=ssum[:, 1, :], in_=ps[:, 7, :].rearrange("p (a b) -> p b a", b=128),
                        axis=mybir.AxisListType.X, op=ALU.add), deps=[t_ssa, ss_free[1]])
                    bank_free[7] = tr
                    tr3 = rsqrt_chain(rs[:, 1, :], ssum[:, 1, :], 1.0 / (NQ * 128), [tr, rs_free[1]])
                    ss_free[1] = tr3
                    tl = None
                    for hq in range(NQ):
                        tl = P.op("dve", lambda e, hq=hq, s=s: e.scalar_tensor_tensor(
                            out=yT[:, NG + hq, s * 128:(s + 1) * 128], in0=yT[:, NG + hq, s * 128:(s + 1) * 128],
                            scalar=gmixT[:, NG + hq:NG + hq + 1], in1=rs[:, 1, :], op0=ALU.mult, op1=ALU.mult),
                            deps=[tr3] + y_parts + consts)
                    rs_free[1] = tl
                    y_done.append(tl)
                    del y_parts[:]
                chk(4)
                gemm_d(ph, "mx", yT, KC, wout, tw, t0, dring, ybuf, yb_state, s_y, list(y_done) + [bank_free[b_] for b_ in range(8)])
                del y_done[:]
                hT_free[0] = [(P.prog["pe"], P.prog["pe"].n)]
                if ti + 1 < len(tiles_own):
                    t_h = load_h(*tiles_own[ti + 1])
            phase_barrier()

    def mixer_wrap():
        try:
            mixer()
        except _Stop:
            phase_barrier()

    MX = int(_os.environ.get("MXSTOP", "99"))

    def chk(n):
        if MX == n:
            raise _Stop()

    sq_free = [None, None]
    ss_free = [None, None]
    rs_free = [None, None]
    ls_free = [None]
    pt_free = [None, None, None]
    pt_i = [0]
    pend_tp = [None]
    y_parts = []
    y_done = []

    tiles_a = [(t0, min(T, NTA - t0)) for t0 in range(0, NTA, T)]
    tiles_c = [(t0, min(T, OWN - t0)) for t0 in range(0, OWN, T)]
    ctx_st0 = (OWN + cfg.HALO) // 128
    v_of_a = lambda st: 1 if st >= ctx_st0 else 0

    import os
    kstop = int(os.environ.get("KSTOP", "99"))
    steps = [
        lambda: phase0(),
        lambda: lpass("p1", NSA, None, xin, None, None, None, hTs[0], 0, v_of_a),
        lambda: ffn("fa", tiles_a, hTs[0], wg[0], wu[0], wd[0], ada_groups=(2, 3, 4)),
        lambda: lpass("p3", NSA, Ys, xin, lambda st: v_of_a(st), 0, xa, hTs[1], 1, v_of_a, gate_make=(0, 1)),
        lambda: mixer_wrap(),
        lambda: lpass("p6", NSO, Ys, xa, lambda st: 2, 1, xmid, hTs[2], 2, lambda st: 0, gate_make=(2,)),
        lambda: ffn("fb", tiles_c, hTs[2], wg[1], wu[1], wd[1]),
        lambda: lpass("p8", NSO, Ys, xmid, lambda st: 3, 2, out, None, None, None, gate_make=(3,)),
    ]
    for i_, st_ in enumerate(steps):
        if i_ <= kstop:
            st_()

    with nc.Block() as block:
        P.emit(block)
    es.close()
    return nc


def _rope_tables(cfg, pos):
    n_freq = 128 // 4
    inv_freq = (10000.0 ** (-np.arange(n_freq, dtype=np.float32) / n_freq)).astype(np.float32)
    row = (pos // cfg.GRID_W).astype(np.float32)
    col = (pos % cfg.GRID_W).astype(np.float32)
    ang = np.concatenate([row[:, None] * inv_freq, col[:, None] * inv_freq], axis=-1).astype(np.float32)
    cos = np.cos(ang).astype(np.float32).T
    sin = np.sin(ang).astype(np.float32).T
    cos2 = np.concatenate([cos, cos], 0)
    sin2 = np.concatenate([-sin, sin], 0)
    return np.ascontiguousarray(cos2), np.ascontiguousarray(sin2)


def _g_layout(w, KC):
    K, F = w.shape
    return np.ascontiguousarray(w.reshape(KC, 128, F // 128, 128).transpose(2, 1, 0, 3)).reshape(F // 128, 128, KC * 128)


def _d_layout(w, DPW):
    Cn, Dm = w.shape[0] // 128, w.shape[1]
    return np.ascontiguousarray(w.reshape(Cn, 128, Dm // DPW, DPW).transpose(2, 0, 1, 3))


def prepare_inputs(cfg, x, c, ctx, c_ctx, w_ada, b_ada, w_ffn_gate, w_ffn_up, w_ffn_down, w_in, w_spatial,
                   b_spatial, sink_logit, g_mix, w_out, ln_gain, ln_bias):
    D, KC = cfg.D, cfg.KC
    f32 = np.float32
    shared = {}
    shared["wada"] = _g_layout(np.asarray(w_ada[0], f32), KC)
    shared["badaT"] = np.ascontiguousarray(np.asarray(b_ada[0], f32).reshape(9 * KC, 128).T)
    for i in range(2):
        shared["wg%d" % i] = _g_layout(np.asarray(w_ffn_gate[0, i], f32), KC)
        shared["wu%d" % i] = _g_layout(np.asarray(w_ffn_up[0, i], f32), KC)
        shared["wd%d" % i] = _d_layout(np.asarray(w_ffn_down[0, i], f32), cfg.DPW)
    shared["win"] = _g_layout(np.asarray(w_in[0], f32), KC)
    shared["wout"] = _d_layout(np.asarray(w_out[0], f32), cfg.DPW)
    ws = np.asarray(w_spatial[0], f32)
    shared["wsT"] = np.ascontiguousarray(ws.transpose(2, 0, 1)).reshape(128, cfg.NG * 128)
    shared["bsp"] = np.ascontiguousarray(np.broadcast_to(np.asarray(b_spatial[0], f32).reshape(1, -1), (128, cfg.NG * 128)))
    shared["sinkb"] = np.ascontiguousarray(np.broadcast_to(np.asarray(sink_logit[0], f32).reshape(1, -1), (128, cfg.NQ)))
    shared["gmixT"] = np.ascontiguousarray(np.asarray(g_mix[0], f32).reshape(cfg.NG + cfg.NQ, 128).T)
    shared["lng"] = np.ascontiguousarray(np.broadcast_to(np.asarray(ln_gain[0], f32)[:, None, :], (3, 128, D)))
    shared["lnb"] = np.ascontiguousarray(np.broadcast_to(np.asarray(ln_bias[0], f32)[:, None, :], (3, 128, D)))
    shared["ident"] = np.eye(128, dtype=f32)
    kk = np.arange(128)[:, None]
    qq = np.arange(128)[None, :]
    m_prev = np.tile((kk >= qq).astype(f32), (1, 4))
    m_next = np.tile((kk <= qq).astype(f32), (1, 4))
    zero = np.zeros_like(m_prev)
    x = np.asarray(x, f32)
    ctx = np.asarray(ctx, f32)
    c = np.asarray(c, f32)
    c_ctx = np.asarray(c_ctx, f32)
    in_maps = []
    cps = cfg.cores_per_seq
    for core in range(cfg.n_cores):
        b, half = core // cps, core % cps
        o0 = half * cfg.OWN
        m = dict(shared)
        first = half == 0
        lastc = half == cps - 1
        if first:
            h0 = o0 + cfg.OWN
        else:
            h0 = o0 - 128
        m["xin"] = np.ascontiguousarray(np.concatenate([x[b, o0:o0 + cfg.OWN], x[b, h0:h0 + 128], ctx[b]], 0))
        cv = np.stack([c[b], c_ctx], -1)
        m["cT"] = np.ascontiguousarray(cv.reshape(KC, 128, 2).transpose(1, 0, 2))
        pos = np.concatenate([np.arange(o0, o0 + cfg.OWN), np.arange(h0, h0 + 128)])
        m["cos2"], m["sin2"] = _rope_tables(cfg, pos)
        fp = zero if first else m_prev
        ln_ = m_next if first else zero
        m["masks"] = np.stack([m_prev, m_next, fp, ln_], 0).astype(ml_dtypes.bfloat16)
        in_maps.append(m)
    return in_maps


_CACHE = {}


def run(cfg, inputs):
    assert cfg.cores_per_seq == 2, "kernel assumes two cores per sequence"
    key = (cfg.D, cfg.SEQ, cfg.BATCH, cfg.n_cores)
    if key not in _CACHE:
        _CACHE[key] = build(cfg)
    nc = _CACHE[key]
    in_maps = prepare_inputs(cfg, **inputs)
    res = run_bass_kernel_spmd(nc, in_maps, core_ids=list(range(cfg.n_cores)))
    outs = [np.asarray(r["out"], np.float32) for r in res.results]
    y = np.zeros((cfg.BATCH, cfg.SEQ, cfg.D), np.float32)
    cps = cfg.cores_per_seq
    for core in range(cfg.n_cores):
        b, half = core // cps, core % cps
        y[b, half * cfg.OWN:(half + 1) * cfg.OWN] = outs[core]
    return y


def kernel(**inputs):
    cfg = Cfg()
    return run(cfg, inputs)
```

```python
import numpy as np
import ml_dtypes
from contextlib import ExitStack
import concourse.bass as bass
import concourse.mybir as mybir
from concourse.bass_utils import run_bass_kernel_spmd

F32 = mybir.dt.float32
BF16 = mybir.dt.bfloat16
AF = mybir.ActivationFunctionType
ALU = mybir.AluOpType
EPS = 1e-6


class Cfg:
    def __init__(self, D=4096, SEQ=4096, BATCH=4, CTX=256, GRID_W=64, n_cores=8):
        self.D = D
        self.SEQ = SEQ
        self.BATCH = BATCH
        self.CTX = CTX
        self.GRID_W = GRID_W
        self.n_cores = n_cores
        self.HD = 128
        self.NG = (D // 2) // 128
        self.NQ = (D // 2) // 128
        self.NKV = self.NQ // 4
        self.DFF = ((8 * D // 3 + 255) // 256) * 256
        self.KC = D // 128
        self.FC = self.DFF // 128
        self.NIN = 2 * self.NG + self.NQ + 2 * self.NKV
        self.cores_per_seq = n_cores // BATCH
        self.OWN = SEQ // self.cores_per_seq
        self.HALO = 128
        self.NTA = self.OWN + self.HALO + CTX
        self.NSO = self.OWN // 128
        self.NSA = self.NTA // 128
        self.T = 512
        self.DPW = 1024
        self.NDP = D // self.DPW
        self.ALPHA = (2.0 * 1) ** 0.25


class _Stop(Exception):
    pass


class Sem:
    def __init__(self, h, name):
        self.h = h
        self.name = name
        self.n = 0


class Arena:
    def __init__(self, nc, es, nbytes):
        self.t = es.enter_context(nc.sbuf_tensor("arena", [128, nbytes // 2], BF16))
        self.nbytes = nbytes
        self.off = 0

    def alloc(self, shape, dt):
        n = 1
        for d in shape[1:]:
            n *= d
        size = n * (4 if dt == F32 else 2)
        size = (size + 63) // 64 * 64
        assert self.off + size <= self.nbytes, ("SBUF arena overflow", self.off, size, self.nbytes)
        ap = self.t[0:shape[0], self.off // 2:self.off // 2 + n * (2 if dt == F32 else 1)]
        self.off += size
        if dt == F32:
            ap = ap.bitcast(F32)
        if len(shape) == 3:
            ap = ap.rearrange("p (a b) -> p a b", b=shape[2])
        elif len(shape) == 4:
            ap = ap.rearrange("p (a b c) -> p a b c", b=shape[2], c=shape[3])
        return ap


class Prog:
    ENG = ("pe", "act", "dve", "pool", "sp")

    def __init__(self, nc, es):
        self.nc = nc
        self.es = es
        self.ops = {e: [] for e in self.ENG}
        self.waited = {e: {} for e in self.ENG}
        self.nsem = 0
        self.pool = {"hw": [], "sw": []}
        self.phase_sems = []
        self.prog = {e: self.newsem("prog_" + e, persistent=True) for e in ("pe", "act", "dve", "pool")}

    def newsem(self, name, persistent=False, kind="hw"):
        if not persistent and self.pool[kind]:
            sm = self.pool[kind].pop()
            self.phase_sems.append((kind, sm))
            return sm
        self.nsem += 1
        h = self.es.enter_context(self.nc.semaphore(name + "_%d" % self.nsem))
        sm = Sem(h, name + "_%d" % self.nsem)
        if not persistent:
            self.phase_sems.append((kind, sm))
        return sm

    def release_phase_sems(self):
        for kind, sm in self.phase_sems:
            self.pool[kind].append(sm)
        self.phase_sems = []

    def _deps(self, eng, deps):
        for d in deps or ():
            if d is None:
                continue
            if isinstance(d, list):
                self._deps(eng, d)
                continue
            sem, val = d
            if self.waited[eng].get(sem.name, 0) >= val:
                continue
            self.waited[eng][sem.name] = val
            self.ops[eng].append(("wait", sem, val))

    def op(self, eng, fn, deps=None, sig=True):
        self._deps(eng, deps)
        if sig:
            sem = self.prog[eng]
            sem.n += 1
            self.ops[eng].append(("op", fn, sem, 1))
            return (sem, sem.n)
        self.ops[eng].append(("op", fn, None, 0))
        return None

    def dma(self, eng, fn, sem, deps=None):
        self._deps(eng, deps)
        sem.n += 16
        self.ops[eng].append(("op", fn, sem, 16))
        return (sem, sem.n)

    def dma_group(self, eng, fns, sem, deps=None):
        self._deps(eng, deps)
        for fn in fns:
            sem.n += 16
            self.ops[eng].append(("op", fn, sem, 16))
        return (sem, sem.n)

    def wait(self, eng, deps):
        self._deps(eng, deps)

    def emit(self, block):
        def run(e, lst):
            for it in lst:
                if it[0] == "wait":
                    e.wait_ge(it[1].h, it[2])
                else:
                    ins = it[1](e)
                    if it[2] is not None:
                        ins.then_inc(it[2].h, it[3])

        ops = self.ops

        @block.tensor
        def _(e):
            run(e, ops["pe"])

        @block.scalar
        def _(e):
            run(e, ops["act"])

        @block.vector
        def _(e):
            run(e, ops["dve"])

        @block.gpsimd
        def _(e):
            run(e, ops["pool"])

        @block.sync
        def _(e):
            run(e, ops["sp"])


def build(cfg):
    nc = bass.Bass("TRN2", target_bir_lowering=False)
    D, KC, FC, T, NTA, OWN = cfg.D, cfg.KC, cfg.FC, cfg.T, cfg.NTA, cfg.OWN
    NG, NQ, NKV, NIN = cfg.NG, cfg.NQ, cfg.NKV, cfg.NIN
    NDP, DPW = cfg.NDP, cfg.DPW
    NSA, NSO = cfg.NSA, cfg.NSO
    ALPHA = cfg.ALPHA
    NR = OWN + cfg.HALO

    def din(name, shape, dt=F32):
        return nc.dram_tensor(name, list(shape), dt, kind="ExternalInput").ap()

    def dscr(name, shape, dt=F32):
        return nc.dram_tensor(name, list(shape), dt, kind="Internal").ap()

    xin = din("xin", [NTA, D])
    cT_in = din("cT", [128, KC, 2])
    wada = din("wada", [9 * KC, 128, KC * 128])
    badaT_in = din("badaT", [128, 9 * KC])
    wg = [din("wg%d" % i, [FC, 128, KC * 128]) for i in range(2)]
    wu = [din("wu%d" % i, [FC, 128, KC * 128]) for i in range(2)]
    wd = [din("wd%d" % i, [NDP, FC, 128, DPW]) for i in range(2)]
    win = din("win", [NIN, 128, KC * 128])
    wout = din("wout", [NDP, KC, 128, DPW])
    wsT_in = din("wsT", [128, NG * 128])
    bsp_in = din("bsp", [128, NG * 128])
    sink_in = din("sinkb", [128, NQ])
    gmixT_in = din("gmixT", [128, NG + NQ])
    lng_in = din("lng", [3, 128, D])
    lnb_in = din("lnb", [3, 128, D])
    cos_in = din("cos2", [128, NR])
    sin_in = din("sin2", [128, NR])
    masks_in = din("masks", [4, 128, 512], BF16)
    ident_in = din("ident", [128, 128])
    out = nc.dram_tensor("out", [OWN, D], F32, kind="ExternalOutput").ap()

    hTs = [dscr("hT%d" % i, [KC, 128, NTA], BF16) for i in range(3)]
    Ys = dscr("Yscr", [NTA, D])
    xa = dscr("xa", [NTA, D])
    xmid = dscr("xmid", [OWN, D])
    gbc = dscr("gbc", [4, 128, D])

    import os as _os
    es = ExitStack()
    P = Prog(nc, es)

    arena = Arena(nc, es, 212800)

    def gsb(name, shape, dt):
        return arena.alloc(list(shape), dt)

    ps = es.enter_context(nc.psum_tensor("ps", [128, 8, 512], F32))
    ident = gsb("ident", [128, 128], F32)
    modT = gsb("modT", [128, 9 * KC, 2], F32)
    csT = gsb("csT", [128, KC, 2], BF16)
    badaT = gsb("badaT", [128, 9 * KC], F32)
    cT_f = gsb("cT_f", [128, KC, 2], F32)
    arena_base = [arena.off]
    s_misc = P.newsem("misc", persistent=True)
    t_ident = P.dma("sp", lambda e: e.dma_start(out=ident[:], in_=ident_in), s_misc)
    s_c0 = P.newsem("c0", persistent=True)
    t_c0 = P.dma_group("sp", [lambda e: e.dma_start(out=cT_f[:], in_=cT_in),
                              lambda e: e.dma_start(out=badaT[:], in_=badaT_in)], s_c0)
    t_cs = P.op("act", lambda e: e.activation(out=csT[:], in_=cT_f[:], func=AF.Silu), deps=[t_c0])

    bank_free = [None] * 8
    state = {"all": []}

    arena_peak = [0]

    def phase_barrier():
        arena_peak[0] = max(arena_peak[0], arena.off)
        if _os.environ.get("KVERBOSE"):
            print("phase end: arena off", arena.off, "ops", {k: len(v) for k, v in P.ops.items()})
        toks = [(P.prog[e], P.prog[e].n) for e in ("pe", "act", "dve", "pool") if P.prog[e].n > 0]
        best = {}
        for (sem, val) in state["all"]:
            if sem.name not in best or best[sem.name][1] < val:
                best[sem.name] = (sem, val)
        toks += list(best.values())
        for e in Prog.ENG:
            P.wait(e, toks)
        state["all"] = []
        P.release_phase_sems()

    def rsqrt_chain(out_ap, in_ap, scale, deps):
        t1 = P.op("dve", lambda e: e.tensor_scalar(out=out_ap, in0=in_ap, scalar1=float(scale), scalar2=EPS,
                                                   op0=ALU.mult, op1=ALU.add), deps=deps)
        t2 = P.op("act", lambda e: e.activation(out=out_ap, in_=out_ap, func=AF.Sqrt), deps=[t1])
        t3 = P.op("dve", lambda e: e.reciprocal(out=out_ap, in_=out_ap), deps=[t2])
        return t3

    def track(tok):
        state["all"].append(tok)
        return tok

    class AdaStream:
        def __init__(self):
            self.grp_tok = {}

        def chunk(self, ring, grp, c):
            bank = 4 + (grp % 2)
            cc = grp * KC + c
            slot, t_w = ring.load(wada[cc])
            tk = None
            for k in range(KC):
                last = k == KC - 1
                tk = P.op("pe", lambda e, bank=bank, c=c, slot=slot, k=k, last=last: e.matmul(
                    ps[:, bank, 2 * c:2 * c + 2], lhsT=ring.buf[:, slot, k * 128:(k + 1) * 128],
                    rhs=csT[:, k, :], start=(k == 0), stop=last),
                    deps=[t_w, t_cs, bank_free[bank] if c == 0 else None] if k == 0 else None, sig=last)
            ring.free[slot] = tk
            if c == KC - 1:
                psv = ps[:, bank, 0:2 * KC].rearrange("p (c v) -> p c v", v=2)
                t_mod = None
                for v in range(2):
                    t_mod = P.op("dve", lambda e, grp=grp, v=v, psv=psv: e.tensor_tensor(
                        out=modT[:, grp * KC:(grp + 1) * KC, v], in0=psv[:, :, v],
                        in1=badaT[:, grp * KC:(grp + 1) * KC], op=ALU.add), deps=[tk, t_cs])
                bank_free[bank] = t_mod
                if grp % 3 == 1:
                    g0 = grp * KC
                    t_mod = P.op("dve", lambda e, g0=g0: e.tensor_scalar(
                        out=modT[:, g0:g0 + KC, :], in0=modT[:, g0:g0 + KC, :], scalar1=1.0, scalar2=None,
                        op0=ALU.add), deps=[t_mod])
                self.grp_tok[grp] = t_mod

    ada = AdaStream()

    def ada_plan(groups, ntiles):
        per = [[] for _ in range(ntiles)]
        for i, g in enumerate(groups):
            per[min(ntiles - 1, i * ntiles // max(1, len(groups)))] += [(g, c) for c in range(KC)]
        return per

    def phase0():
        arena.off = arena_base[0]
        ring = GRing(None, "adaring", 4)
        for grp in (0, 1):
            for c in range(KC):
                ada.chunk(ring, grp, c)
        phase_barrier()

    def make_gates(sb, gis):
        gates = [(0, 0, 0.5), (0, 1, 0.5), (1, 0, 1.0), (2, 0, 0.5)]
        ones_f = sb("ones_f", [128, 128], F32)
        dm = sb("dm", [128, 8, 128], F32)
        gs = sb("gs", [128, D], F32)
        s_st = P.newsem("gst")
        t_ones = P.op("dve", lambda e: e.memset(ones_f[:], 1.0))
        dm_free = [None] * 8
        gs_free = None
        di = 0
        toks = []
        for gi in gis:
            s_, v, fac = gates[gi]
            t_ev = None
            for c in range(KC):
                q = c % 4
                bank = 2 + ((c // 4) % 2)
                col = (s_ * 3 + 2) * KC + c
                dmi = di % 8
                di += 1
                t_dm = P.op("dve", lambda e, dmi=dmi, col=col, v=v: e.tensor_scalar(
                    out=dm[:, dmi, :], in0=ident[:], scalar1=modT[:, col, v:v + 1], scalar2=None,
                    op0=ALU.mult), deps=[t_ident, dm_free[dmi]])
                t_mm = P.op("pe", lambda e, bank=bank, q=q, dmi=dmi: e.matmul(
                    ps[:, bank, q * 128:(q + 1) * 128], lhsT=ones_f[:], rhs=dm[:, dmi, :],
                    start=True, stop=True), deps=[t_dm, t_ones, bank_free[bank] if q == 0 else None])
                dm_free[dmi] = t_mm
                if q == 3 or c == KC - 1:
                    c0 = c - q
                    t_ev = P.op("act", lambda e, bank=bank, c0=c0, c=c, q=q, fac=fac: e.activation(
                        out=gs[:, c0 * 128:(c + 1) * 128], in_=ps[:, bank, 0:(q + 1) * 128],
                        func=AF.Identity, scale=fac), deps=[t_mm, gs_free])
                    bank_free[bank] = t_ev
            gs_free = track(P.dma("sp", lambda e, gi=gi: e.dma_start(out=gbc[gi], in_=gs[:]), s_st, deps=[t_ev]))
            toks.append(gs_free)
        return toks

    def lpass(name, nsub, src_y, src_res, gate_of, ln_idx, x_dst, hT_dst, mod_s, v_of, gate_make=()):
        with ExitStack() as ph:
            arena.off = arena_base[0]

            def sb(nm, shape, dt):
                return arena.alloc(list(shape), dt)
            NB = 3
            do_ln = src_y is not None
            t_gates = make_gates(sb, list(gate_make)) if gate_make else []
            rb = sb("rb", [128, NB, D], F32)
            s_ld = P.newsem(name + "ld")
            s_rb = [P.newsem(name + "rb") for _ in range(NB)]
            s_yb = [P.newsem(name + "yb") for _ in range(NB)]
            s_xs = [P.newsem(name + "xs") for _ in range(NB)]
            s_hs = [P.newsem(name + "hs") for _ in range(2)]
            NCH = D // 512
            if do_ln:
                yb = sb("yb", [128, NB, D], F32)
                gidx = sorted(set(gate_of(st) for st in range(nsub)))
                gt = sb("gt", [128, len(gidx), D], F32)
                gn = sb("gn", [128, D], F32)
                bs = sb("bs", [128, D], F32)
                stats = sb("stats", [128, NB, NCH * 6], F32)
                mv = sb("mv", [128, NB, 2], F32)
                rstd = sb("rstd", [128, NB, 1], F32)
                fns = [(lambda e, i=i, g=g: e.dma_start(out=gt[:, i, :], in_=gbc[g])) for i, g in enumerate(gidx)]
                fns.append(lambda e: e.dma_start(out=gn[:], in_=lng_in[ln_idx]))
                fns.append(lambda e: e.dma_start(out=bs[:], in_=lnb_in[ln_idx]))
                t_consts = [P.dma_group("sp", fns, s_ld, deps=t_gates)]
            if hT_dst is not None:
                ho = sb("ho", [128, 2, KC, 128], BF16)
            rb_free = [None] * NB
            yb_free = [None] * NB
            ho_free = [None] * 2
            loads = {}

            def issue_load(st):
                b = st % NB
                t1 = P.dma("sp", lambda e, st=st, b=b: e.dma_start(out=rb[:, b, :], in_=src_res[st * 128:(st + 1) * 128, :]),
                           s_rb[b], deps=rb_free[b])
                t2 = None
                if do_ln:
                    t2 = P.dma("sp", lambda e, st=st, b=b: e.dma_start(out=yb[:, b, :], in_=src_y[st * 128:(st + 1) * 128, :]),
                               s_yb[b], deps=yb_free[b])
                loads[st] = (t1, t2)

            issue_load(0)
            if nsub > 1:
                issue_load(1)
            for st in range(nsub):
                b = st % NB
                if st + 2 < nsub:
                    issue_load(st + 2)
                t1, t2 = loads[st]
                t_x = t1
                if do_ln:
                    gi = gidx.index(gate_of(st))
                    ta = P.op("dve", lambda e, b=b, gi=gi: e.tensor_tensor(
                        out=yb[:, b, :], in0=yb[:, b, :], in1=gt[:, gi, :], op=ALU.mult), deps=[t2] + t_consts)
                    tb = P.op("dve", lambda e, b=b: e.scalar_tensor_tensor(
                        out=rb[:, b, :], in0=rb[:, b, :], scalar=ALPHA, in1=yb[:, b, :], op0=ALU.mult, op1=ALU.add),
                        deps=[ta, t1])
                    yb_free[b] = [tb]
                    tc = None
                    for ch in range(NCH):
                        tc = P.op("dve", lambda e, b=b, ch=ch: e.bn_stats(
                            stats[:, b, ch * 6:(ch + 1) * 6], rb[:, b, ch * 512:(ch + 1) * 512]), deps=[tb])
                    td = P.op("dve", lambda e, b=b: e.bn_aggr(mv[:, b, :], stats[:, b, :]), deps=[tc])
                    te = rsqrt_chain(rstd[:, b, :], mv[:, b, 1:2], 1.0, [td])
                    tf = P.op("dve", lambda e, b=b: e.tensor_scalar(
                        out=rb[:, b, :], in0=rb[:, b, :], scalar1=mv[:, b, 0:1], scalar2=rstd[:, b, 0:1],
                        op0=ALU.subtract, op1=ALU.mult), deps=[te])
                    tg = P.op("pool", lambda e, b=b: e.tensor_tensor(
                        out=rb[:, b, :], in0=rb[:, b, :], in1=gn[:], op=ALU.mult), deps=[tf] + t_consts)
                    t_x = P.op("pool", lambda e, b=b: e.tensor_tensor(
                        out=rb[:, b, :], in0=rb[:, b, :], in1=bs[:], op=ALU.add), deps=[tg])
                frees = []
                if x_dst is not None:
                    frees.append(track(P.dma("sp", lambda e, st=st, b=b: e.dma_start(
                        out=x_dst[st * 128:(st + 1) * 128, :], in_=rb[:, b, :]), s_xs[b], deps=[t_x])))
                if hT_dst is not None:
                    v = v_of(st)
                    bh = st % 2
                    t_tp = None
                    t_ev = None
                    for c in range(KC):
                        q = c % 4
                        bank = (c // 4) % 8
                        t_tp = P.op("pe", lambda e, b=b, c=c, q=q, bank=bank: e.transpose(
                            out=ps[:, bank, q * 128:(q + 1) * 128], in_=rb[:, b, c * 128:(c + 1) * 128],
                            identity=ident[:]), deps=[t_x, t_ident, bank_free[bank] if q == 0 else None],
                            sig=(q == 3))
                        if q == 3:
                            for qq in range(4):
                                cc = c - 3 + qq
                                t_ev = P.op("act", lambda e, bh=bh, cc=cc, qq=qq, bank=bank, v=v: e.activation(
                                    out=ho[:, bh, cc, :], in_=ps[:, bank, qq * 128:(qq + 1) * 128], func=AF.Identity,
                                    bias=modT[:, (mod_s * 3 + 0) * KC + cc, v:v + 1],
                                    scale=modT[:, (mod_s * 3 + 1) * KC + cc, v:v + 1]),
                                    deps=[t_tp] + (ho_free[bh] or []))
                            bank_free[bank] = t_ev
                    frees.append(t_tp)
                    t_hs = track(P.dma("sp", lambda e, st=st, bh=bh: e.dma_start(
                        out=hT_dst[:, :, st * 128:(st + 1) * 128].rearrange("c p t -> p c t"), in_=ho[:, bh, :, :]),
                        s_hs[bh], deps=[t_ev]))
                    ho_free[bh] = [t_hs]
                rb_free[b] = frees
            phase_barrier()

    class GRing:
        def __init__(self, ph, name, nslots):
            self.n = nslots
            self.buf = arena.alloc([128, nslots, KC * 128], BF16)
            self.free = [None] * nslots
            self.i = 0
            self.sem = [P.newsem(name, kind="sw") for _ in range(nslots)]

        def load(self, src):
            slot = self.i % self.n
            self.i += 1
            tok = P.dma("pool", lambda e, slot=slot, src=src: e.dma_start(out=self.buf[:, slot, :], in_=src),
                        self.sem[slot], deps=[self.free[slot]])
            return slot, tok

    def gemm_g(ring, slot, t_w, hT, tw, bank, extra_deps):
        tk = None
        for k in range(KC):
            last = k == KC - 1
            tk = P.op("pe", lambda e, slot=slot, k=k, bank=bank, last=last: e.matmul(
                ps[:, bank, 0:tw], lhsT=ring.buf[:, slot, k * 128:(k + 1) * 128], rhs=hT[:, k, 0:tw],
                start=(k == 0), stop=last), deps=([t_w, bank_free[bank]] + extra_deps) if k == 0 else None, sig=last)
        ring.free[slot] = tk
        return tk

    def gemm_d(ph, name, aT, nchunks, wsrc, tw, y_dst_rows, dring, ybuf, yb_state, s_y, a_ready):
        S = tw // 128
        GC = 2
        for dp in range(NDP):
            tk = None
            for c0 in range(0, nchunks, GC):
                gc = min(GC, nchunks - c0)
                slot = dring["i"] % dring["n"]
                dring["i"] += 1
                t_w = P.dma("pool", lambda e, slot=slot, dp=dp, c0=c0, gc=gc: e.dma_start(
                    out=dring["buf"][:, slot, 0:gc, :], in_=wsrc[dp, c0:c0 + gc].rearrange("g p n -> p g n")),
                    dring["sem"][slot], deps=[dring["free"][slot]])
                for cl in range(gc):
                    c = c0 + cl
                    for s in range(S):
                        for hf in range(2):
                            bank = s * 2 + hf
                            first = c == 0
                            last = c == nchunks - 1
                            deps = None
                            if cl == 0 and s == 0 and hf == 0:
                                deps = [t_w] + a_ready
                            if first:
                                deps = (deps or []) + [bank_free[bank]]
                            sig = (cl == gc - 1 and s == S - 1 and hf == 1)
                            tk = P.op("pe", lambda e, bank=bank, c=c, s=s, slot=slot, cl=cl, hf=hf, first=first, last=last: e.matmul(
                                ps[:, bank, :], lhsT=aT[:, c, s * 128:(s + 1) * 128],
                                rhs=dring["buf"][:, slot, cl, hf * 512:(hf + 1) * 512], start=first, stop=last),
                                deps=deps, sig=sig)
                dring["free"][slot] = tk
            for s in range(S):
                yi = yb_state["i"] % len(yb_state["free"])
                yb_state["i"] += 1
                eng = "act" if s % 2 == 0 else "dve"
                if eng == "act":
                    t_ev = P.op("act", lambda e, yi=yi, s=s: e.activation(
                        out=ybuf[:, yi, :], in_=ps[:, 2 * s:2 * s + 2, :].rearrange("p a b -> p (a b)"), func=AF.Copy),
                        deps=[tk, yb_state["free"][yi]])
                else:
                    t_ev = P.op("dve", lambda e, yi=yi, s=s: e.tensor_copy(
                        out=ybuf[:, yi, :], in_=ps[:, 2 * s:2 * s + 2, :].rearrange("p a b -> p (a b)")),
                        deps=[tk, yb_state["free"][yi]])
                bank_free[2 * s] = t_ev
                bank_free[2 * s + 1] = t_ev
                r0 = y_dst_rows + s * 128
                yb_state["free"][yi] = track(P.dma("sp", lambda e, yi=yi, r0=r0, dp=dp: e.dma_start(
                    out=Ys[r0:r0 + 128, dp * DPW:(dp + 1) * DPW], in_=ybuf[:, yi, :]), s_y[yi], deps=[t_ev]))

    def make_dring(ph, name, n=4):
        return {"buf": arena.alloc([128, n, 2, DPW], BF16), "n": n, "i": 0,
                "free": [None] * n, "sem": [P.newsem(name, kind="sw") for _ in range(n)]}

    def ffn(name, tiles, hT_src, wg_, wu_, wd_, ada_groups=()):
        with ExitStack() as ph:
            arena.off = arena_base[0]

            def sb(nm, shape, dt):
                return arena.alloc(list(shape), dt)
            hT = sb("hT", [128, KC, T], BF16)
            actT = sb("actT", [128, FC, T], BF16)
            ring = GRing(ph, name + "ring", 5)
            dring = make_dring(ph, name + "dring")
            sgt = sb("sgt", [128, 2, T], F32)
            ybuf = sb("ybuf", [128, 4, DPW], F32)
            yb_state = {"i": 0, "free": [None] * 4}
            s_h = P.newsem(name + "h")
            s_y = [P.newsem(name + "y") for _ in range(4)]
            hT_free = None
            sgt_free = [None, None]
            t_h = None

            def load_h(t0, tw):
                return P.dma("sp", lambda e, t0=t0, tw=tw: e.dma_start(
                    out=hT[:, :, 0:tw], in_=hT_src[:, :, t0:t0 + tw].rearrange("c p t -> p c t")), s_h,
                    deps=[hT_free])

            t_h = load_h(*tiles[0])
            plan = ada_plan(list(ada_groups), len(tiles))
            for ti, (t0, tw) in enumerate(tiles):
                ev_toks = []
                tk = None
                pend = list(plan[ti])
                for j in range(FC):
                    pb = j % 2
                    npump = -(-len(pend) // (FC - j)) if j % 2 == 0 or len(pend) > (FC - j) else 0
                    for _ in range(npump):
                        ada.chunk(ring, *pend.pop(0))
                    slot_g, tw_g = ring.load(wg_[j])
                    slot_u, tw_u = ring.load(wu_[j])
                    tkg = gemm_g(ring, slot_g, tw_g, hT, tw, 2 * pb, [t_h])
                    tk = gemm_g(ring, slot_u, tw_u, hT, tw, 2 * pb + 1, [t_h])
                    t_s = P.op("act", lambda e, pb=pb, tw=tw: e.activation(
                        out=sgt[:, pb, 0:tw], in_=ps[:, 2 * pb, 0:tw], func=AF.Silu), deps=[tkg, sgt_free[pb]])
                    t_m = P.op("dve", lambda e, pb=pb, j=j, tw=tw: e.tensor_tensor(
                        out=actT[:, j, 0:tw], in0=sgt[:, pb, 0:tw], in1=ps[:, 2 * pb + 1, 0:tw], op=ALU.mult),
                        deps=[t_s, tk])
                    sgt_free[pb] = t_m
                    bank_free[2 * pb] = t_m
                    bank_free[2 * pb + 1] = t_m
                    ev_toks = [t_m] if j == FC - 1 else ev_toks
                hT_free = tk
                if ti + 1 < len(tiles):
                    t_h = load_h(*tiles[ti + 1])
                gemm_d(ph, name, actT, FC, wd_, tw, t0, dring, ybuf, yb_state, s_y, ev_toks + [bank_free[2], bank_free[0]])
            phase_barrier()

    def mixer():
        with ExitStack() as ph:
            arena.off = arena_base[0]

            def sb(nm, shape, dt):
                return arena.alloc(list(shape), dt)
            KT = sb("KT", [128, NKV, NTA], BF16)
            V = sb("V", [128, NSA, NKV * 128], BF16)
            hT = sb("hT", [128, KC, T], BF16)
            uT = sb("uT", [128, NG, T], BF16)
            vn = sb("vn", [128, 4, NG * 128], BF16)
            qT = sb("qT", [128, NQ, T], BF16)
            ring = GRing(ph, "mxring", 3)
            dring = make_dring(ph, "mxdring", 3)
            ybuf = sb("ybuf", [128, 2, DPW], F32)
            yb_state = {"i": 0, "free": [None] * 2}
            tmp = sb("tmp", [128, 4, 512], F32)
            PT = sb("PT", [128, 3, 512], BF16)
            sq = sb("sq", [128, 2, 512], BF16)
            cosb = sb("cosb", [128, 512], F32)
            sinb = sb("sinb", [128, 512], F32)
            masks = sb("masks", [128, 4, 512], BF16)
            bsp = sb("bsp", [128, NG * 128], F32)
            wsT = sb("wsT", [128, NG * 128], BF16)
            esink = sb("esink", [128, NQ], F32)
            gmixT = sb("gmixT", [128, NG + NQ], F32)
            ones_b = sb("ones_b", [128, 128], BF16)
            lsum = sb("lsum", [128, 512], F32)
            ssum = sb("ssum", [128, 2, 128], F32)
            rs = sb("rs", [128, 2, 128], F32)
            NVC = max(1, NG * 128 // 512)
            vst = sb("vst", [128, 4, NVC * 6], F32)
            vmv = sb("vmv", [128, 4, 2], F32)
            vrs = sb("vrs", [128, 4, 1], F32)
            s_c = P.newsem("mxc")
            s_h = P.newsem("mxh")
            s_y = [P.newsem("mxy") for _ in range(4)]
            s_t = P.newsem("mxt")
            fns = [(lambda e, k_=k_: e.dma_start(out=masks[:, k_, :], in_=masks_in[k_])) for k_ in range(4)]
            fns.append(lambda e: e.dma_start(out=bsp[:], in_=bsp_in))
            fns.append(lambda e: e.dma_start(out=esink[:], in_=sink_in))
            fns.append(lambda e: e.dma_start(out=gmixT[:], in_=gmixT_in))
            tc_ = [P.dma_group("sp", fns, s_c)]
            t_k1 = P.op("act", lambda e: e.activation(out=esink[:], in_=esink[:], func=AF.Exp), deps=tc_)
            s_ws = P.newsem("mxws", kind="sw")
            t_k2 = P.dma("pool", lambda e: e.dma_start(out=wsT[:], in_=wsT_in), s_ws)
            t_k3 = P.op("dve", lambda e: e.memset(ones_b[:], 1.0))
            consts = tc_ + [t_k1, t_k2, t_k3]
            chk(10)
            SC = 1.0 / float(np.sqrt(128.0))
            tmp_free = [None] * 4
            tmp_i = [0]

            def tmp_slot():
                i = tmp_i[0] % 4
                tmp_i[0] += 1
                return i

            hT_free = [None]

            def load_h(t0, tw):
                return P.dma("sp", lambda e, t0=t0, tw=tw: e.dma_start(
                    out=hT[:, :, 0:tw], in_=hTs[1][:, :, t0:t0 + tw].rearrange("c p t -> p c t")), s_h,
                    deps=hT_free[0])

            rope_free = [None]

            def load_rope(t0, tw):
                d = rope_free[0]
                a = P.dma_group("sp", [lambda e, t0=t0, tw=tw: e.dma_start(out=cosb[:, 0:tw], in_=cos_in[:, t0:t0 + tw]),
                                       lambda e, t0=t0, tw=tw: e.dma_start(out=sinb[:, 0:tw], in_=sin_in[:, t0:t0 + tw])],
                                s_t, deps=d)
                return [a]

            MXVAR = int(_os.environ.get("MXVAR", "0"))

            def rope_evac(bank, tw_r, dst, t_mm, t_rope):
                if MXVAR == 1:
                    t = P.op("act", lambda e: e.activation(out=dst, in_=ps[:, bank, 0:tw_r], func=AF.Copy), deps=[t_mm])
                    return t, [t]
                if MXVAR == 2:
                    i0 = tmp_slot()
                    ta = P.op("act", lambda e: e.activation(out=tmp[0:64, i0, 0:tw_r], in_=ps[64:128, bank, 0:tw_r], func=AF.Copy),
                              deps=[t_mm, tmp_free[i0]])
                    tb = P.op("act", lambda e: e.activation(out=tmp[64:128, i0, 0:tw_r], in_=ps[0:64, bank, 0:tw_r], func=AF.Copy),
                              deps=[t_mm])
                    t = P.op("act", lambda e: e.activation(out=dst, in_=tmp[:, i0, 0:tw_r], func=AF.Copy), deps=[ta, tb])
                    tmp_free[i0] = t
                    return t, [t]
                if MXVAR == 3:
                    i0 = tmp_slot()
                    ta = P.op("dve", lambda e: e.tensor_tensor(out=tmp[:, i0, 0:tw_r], in0=ps[:, bank, 0:tw_r], in1=cosb[:, 0:tw_r],
                                                             op=ALU.mult), deps=[t_mm, tmp_free[i0]] + t_rope)
                    t = P.op("pool", lambda e: e.tensor_tensor(out=tmp[:, i0, 0:tw_r], in0=tmp[:, i0, 0:tw_r], in1=sinb[:, 0:tw_r],
                                                            op=ALU.mult), deps=[ta] + t_rope)
                    t2 = P.op("dve", lambda e: e.tensor_copy(out=dst, in_=tmp[:, i0, 0:tw_r]), deps=[t])
                    tmp_free[i0] = t2
                    return t2, [ta]
                i0 = tmp_slot()
                i1 = tmp_slot()
                ta = P.op("act", lambda e: e.activation(out=tmp[0:64, i0, 0:tw_r], in_=ps[64:128, bank, 0:tw_r], func=AF.Copy),
                          deps=[t_mm, tmp_free[i0]])
                tb = P.op("act", lambda e: e.activation(out=tmp[64:128, i0, 0:tw_r], in_=ps[0:64, bank, 0:tw_r], func=AF.Copy),
                          deps=[t_mm])
                tc2 = P.op("dve", lambda e: e.tensor_tensor(out=tmp[:, i1, 0:tw_r], in0=ps[:, bank, 0:tw_r], in1=cosb[:, 0:tw_r],
                                                             op=ALU.mult), deps=[t_mm, tb, tmp_free[i1]] + t_rope)
                td = P.op("dve", lambda e: e.tensor_tensor(out=tmp[:, i0, 0:tw_r], in0=tmp[:, i0, 0:tw_r], in1=sinb[:, 0:tw_r],
                                                            op=ALU.mult), deps=[ta, tb] + t_rope + ([tc2] if MXVAR == 5 else []))
                if MXVAR == 6:
                    te0 = P.op("dve", lambda e: e.tensor_tensor(out=tmp[:, i1, 0:tw_r], in0=tmp[:, i1, 0:tw_r], in1=tmp[:, i0, 0:tw_r], op=ALU.add),
                               deps=[tc2, td])
                    te = P.op("dve", lambda e: e.tensor_copy(out=dst, in_=tmp[:, i1, 0:tw_r]), deps=[te0])
                elif MXVAR == 7:
                    te = P.op("dve", lambda e: e.tensor_copy(out=dst, in_=tmp[:, i1, 0:tw_r]), deps=[tc2, td])
                else:
                    te = P.op("dve", lambda e: e.tensor_tensor(out=dst, in0=tmp[:, i1, 0:tw_r], in1=tmp[:, i0, 0:tw_r], op=ALU.add),
                              deps=[tc2, td])
                tmp_free[i0] = te
                tmp_free[i1] = te
                return te, [tb, tc2]

            def transpose_to(src_f32, nsub_, dsts, t_src, bank):
                t_tp = None
                for s in range(nsub_):
                    t_tp = P.op("pe", lambda e, s=s: e.transpose(out=ps[:, bank, s * 128:(s + 1) * 128],
                                                                 in_=src_f32[:, s * 128:(s + 1) * 128], identity=ident[:]),
                                deps=[t_src, t_ident, bank_free[bank] if s == 0 else None], sig=(s == nsub_ - 1))
                t_ev = None
                for s in range(nsub_):
                    t_ev = P.op("dve", lambda e, s=s: e.tensor_copy(out=dsts[s], in_=ps[:, bank, s * 128:(s + 1) * 128]),
                                deps=[t_tp])
                bank_free[bank] = t_ev
                return t_tp, t_ev

            tiles_all = [(t0, min(T, NTA - t0)) for t0 in range(0, NTA, T)]
            t_h = load_h(*tiles_all[0])
            kv_ready = []
            for ti, (t0, tw) in enumerate(tiles_all):
                tw_r = max(0, min(tw, NR - t0))
                t_rope = load_rope(t0, tw_r) if tw_r > 0 else []
                tk = None
                readers = []
                for jj in range(2 * NKV):
                    j = 2 * NG + NQ + jj
                    bank = jj % 4
                    slot, t_w = ring.load(win[j])
                    tk = gemm_g(ring, slot, t_w, hT, tw, bank, [t_h])
                    if pend_tp[0] is not None:
                        pend_tp[0]()
                        pend_tp[0] = None
                    if jj == 1:
                        chk(11)
                    if jj < NKV:
                        toks = []
                        if tw_r < tw:
                            tcp = P.op("act", lambda e, jj=jj, t0=t0, tw=tw, tw_r=tw_r, bank=bank: e.activation(
                                out=KT[:, jj, t0 + tw_r:t0 + tw], in_=ps[:, bank, tw_r:tw], func=AF.Copy), deps=[tk])
                            toks.append(tcp)
                        if tw_r > 0:
                            te, rd = rope_evac(bank, tw_r, KT[:, jj, t0:t0 + tw_r], tk, t_rope)
                            toks += [te] + rd
                            readers += rd
                        bank_free[bank] = toks
                        kv_ready.append(toks)
                    else:
                        h = jj - NKV
                        i0 = tmp_slot()
                        t_c = P.op("act", lambda e, i0=i0, bank=bank, tw=tw: e.activation(
                            out=tmp[:, i0, 0:tw], in_=ps[:, bank, 0:tw], func=AF.Copy), deps=[tk, tmp_free[i0]])
                        bank_free[bank] = t_c
                        nsub_ = tw // 128
                        dsts = [V[:, t0 // 128 + s, h * 128:(h + 1) * 128] for s in range(nsub_)]

                        def do_tp0(i0=i0, dsts=dsts, t_c=t_c, jj=jj, nsub_=nsub_):
                            t_tp, t_ev = transpose_to(tmp[:, i0, :], nsub_, dsts, t_c, 6 + (jj % 2))
                            tmp_free[i0] = t_tp
                            kv_ready.append(t_ev)
                        pend_tp[0] = do_tp0
                if pend_tp[0] is not None:
                    pend_tp[0]()
                    pend_tp[0] = None
                chk(12)
                rope_free[0] = readers + [bank_free[b_] for b_ in range(4)]
                hT_free[0] = [tk]
                if ti + 1 < len(tiles_all):
                    t_h = load_h(*tiles_all[ti + 1])

            chk(0)
            tiles_own = [(t0, min(T, OWN - t0)) for t0 in range(0, OWN, T)]
            t_h = load_h(*tiles_own[0])
            yT = hT
            plan_mx = ada_plan([5, 6, 7, 8], len(tiles_own))
            for ti, (t0, tw) in enumerate(tiles_own):
                S = tw // 128
                t_rope = load_rope(t0, tw)
                pend = list(plan_mx[ti])
                readers = []
                vn_ready = []
                u_ready = []
                q_ready = []
                tk = None
                NJ = 2 * NG + NQ
                for j in range(NJ):
                    bank = j % 4
                    for _ in range(-(-len(pend) // (NJ - j))):
                        ada.chunk(ring, *pend.pop(0))
                    slot, t_w = ring.load(win[j])
                    tk = gemm_g(ring, slot, t_w, hT, tw, bank, [t_h])
                    if pend_tp[0] is not None:
                        pend_tp[0]()
                        pend_tp[0] = None
                    if j < NG:
                        t_e = P.op("act", lambda e, j=j, bank=bank, tw=tw: e.activation(
                            out=uT[:, j, 0:tw], in_=ps[:, bank, 0:tw], func=AF.Gelu), deps=[tk])
                        bank_free[bank] = t_e
                        u_ready.append(t_e)
                    elif j < 2 * NG:
                        g = j - NG
                        i0 = tmp_slot()
                        t_c = P.op("act", lambda e, i0=i0, bank=bank, tw=tw: e.activation(
                            out=tmp[:, i0, 0:tw], in_=ps[:, bank, 0:tw], func=AF.Gelu), deps=[tk, tmp_free[i0]])
                        bank_free[bank] = t_c
                        dsts = [vn[:, s, g * 128:(g + 1) * 128] for s in range(S)]

                        def do_tp(i0=i0, dsts=dsts, t_c=t_c, j=j, S=S):
                            t_tp, t_ev = transpose_to(tmp[:, i0, :], S, dsts, t_c, 6 + (j % 2))
                            tmp_free[i0] = t_tp
                            vn_ready.append(t_ev)
                        pend_tp[0] = do_tp
                    else:
                        hq = j - 2 * NG
                        te, rd = rope_evac(bank, tw, qT[:, hq, 0:tw], tk, t_rope)
                        readers += rd
                        bank_free[bank] = [te] + rd
                        q_ready.append(te)
                if pend_tp[0] is not None:
                    pend_tp[0]()
                    pend_tp[0] = None
                rope_free[0] = readers + [bank_free[b_] for b_ in range(4)]
                chk(1)
                for s in range(S):
                    t1 = None
                    VW = NG * 128 // NVC
                    for ch in range(NVC):
                        t1 = P.op("dve", lambda e, s=s, ch=ch: e.bn_stats(
                            vst[:, s, ch * 6:(ch + 1) * 6], vn[:, s, ch * VW:(ch + 1) * VW]), deps=vn_ready)
                    t2 = P.op("dve", lambda e, s=s: e.bn_aggr(vmv[:, s, :], vst[:, s, :]), deps=[t1])
                    t3 = rsqrt_chain(vrs[:, s, :], vmv[:, s, 1:2], 1.0, [t2])
                    t4 = P.op("dve", lambda e, s=s: e.tensor_scalar(
                        out=vn[:, s, :], in0=vn[:, s, :], scalar1=vmv[:, s, 0:1], scalar2=vrs[:, s, 0:1],
                        op0=ALU.subtract, op1=ALU.mult), deps=[t3])
                    vn_ready.append(t4)
                chk(2)
                y_wr = [tk]
                SSB = 6
                n4 = (NG + 3) // 4
                msteps = [(s_, g4) for s_ in range(S) for g4 in range(n4)]
                ypart = {}

                def mlp_final(s_, t_ssm):
                    wl = min(4, NG)
                    tr = P.op("dve", lambda e, wl=wl: e.tensor_reduce(
                        out=ssum[:, 0, :], in_=ps[:, SSB, 0:wl * 128].rearrange("p (a b) -> p b a", b=128),
                        axis=mybir.AxisListType.X, op=ALU.add), deps=[t_ssm, ss_free[0]])
                    bank_free[SSB] = tr
                    tr3 = rsqrt_chain(rs[:, 0, :], ssum[:, 0, :], 1.0 / (NG * 128), [tr, rs_free[0]])
                    ss_free[0] = tr3
                    tl = None
                    for g in range(NG):
                        tl = P.op("dve", lambda e, g=g, s_=s_: e.scalar_tensor_tensor(
                            out=yT[:, g, s_ * 128:(s_ + 1) * 128], in0=yT[:, g, s_ * 128:(s_ + 1) * 128],
                            scalar=gmixT[:, g:g + 1], in1=rs[:, 0, :], op0=ALU.mult, op1=ALU.mult),
                            deps=[tr3] + ypart[("m", s_)] + consts)
                    rs_free[0] = tl
                    y_done.append(tl)

                def mlp_mixed(i):
                    s_, g4 = msteps[i]
                    g0 = g4 * 4
                    ng_ = min(4, NG - g0)
                    bank = 2 + (i % 2)
                    t_mm = None
                    for gg in range(ng_):
                        g = g0 + gg
                        t_mm = P.op("pe", lambda e, g=g, gg=gg, s_=s_, bank=bank: e.matmul(
                            ps[:, bank, gg * 128:(gg + 1) * 128], lhsT=vn[:, s_, g * 128:(g + 1) * 128],
                            rhs=wsT[:, g * 128:(g + 1) * 128], start=True, stop=True),
                            deps=(vn_ready + consts + [bank_free[bank]]) if gg == 0 else None, sig=(gg == ng_ - 1))
                    ti0 = tmp_slot()
                    w_ = ng_ * 128
                    ta = P.op("dve", lambda e, ti0=ti0, bank=bank, g0=g0, w_=w_: e.tensor_tensor(
                        out=tmp[:, ti0, 0:w_], in0=ps[:, bank, 0:w_], in1=bsp[:, g0 * 128:g0 * 128 + w_], op=ALU.add),
                        deps=[t_mm, tmp_free[ti0]])
                    bank_free[bank] = ta
                    tb = P.op("dve", lambda e, ti0=ti0, g0=g0, ng_=ng_, s_=s_: e.tensor_tensor(
                        out=tmp[:, ti0, 0:ng_ * 128].rearrange("p (a b) -> p a b", b=128),
                        in0=tmp[:, ti0, 0:ng_ * 128].rearrange("p (a b) -> p a b", b=128),
                        in1=uT[:, g0:g0 + ng_, s_ * 128:(s_ + 1) * 128], op=ALU.mult), deps=[ta] + u_ready)
                    sqi = sq_i[0] % 2
                    sq_i[0] += 1
                    tcq = P.op("act", lambda e, ti0=ti0, sqi=sqi, w_=w_: e.activation(
                        out=sq[:, sqi, 0:w_], in_=tmp[:, ti0, 0:w_], func=AF.Square), deps=[tb, sq_free[sqi]])
                    tdq = P.op("pool", lambda e, ti0=ti0, g0=g0, ng_=ng_, s_=s_: e.tensor_copy(
                        out=yT[:, g0:g0 + ng_, s_ * 128:(s_ + 1) * 128],
                        in_=tmp[:, ti0, 0:ng_ * 128].rearrange("p (a b) -> p a b", b=128)), deps=[tb] + y_wr)
                    tmp_free[ti0] = tdq
                    ypart.setdefault(("m", s_), []).append(tdq)
                    return (sqi, w_, tcq)

                def mlp_ss(i, st_):
                    s_, g4 = msteps[i]
                    sqi, w_, tcq = st_
                    t_ssm = P.op("pe", lambda e, sqi=sqi, w_=w_, g4=g4: e.matmul(
                        ps[:, SSB, 0:w_], lhsT=ones_b[:], rhs=sq[:, sqi, 0:w_], start=(g4 == 0), stop=(g4 == n4 - 1)),
                        deps=[tcq, consts[-1], bank_free[SSB] if g4 == 0 else None])
                    sq_free[sqi] = t_ssm
                    if g4 == n4 - 1:
                        mlp_final(s_, t_ssm)

                prev = None
                for i in range(len(msteps)):
                    cur = mlp_mixed(i)
                    if prev is not None:
                        mlp_ss(i - 1, prev)
                    prev = cur
                mlp_ss(len(msteps) - 1, prev)
                chk(3)
                def kbs_of(s_):
                    qb = t0 // 128 + s_
                    kbs = [(NR // 128 + cb, None) for cb in range(cfg.CTX // 128)]
                    kbs.append((OWN // 128, 2) if qb == 0 else (qb - 1, 0))
                    kbs.append((qb, None))
                    kbs.append((OWN // 128, 3) if qb == NSO - 1 else (qb + 1, 1))
                    return kbs

                groups = [(s_, h, kbs_of(s_)) for s_ in range(S) for h in range(NKV)]
                asteps = [(gi, ki) for gi, g_ in enumerate(groups) for ki in range(len(g_[2]))]
                SBK = [0, 1, 6]
                LA = 2
                exp_tok = {}
                pend_ss = [None]

                def att_S(i):
                    gi, ki = asteps[i]
                    s_, h, kbs = groups[gi]
                    kb, mk = kbs[ki]
                    bank = SBK[i % 3]
                    t_s = P.op("pe", lambda e, h=h, kb=kb, s_=s_, bank=bank: e.matmul(
                        ps[:, bank, :].rearrange("p (a b) -> p a b", b=128),
                        lhsT=KT[:, h, kb * 128:(kb + 1) * 128], rhs=qT[:, 4 * h:4 * h + 4, s_ * 128:(s_ + 1) * 128],
                        start=True, stop=True), deps=kv_ready + q_ready + [bank_free[bank]])
                    pi = pt_i[0] % 3
                    pt_i[0] += 1
                    t_e = P.op("act", lambda e, pi=pi, bank=bank: e.activation(
                        out=PT[:, pi, :], in_=ps[:, bank, :], func=AF.Exp, scale=SC), deps=[t_s, pt_free[pi]])
                    bank_free[bank] = t_e
                    if mk is not None:
                        t_e = P.op("pool", lambda e, pi=pi, mk=mk: e.tensor_tensor(
                            out=PT[:, pi, :], in0=PT[:, pi, :], in1=masks[:, mk, :], op=ALU.mult), deps=[t_e] + consts)
                    exp_tok[i] = (t_e, pi)

                def att_ss(gi, sqi, tcq):
                    s_, h, kbs = groups[gi]
                    t_ssa = P.op("pe", lambda e, sqi=sqi, h=h: e.matmul(
                        ps[:, 7, :], lhsT=ones_b[:], rhs=sq[:, sqi, :], start=(h == 0), stop=(h == NKV - 1)),
                        deps=[tcq, bank_free[7] if h == 0 else None])
                    sq_free[sqi] = t_ssa
                    if h == NKV - 1:
                        tr = P.op("dve", lambda e: e.tensor_reduce(
                            out=ssum[:, 1, :], in_=ps[:, 7, :].rearrange("p (a b) -> p b a", b=128),
                            axis=mybir.AxisListType.X, op=ALU.add), deps=[t_ssa, ss_free[1]])
                        bank_free[7] = tr
                        tr3 = rsqrt_chain(rs[:, 1, :], ssum[:, 1, :], 1.0 / (NQ * 128), [tr, rs_free[1]])
                        ss_free[1] = tr3
                        tl = None
                        for hq in range(NQ):
                            tl = P.op("dve", lambda e, hq=hq, s_=s_: e.scalar_tensor_tensor(
                                out=yT[:, NG + hq, s_ * 128:(s_ + 1) * 128], in0=yT[:, NG + hq, s_ * 128:(s_ + 1) * 128],
                                scalar=gmixT[:, NG + hq:NG + hq + 1], in1=rs[:, 1, :], op0=ALU.mult, op1=ALU.mult),
                                deps=[tr3] + ypart[("a", s_)] + consts)
                        rs_free[1] = tl
                        y_done.append(tl)

                def att_PV(i):
                    gi, ki = asteps[i]
                    s_, h, kbs = groups[gi]
                    kb, mk = kbs[ki]
                    OB, LB = (4, 5) if gi % 2 == 0 else (2, 3)
                    t_e, pi = exp_tok.pop(i)
                    first = ki == 0
                    last = ki == len(kbs) - 1
                    P.op("pe", lambda e, pi=pi, kb=kb, h=h, first=first, last=last: e.matmul(
                        ps[:, OB, :], lhsT=V[:, kb, h * 128:(h + 1) * 128], rhs=PT[:, pi, :], start=first, stop=last),
                        deps=[t_e, bank_free[OB] if first else None], sig=False)
                    t_l = P.op("pe", lambda e, pi=pi, first=first, last=last: e.matmul(
                        ps[:, LB, :], lhsT=ones_b[:], rhs=PT[:, pi, :], start=first, stop=last),
                        deps=[bank_free[LB] if first else None])
                    pt_free[pi] = t_l
                    if not last:
                        return
                    tn = None
                    for g in range(4):
                        tn = P.op("dve", lambda e, g=g, h=h, LB=LB: e.tensor_scalar(
                            out=lsum[:, g * 128:(g + 1) * 128], in0=ps[:, LB, g * 128:(g + 1) * 128],
                            scalar1=esink[:, 4 * h + g:4 * h + g + 1], scalar2=None, op0=ALU.add),
                            deps=[t_l, ls_free[0]] + consts)
                    bank_free[LB] = tn
                    tn2 = P.op("dve", lambda e: e.reciprocal(out=lsum[:], in_=lsum[:]), deps=[tn])
                    ti0 = tmp_slot()
                    tn3 = P.op("dve", lambda e, ti0=ti0, OB=OB: e.tensor_tensor(
                        out=tmp[:, ti0, :], in0=ps[:, OB, :], in1=lsum[:], op=ALU.mult), deps=[tn2, tmp_free[ti0]])
                    bank_free[OB] = tn3
                    ls_free[0] = tn3
                    sqi = sq_i[0] % 2
                    sq_i[0] += 1
                    tcq = P.op("act", lambda e, ti0=ti0, sqi=sqi: e.activation(
                        out=sq[:, sqi, :], in_=tmp[:, ti0, :], func=AF.Square), deps=[tn3, sq_free[sqi]])
                    tdq = P.op("pool", lambda e, ti0=ti0, h=h, s_=s_: e.tensor_copy(
                        out=yT[:, NG + 4 * h:NG + 4 * h + 4, s_ * 128:(s_ + 1) * 128],
                        in_=tmp[:, ti0, :].rearrange("p (a b) -> p a b", b=128)), deps=[tn3] + y_wr)
                    tmp_free[ti0] = tdq
                    ypart.setdefault(("a", s_), []).append(tdq)
                    if pend_ss[0] is not None:
                        att_ss(*pend_ss[0])
                    pend_ss[0] = (gi, sqi, tcq)

                na = len(asteps)
                for i in range(min(LA, na)):
                    att_S(i)
                for i in range(na):
                    if i + LA < na:
                        att_S(i + LA)
                    att_PV(i)
                if pend_ss[0] is not None:
                    att_ss(*pend_ss[0])
                    pend_ss[0] = None
                chk(4)
                gemm_d(ph, "mx", yT, KC, wout, tw, t0, dring, ybuf, yb_state, s_y, list(y_done) + [bank_free[b_] for b_ in range(8)])
                del y_done[:]
                hT_free[0] = [(P.prog["pe"], P.prog["pe"].n)]
                if ti + 1 < len(tiles_own):
                    t_h = load_h(*tiles_own[ti + 1])
            phase_barrier()

    def mixer_wrap():
        try:
            mixer()
        except _Stop:
            phase_barrier()

    MX = int(_os.environ.get("MXSTOP", "99"))

    def chk(n):
        if MX == n:
            raise _Stop()

    sq_free = [None, None]
    ss_free = [None, None]
    rs_free = [None, None]
    ls_free = [None]
    pt_free = [None, None, None]
    pt_i = [0]
    pend_tp = [None]
    sq_i = [0]
    y_parts = []
    y_done = []

    tiles_a = [(t0, min(T, NTA - t0)) for t0 in range(0, NTA, T)]
    tiles_c = [(t0, min(T, OWN - t0)) for t0 in range(0, OWN, T)]
    ctx_st0 = (OWN + cfg.HALO) // 128
    v_of_a = lambda st: 1 if st >= ctx_st0 else 0

    import os
    kstop = int(os.environ.get("KSTOP", "99"))
    steps = [
        lambda: phase0(),
        lambda: lpass("p1", NSA, None, xin, None, None, None, hTs[0], 0, v_of_a),
        lambda: ffn("fa", tiles_a, hTs[0], wg[0], wu[0], wd[0], ada_groups=(2, 3, 4)),
        lambda: lpass("p3", NSA, Ys, xin, lambda st: v_of_a(st), 0, xa, hTs[1], 1, v_of_a, gate_make=(0, 1)),
        lambda: mixer_wrap(),
        lambda: lpass("p6", NSO, Ys, xa, lambda st: 2, 1, xmid, hTs[2], 2, lambda st: 0, gate_make=(2,)),
        lambda: ffn("fb", tiles_c, hTs[2], wg[1], wu[1], wd[1]),
        lambda: lpass("p8", NSO, Ys, xmid, lambda st: 3, 2, out, None, None, None, gate_make=(3,)),
    ]
    for i_, st_ in enumerate(steps):
        if i_ <= kstop:
            st_()

    with nc.Block() as block:
        P.emit(block)
    es.close()
    return nc


def _rope_tables(cfg, pos):
    n_freq = 128 // 4
    inv_freq = (10000.0 ** (-np.arange(n_freq, dtype=np.float32) / n_freq)).astype(np.float32)
    row = (pos // cfg.GRID_W).astype(np.float32)
    col = (pos % cfg.GRID_W).astype(np.float32)
    ang = np.concatenate([row[:, None] * inv_freq, col[:, None] * inv_freq], axis=-1).astype(np.float32)
    cos = np.cos(ang).astype(np.float32).T
    sin = np.sin(ang).astype(np.float32).T
    cos2 = np.concatenate([cos, cos], 0)
    sin2 = np.concatenate([-sin, sin], 0)
    return np.ascontiguousarray(cos2), np.ascontiguousarray(sin2)


def _g_layout(w, KC):
    K, F = w.shape
    return np.ascontiguousarray(w.reshape(KC, 128, F // 128, 128).transpose(2, 1, 0, 3)).reshape(F // 128, 128, KC * 128)


def _d_layout(w, DPW):
    Cn, Dm = w.shape[0] // 128, w.shape[1]
    return np.ascontiguousarray(w.reshape(Cn, 128, Dm // DPW, DPW).transpose(2, 0, 1, 3))


def prepare_inputs(cfg, x, c, ctx, c_ctx, w_ada, b_ada, w_ffn_gate, w_ffn_up, w_ffn_down, w_in, w_spatial,
                   b_spatial, sink_logit, g_mix, w_out, ln_gain, ln_bias):
    D, KC = cfg.D, cfg.KC
    f32 = np.float32
    shared = {}
    shared["wada"] = _g_layout(np.asarray(w_ada[0], f32), KC)
    shared["badaT"] = np.ascontiguousarray(np.asarray(b_ada[0], f32).reshape(9 * KC, 128).T)
    for i in range(2):
        shared["wg%d" % i] = _g_layout(np.asarray(w_ffn_gate[0, i], f32), KC)
        shared["wu%d" % i] = _g_layout(np.asarray(w_ffn_up[0, i], f32), KC)
        shared["wd%d" % i] = _d_layout(np.asarray(w_ffn_down[0, i], f32), cfg.DPW)
    shared["win"] = _g_layout(np.asarray(w_in[0], f32), KC)
    shared["wout"] = _d_layout(np.asarray(w_out[0], f32), cfg.DPW)
    ws = np.asarray(w_spatial[0], f32)
    shared["wsT"] = np.ascontiguousarray(ws.transpose(2, 0, 1)).reshape(128, cfg.NG * 128)
    shared["bsp"] = np.ascontiguousarray(np.broadcast_to(np.asarray(b_spatial[0], f32).reshape(1, -1), (128, cfg.NG * 128)))
    shared["sinkb"] = np.ascontiguousarray(np.broadcast_to(np.asarray(sink_logit[0], f32).reshape(1, -1), (128, cfg.NQ)))
    shared["gmixT"] = np.ascontiguousarray(np.asarray(g_mix[0], f32).reshape(cfg.NG + cfg.NQ, 128).T)
    shared["lng"] = np.ascontiguousarray(np.broadcast_to(np.asarray(ln_gain[0], f32)[:, None, :], (3, 128, D)))
    shared["lnb"] = np.ascontiguousarray(np.broadcast_to(np.asarray(ln_bias[0], f32)[:, None, :], (3, 128, D)))
    shared["ident"] = np.eye(128, dtype=f32)
    kk = np.arange(128)[:, None]
    qq = np.arange(128)[None, :]
    m_prev = np.tile((kk >= qq).astype(f32), (1, 4))
    m_next = np.tile((kk <= qq).astype(f32), (1, 4))
    zero = np.zeros_like(m_prev)
    x = np.asarray(x, f32)
    ctx = np.asarray(ctx, f32)
    c = np.asarray(c, f32)
    c_ctx = np.asarray(c_ctx, f32)
    in_maps = []
    cps = cfg.cores_per_seq
    for core in range(cfg.n_cores):
        b, half = core // cps, core % cps
        o0 = half * cfg.OWN
        m = dict(shared)
        first = half == 0
        lastc = half == cps - 1
        if first:
            h0 = o0 + cfg.OWN
        else:
            h0 = o0 - 128
        m["xin"] = np.ascontiguousarray(np.concatenate([x[b, o0:o0 + cfg.OWN], x[b, h0:h0 + 128], ctx[b]], 0))
        cv = np.stack([c[b], c_ctx], -1)
        m["cT"] = np.ascontiguousarray(cv.reshape(KC, 128, 2).transpose(1, 0, 2))
        pos = np.concatenate([np.arange(o0, o0 + cfg.OWN), np.arange(h0, h0 + 128)])
        m["cos2"], m["sin2"] = _rope_tables(cfg, pos)
        fp = zero if first else m_prev
        ln_ = m_next if first else zero
        m["masks"] = np.stack([m_prev, m_next, fp, ln_], 0).astype(ml_dtypes.bfloat16)
        in_maps.append(m)
    return in_maps


_CACHE = {}


def run(cfg, inputs):
    assert cfg.cores_per_seq == 2, "kernel assumes two cores per sequence"
    key = (cfg.D, cfg.SEQ, cfg.BATCH, cfg.n_cores)
    if key not in _CACHE:
        _CACHE[key] = build(cfg)
    nc = _CACHE[key]
    in_maps = prepare_inputs(cfg, **inputs)
    res = run_bass_kernel_spmd(nc, in_maps, core_ids=list(range(cfg.n_cores)))
    outs = [np.asarray(r["out"], np.float32) for r in res.results]
    y = np.zeros((cfg.BATCH, cfg.SEQ, cfg.D), np.float32)
    cps = cfg.cores_per_seq
    for core in range(cfg.n_cores):
        b, half = core // cps, core % cps
        y[b, half * cfg.OWN:(half + 1) * cfg.OWN] = outs[core]
    return y


def kernel(**inputs):
    cfg = Cfg()
    return run(cfg, inputs)
```

```python
import numpy as np
import ml_dtypes
from contextlib import ExitStack
import concourse.bass as bass
import concourse.mybir as mybir
from concourse.bass_utils import run_bass_kernel_spmd

F32 = mybir.dt.float32
BF16 = mybir.dt.bfloat16
AF = mybir.ActivationFunctionType
ALU = mybir.AluOpType
EPS = 1e-6


class Cfg:
    def __init__(self, D=4096, SEQ=4096, BATCH=4, CTX=256, GRID_W=64, n_cores=8):
        self.D = D
        self.SEQ = SEQ
        self.BATCH = BATCH
        self.CTX = CTX
        self.GRID_W = GRID_W
        self.n_cores = n_cores
        self.HD = 128
        self.NG = (D // 2) // 128
        self.NQ = (D // 2) // 128
        self.NKV = self.NQ // 4
        self.DFF = ((8 * D // 3 + 255) // 256) * 256
        self.KC = D // 128
        self.FC = self.DFF // 128
        self.NIN = 2 * self.NG + self.NQ + 2 * self.NKV
        self.cores_per_seq = n_cores // BATCH
        self.OWN = SEQ // self.cores_per_seq
        self.HALO = 128
        self.NTA = self.OWN + self.HALO + CTX
        self.NSO = self.OWN // 128
        self.NSA = self.NTA // 128
        self.T = 512
        self.DPW = 1024
        self.NDP = D // self.DPW
        self.ALPHA = (2.0 * 1) ** 0.25


class _Stop(Exception):
    pass


class Sem:
    def __init__(self, h, name):
        self.h = h
        self.name = name
        self.n = 0


class Arena:
    def __init__(self, nc, es, nbytes):
        self.t = es.enter_context(nc.sbuf_tensor("arena", [128, nbytes // 2], BF16))
        self.nbytes = nbytes
        self.off = 0

    def alloc(self, shape, dt):
        n = 1
        for d in shape[1:]:
            n *= d
        size = n * (4 if dt == F32 else 2)
        size = (size + 63) // 64 * 64
        assert self.off + size <= self.nbytes, ("SBUF arena overflow", self.off, size, self.nbytes)
        ap = self.t[0:shape[0], self.off // 2:self.off // 2 + n * (2 if dt == F32 else 1)]
        self.off += size
        if dt == F32:
            ap = ap.bitcast(F32)
        if len(shape) == 3:
            ap = ap.rearrange("p (a b) -> p a b", b=shape[2])
        elif len(shape) == 4:
            ap = ap.rearrange("p (a b c) -> p a b c", b=shape[2], c=shape[3])
        return ap


class Prog:
    ENG = ("pe", "act", "dve", "pool", "sp")

    def __init__(self, nc, es):
        self.nc = nc
        self.es = es
        self.ops = {e: [] for e in self.ENG}
        self.waited = {e: {} for e in self.ENG}
        self.nsem = 0
        self.pool = {"hw": [], "sw": []}
        self.phase_sems = []
        self.prog = {e: self.newsem("prog_" + e, persistent=True) for e in ("pe", "act", "dve", "pool")}

    def newsem(self, name, persistent=False, kind="hw"):
        if not persistent and self.pool[kind]:
            sm = self.pool[kind].pop()
            self.phase_sems.append((kind, sm))
            return sm
        self.nsem += 1
        h = self.es.enter_context(self.nc.semaphore(name + "_%d" % self.nsem))
        sm = Sem(h, name + "_%d" % self.nsem)
        if not persistent:
            self.phase_sems.append((kind, sm))
        return sm

    def release_phase_sems(self):
        for kind, sm in self.phase_sems:
            self.pool[kind].append(sm)
        self.phase_sems = []

    def _deps(self, eng, deps):
        for d in deps or ():
            if d is None:
                continue
            if isinstance(d, list):
                self._deps(eng, d)
                continue
            sem, val = d
            if self.waited[eng].get(sem.name, 0) >= val:
                continue
            self.waited[eng][sem.name] = val
            self.ops[eng].append(("wait", sem, val))

    def op(self, eng, fn, deps=None, sig=True):
        self._deps(eng, deps)
        if sig:
            sem = self.prog[eng]
            sem.n += 1
            self.ops[eng].append(("op", fn, sem, 1))
            return (sem, sem.n)
        self.ops[eng].append(("op", fn, None, 0))
        return None

    def dma(self, eng, fn, sem, deps=None):
        self._deps(eng, deps)
        sem.n += 16
        self.ops[eng].append(("op", fn, sem, 16))
        return (sem, sem.n)

    def dma_group(self, eng, fns, sem, deps=None):
        self._deps(eng, deps)
        for fn in fns:
            sem.n += 16
            self.ops[eng].append(("op", fn, sem, 16))
        return (sem, sem.n)

    def wait(self, eng, deps):
        self._deps(eng, deps)

    def emit(self, block):
        def run(e, lst):
            for it in lst:
                if it[0] == "wait":
                    e.wait_ge(it[1].h, it[2])
                else:
                    ins = it[1](e)
                    if it[2] is not None:
                        ins.then_inc(it[2].h, it[3])

        ops = self.ops

        @block.tensor
        def _(e):
            run(e, ops["pe"])

        @block.scalar
        def _(e):
            run(e, ops["act"])

        @block.vector
        def _(e):
            run(e, ops["dve"])

        @block.gpsimd
        def _(e):
            run(e, ops["pool"])

        @block.sync
        def _(e):
            run(e, ops["sp"])


def build(cfg):
    nc = bass.Bass("TRN2", target_bir_lowering=False)
    D, KC, FC, T, NTA, OWN = cfg.D, cfg.KC, cfg.FC, cfg.T, cfg.NTA, cfg.OWN
    NG, NQ, NKV, NIN = cfg.NG, cfg.NQ, cfg.NKV, cfg.NIN
    NDP, DPW = cfg.NDP, cfg.DPW
    NSA, NSO = cfg.NSA, cfg.NSO
    ALPHA = cfg.ALPHA
    NR = OWN + cfg.HALO

    def din(name, shape, dt=F32):
        return nc.dram_tensor(name, list(shape), dt, kind="ExternalInput").ap()

    def dscr(name, shape, dt=F32):
        return nc.dram_tensor(name, list(shape), dt, kind="Internal").ap()

    xin = din("xin", [NTA, D])
    cT_in = din("cT", [128, KC, 2])
    wada = din("wada", [9 * KC, 128, KC * 128])
    badaT_in = din("badaT", [128, 9 * KC])
    wg = [din("wg%d" % i, [FC, 128, KC * 128]) for i in range(2)]
    wu = [din("wu%d" % i, [FC, 128, KC * 128]) for i in range(2)]
    wd = [din("wd%d" % i, [NDP, FC, 128, DPW]) for i in range(2)]
    win = din("win", [NIN, 128, KC * 128])
    wout = din("wout", [NDP, KC, 128, DPW])
    wsT_in = din("wsT", [128, NG * 128])
    bsp_in = din("bsp", [128, NG * 128])
    sink_in = din("sinkb", [128, NQ])
    gmixT_in = din("gmixT", [128, NG + NQ])
    lng_in = din("lng", [3, 128, D])
    lnb_in = din("lnb", [3, 128, D])
    cos_in = din("cos2", [128, NR])
    sin_in = din("sin2", [128, NR])
    masks_in = din("masks", [4, 128, 512], BF16)
    ident_in = din("ident", [128, 128])
    out = nc.dram_tensor("out", [OWN, D], F32, kind="ExternalOutput").ap()

    hTs = [dscr("hT%d" % i, [KC, 128, NTA], BF16) for i in range(3)]
    Ys = dscr("Yscr", [NTA, D])
    xa = dscr("xa", [NTA, D])
    xmid = dscr("xmid", [OWN, D])
    gbc = dscr("gbc", [4, 128, D])

    import os as _os
    es = ExitStack()
    P = Prog(nc, es)

    arena = Arena(nc, es, 212800)

    def gsb(name, shape, dt):
        return arena.alloc(list(shape), dt)

    ps = es.enter_context(nc.psum_tensor("ps", [128, 8, 512], F32))
    ident = gsb("ident", [128, 128], F32)
    modT = gsb("modT", [128, 9 * KC, 2], F32)
    csT = gsb("csT", [128, KC, 2], BF16)
    badaT = gsb("badaT", [128, 9 * KC], F32)
    cT_f = gsb("cT_f", [128, KC, 2], F32)
    arena_base = [arena.off]
    s_misc = P.newsem("misc", persistent=True)
    t_ident = P.dma("sp", lambda e: e.dma_start(out=ident[:], in_=ident_in), s_misc)
    s_c0 = P.newsem("c0", persistent=True)
    t_c0 = P.dma_group("sp", [lambda e: e.dma_start(out=cT_f[:], in_=cT_in),
                              lambda e: e.dma_start(out=badaT[:], in_=badaT_in)], s_c0)
    t_cs = P.op("act", lambda e: e.activation(out=csT[:], in_=cT_f[:], func=AF.Silu), deps=[t_c0])

    bank_free = [None] * 8
    state = {"all": []}

    arena_peak = [0]

    def phase_barrier():
        arena_peak[0] = max(arena_peak[0], arena.off)
        if _os.environ.get("KVERBOSE"):
            print("phase end: arena off", arena.off, "ops", {k: len(v) for k, v in P.ops.items()})
        toks = [(P.prog[e], P.prog[e].n) for e in ("pe", "act", "dve", "pool") if P.prog[e].n > 0]
        best = {}
        for (sem, val) in state["all"]:
            if sem.name not in best or best[sem.name][1] < val:
                best[sem.name] = (sem, val)
        toks += list(best.values())
        for e in Prog.ENG:
            P.wait(e, toks)
        state["all"] = []
        P.release_phase_sems()

    def rsqrt_chain(out_ap, in_ap, scale, deps):
        t1 = P.op("dve", lambda e: e.tensor_scalar(out=out_ap, in0=in_ap, scalar1=float(scale), scalar2=EPS,
                                                   op0=ALU.mult, op1=ALU.add), deps=deps)
        t2 = P.op("act", lambda e: e.activation(out=out_ap, in_=out_ap, func=AF.Sqrt), deps=[t1])
        t3 = P.op("dve", lambda e: e.reciprocal(out=out_ap, in_=out_ap), deps=[t2])
        return t3

    def track(tok):
        state["all"].append(tok)
        return tok

    class AdaStream:
        def __init__(self):
            self.grp_tok = {}

        def chunk(self, ring, grp, c):
            bank = 4 + (grp % 2)
            cc = grp * KC + c
            slot, t_w = ring.load(wada[cc])
            tk = None
            for k in range(KC):
                last = k == KC - 1
                tk = P.op("pe", lambda e, bank=bank, c=c, slot=slot, k=k, last=last: e.matmul(
                    ps[:, bank, 2 * c:2 * c + 2], lhsT=ring.buf[:, slot, k * 128:(k + 1) * 128],
                    rhs=csT[:, k, :], start=(k == 0), stop=last),
                    deps=[t_w, t_cs, bank_free[bank] if c == 0 else None] if k == 0 else None, sig=last)
            ring.free[slot] = tk
            if c == KC - 1:
                psv = ps[:, bank, 0:2 * KC].rearrange("p (c v) -> p c v", v=2)
                t_mod = None
                for v in range(2):
                    t_mod = P.op("dve", lambda e, grp=grp, v=v, psv=psv: e.tensor_tensor(
                        out=modT[:, grp * KC:(grp + 1) * KC, v], in0=psv[:, :, v],
                        in1=badaT[:, grp * KC:(grp + 1) * KC], op=ALU.add), deps=[tk, t_cs])
                bank_free[bank] = t_mod
                if grp % 3 == 1:
                    g0 = grp * KC
                    t_mod = P.op("dve", lambda e, g0=g0: e.tensor_scalar(
                        out=modT[:, g0:g0 + KC, :], in0=modT[:, g0:g0 + KC, :], scalar1=1.0, scalar2=None,
                        op0=ALU.add), deps=[t_mod])
                self.grp_tok[grp] = t_mod

    ada = AdaStream()

    def ada_plan(groups, ntiles):
        per = [[] for _ in range(ntiles)]
        for i, g in enumerate(groups):
            per[min(ntiles - 1, i * ntiles // max(1, len(groups)))] += [(g, c) for c in range(KC)]
        return per

    def phase0():
        arena.off = arena_base[0]
        ring = GRing(None, "adaring", 4)
        for grp in (0, 1):
            for c in range(KC):
                ada.chunk(ring, grp, c)
        phase_barrier()

    def make_gates(sb, gis):
        gates = [(0, 0, 0.5), (0, 1, 0.5), (1, 0, 1.0), (2, 0, 0.5)]
        ones_f = sb("ones_f", [128, 128], F32)
        dm = sb("dm", [128, 8, 128], F32)
        gs = sb("gs", [128, D], F32)
        s_st = P.newsem("gst")
        t_ones = P.op("dve", lambda e: e.memset(ones_f[:], 1.0))
        dm_free = [None] * 8
        gs_free = None
        di = 0
        toks = []
        for gi in gis:
            s_, v, fac = gates[gi]
            t_ev = None
            for c in range(KC):
                q = c % 4
                bank = 2 + ((c // 4) % 2)
                col = (s_ * 3 + 2) * KC + c
                dmi = di % 8
                di += 1
                t_dm = P.op("dve", lambda e, dmi=dmi, col=col, v=v: e.tensor_scalar(
                    out=dm[:, dmi, :], in0=ident[:], scalar1=modT[:, col, v:v + 1], scalar2=None,
                    op0=ALU.mult), deps=[t_ident, dm_free[dmi]])
                t_mm = P.op("pe", lambda e, bank=bank, q=q, dmi=dmi: e.matmul(
                    ps[:, bank, q * 128:(q + 1) * 128], lhsT=ones_f[:], rhs=dm[:, dmi, :],
                    start=True, stop=True), deps=[t_dm, t_ones, bank_free[bank] if q == 0 else None])
                dm_free[dmi] = t_mm
                if q == 3 or c == KC - 1:
                    c0 = c - q
                    t_ev = P.op("act", lambda e, bank=bank, c0=c0, c=c, q=q, fac=fac: e.activation(
                        out=gs[:, c0 * 128:(c + 1) * 128], in_=ps[:, bank, 0:(q + 1) * 128],
                        func=AF.Identity, scale=fac), deps=[t_mm, gs_free])
                    bank_free[bank] = t_ev
            gs_free = track(P.dma("sp", lambda e, gi=gi: e.dma_start(out=gbc[gi], in_=gs[:]), s_st, deps=[t_ev]))
            toks.append(gs_free)
        return toks

    def lpass(name, nsub, src_y, src_res, gate_of, ln_idx, x_dst, hT_dst, mod_s, v_of, gate_make=()):
        with ExitStack() as ph:
            arena.off = arena_base[0]

            def sb(nm, shape, dt):
                return arena.alloc(list(shape), dt)
            NB = 3
            do_ln = src_y is not None
            t_gates = make_gates(sb, list(gate_make)) if gate_make else []
            rb = sb("rb", [128, NB, D], F32)
            s_ld = P.newsem(name + "ld")
            s_rb = [P.newsem(name + "rb") for _ in range(NB)]
            s_yb = [P.newsem(name + "yb") for _ in range(NB)]
            s_xs = [P.newsem(name + "xs") for _ in range(NB)]
            s_hs = [P.newsem(name + "hs") for _ in range(2)]
            NCH = D // 512
            if do_ln:
                yb = sb("yb", [128, NB, D], F32)
                gidx = sorted(set(gate_of(st) for st in range(nsub)))
                gt = sb("gt", [128, len(gidx), D], F32)
                gn = sb("gn", [128, D], F32)
                bs = sb("bs", [128, D], F32)
                stats = sb("stats", [128, NB, NCH * 6], F32)
                mv = sb("mv", [128, NB, 2], F32)
                rstd = sb("rstd", [128, NB, 1], F32)
                nmr = sb("nmr", [128, NB, 1], F32)
                fns = [(lambda e, i=i, g=g: e.dma_start(out=gt[:, i, :], in_=gbc[g])) for i, g in enumerate(gidx)]
                fns.append(lambda e: e.dma_start(out=gn[:], in_=lng_in[ln_idx]))
                fns.append(lambda e: e.dma_start(out=bs[:], in_=lnb_in[ln_idx]))
                t_consts = [P.dma_group("sp", fns, s_ld, deps=t_gates)]
            if hT_dst is not None:
                ho = sb("ho", [128, 2, KC, 128], BF16)
            rb_free = [None] * NB
            yb_free = [None] * NB
            ho_free = [None] * 2
            loads = {}

            def issue_load(st):
                b = st % NB
                t1 = P.dma("sp", lambda e, st=st, b=b: e.dma_start(out=rb[:, b, :], in_=src_res[st * 128:(st + 1) * 128, :]),
                           s_rb[b], deps=rb_free[b])
                t2 = None
                if do_ln:
                    t2 = P.dma("sp", lambda e, st=st, b=b: e.dma_start(out=yb[:, b, :], in_=src_y[st * 128:(st + 1) * 128, :]),
                               s_yb[b], deps=yb_free[b])
                loads[st] = (t1, t2)

            issue_load(0)
            if nsub > 1:
                issue_load(1)
            for st in range(nsub):
                b = st % NB
                if st + 2 < nsub:
                    issue_load(st + 2)
                t1, t2 = loads[st]
                t_x = t1
                if do_ln:
                    gi = gidx.index(gate_of(st))
                    ta = P.op("dve", lambda e, b=b, gi=gi: e.tensor_tensor(
                        out=yb[:, b, :], in0=yb[:, b, :], in1=gt[:, gi, :], op=ALU.mult), deps=[t2] + t_consts)
                    tb = P.op("dve", lambda e, b=b: e.scalar_tensor_tensor(
                        out=rb[:, b, :], in0=rb[:, b, :], scalar=ALPHA, in1=yb[:, b, :], op0=ALU.mult, op1=ALU.add),
                        deps=[ta, t1])
                    yb_free[b] = [tb]
                    tc = None
                    for ch in range(NCH):
                        tc = P.op("dve", lambda e, b=b, ch=ch: e.bn_stats(
                            stats[:, b, ch * 6:(ch + 1) * 6], rb[:, b, ch * 512:(ch + 1) * 512]), deps=[tb])
                    td = P.op("dve", lambda e, b=b: e.bn_aggr(mv[:, b, :], stats[:, b, :]), deps=[tc])
                    te = rsqrt_chain(rstd[:, b, :], mv[:, b, 1:2], 1.0, [td])
                    tf0 = P.op("dve", lambda e, b=b: e.tensor_scalar(
                        out=nmr[:, b, :], in0=mv[:, b, 0:1], scalar1=rstd[:, b, 0:1], scalar2=-1.0,
                        op0=ALU.mult, op1=ALU.mult), deps=[te])
                    tf = P.op("act", lambda e, b=b: e.activation(
                        out=rb[:, b, :], in_=rb[:, b, :], func=AF.Identity, bias=nmr[:, b, 0:1], scale=rstd[:, b, 0:1]),
                        deps=[tf0])
                    tg = P.op("dve", lambda e, b=b: e.tensor_tensor(
                        out=rb[:, b, :], in0=rb[:, b, :], in1=gn[:], op=ALU.mult), deps=[tf] + t_consts)
                    t_x = P.op("dve", lambda e, b=b: e.tensor_tensor(
                        out=rb[:, b, :], in0=rb[:, b, :], in1=bs[:], op=ALU.add), deps=[tg])
                frees = []
                if x_dst is not None:
                    frees.append(track(P.dma("sp", lambda e, st=st, b=b: e.dma_start(
                        out=x_dst[st * 128:(st + 1) * 128, :], in_=rb[:, b, :]), s_xs[b], deps=[t_x])))
                if hT_dst is not None:
                    v = v_of(st)
                    bh = st % 2
                    t_tp = None
                    t_ev = None
                    for c in range(KC):
                        q = c % 4
                        bank = (c // 4) % 8
                        t_tp = P.op("pe", lambda e, b=b, c=c, q=q, bank=bank: e.transpose(
                            out=ps[:, bank, q * 128:(q + 1) * 128], in_=rb[:, b, c * 128:(c + 1) * 128],
                            identity=ident[:]), deps=[t_x, t_ident, bank_free[bank] if q == 0 else None],
                            sig=(q == 3))
                        if q == 3:
                            for qq in range(4):
                                cc = c - 3 + qq
                                t_ev = P.op("act", lambda e, bh=bh, cc=cc, qq=qq, bank=bank, v=v: e.activation(
                                    out=ho[:, bh, cc, :], in_=ps[:, bank, qq * 128:(qq + 1) * 128], func=AF.Identity,
                                    bias=modT[:, (mod_s * 3 + 0) * KC + cc, v:v + 1],
                                    scale=modT[:, (mod_s * 3 + 1) * KC + cc, v:v + 1]),
                                    deps=[t_tp] + (ho_free[bh] or []))
                            bank_free[bank] = t_ev
                    frees.append(t_tp)
                    t_hs = track(P.dma("sp", lambda e, st=st, bh=bh: e.dma_start(
                        out=hT_dst[:, :, st * 128:(st + 1) * 128].rearrange("c p t -> p c t"), in_=ho[:, bh, :, :]),
                        s_hs[bh], deps=[t_ev]))
                    ho_free[bh] = [t_hs]
                rb_free[b] = frees
            phase_barrier()

    class GRing:
        def __init__(self, ph, name, nslots):
            self.n = nslots
            self.buf = arena.alloc([128, nslots, KC * 128], BF16)
            self.free = [None] * nslots
            self.i = 0
            self.sem = [P.newsem(name, kind="sw") for _ in range(nslots)]

        def load(self, src):
            slot = self.i % self.n
            self.i += 1
            tok = P.dma("pool", lambda e, slot=slot, src=src: e.dma_start(out=self.buf[:, slot, :], in_=src),
                        self.sem[slot], deps=[self.free[slot]])
            return slot, tok

    def gemm_g(ring, slot, t_w, hT, tw, bank, extra_deps):
        tk = None
        for k in range(KC):
            last = k == KC - 1
            tk = P.op("pe", lambda e, slot=slot, k=k, bank=bank, last=last: e.matmul(
                ps[:, bank, 0:tw], lhsT=ring.buf[:, slot, k * 128:(k + 1) * 128], rhs=hT[:, k, 0:tw],
                start=(k == 0), stop=last), deps=([t_w, bank_free[bank]] + extra_deps) if k == 0 else None, sig=last)
        ring.free[slot] = tk
        return tk

    def gemm_d(ph, name, aT, nchunks, wsrc, tw, y_dst_rows, dring, ybuf, yb_state, s_y, a_ready):
        S = tw // 128
        GC = 2
        for dp in range(NDP):
            tk = None
            for c0 in range(0, nchunks, GC):
                gc = min(GC, nchunks - c0)
                slot = dring["i"] % dring["n"]
                dring["i"] += 1
                t_w = P.dma("pool", lambda e, slot=slot, dp=dp, c0=c0, gc=gc: e.dma_start(
                    out=dring["buf"][:, slot, 0:gc, :], in_=wsrc[dp, c0:c0 + gc].rearrange("g p n -> p g n")),
                    dring["sem"][slot], deps=[dring["free"][slot]])
                for cl in range(gc):
                    c = c0 + cl
                    for s in range(S):
                        for hf in range(2):
                            bank = s * 2 + hf
                            first = c == 0
                            last = c == nchunks - 1
                            deps = None
                            if cl == 0 and s == 0 and hf == 0:
                                deps = [t_w] + a_ready
                            if first:
                                deps = (deps or []) + [bank_free[bank]]
                            sig = (cl == gc - 1 and s == S - 1 and hf == 1)
                            tk = P.op("pe", lambda e, bank=bank, c=c, s=s, slot=slot, cl=cl, hf=hf, first=first, last=last: e.matmul(
                                ps[:, bank, :], lhsT=aT[:, c, s * 128:(s + 1) * 128],
                                rhs=dring["buf"][:, slot, cl, hf * 512:(hf + 1) * 512], start=first, stop=last),
                                deps=deps, sig=sig)
                dring["free"][slot] = tk
            for s in range(S):
                yi = yb_state["i"] % len(yb_state["free"])
                yb_state["i"] += 1
                eng = "act" if s % 2 == 0 else "dve"
                if eng == "act":
                    t_ev = P.op("act", lambda e, yi=yi, s=s: e.activation(
                        out=ybuf[:, yi, :], in_=ps[:, 2 * s:2 * s + 2, :].rearrange("p a b -> p (a b)"), func=AF.Copy),
                        deps=[tk, yb_state["free"][yi]])
                else:
                    t_ev = P.op("dve", lambda e, yi=yi, s=s: e.tensor_copy(
                        out=ybuf[:, yi, :], in_=ps[:, 2 * s:2 * s + 2, :].rearrange("p a b -> p (a b)")),
                        deps=[tk, yb_state["free"][yi]])
                bank_free[2 * s] = t_ev
                bank_free[2 * s + 1] = t_ev
                r0 = y_dst_rows + s * 128
                yb_state["free"][yi] = track(P.dma("sp", lambda e, yi=yi, r0=r0, dp=dp: e.dma_start(
                    out=Ys[r0:r0 + 128, dp * DPW:(dp + 1) * DPW], in_=ybuf[:, yi, :]), s_y[yi], deps=[t_ev]))

    def make_dring(ph, name, n=4):
        return {"buf": arena.alloc([128, n, 2, DPW], BF16), "n": n, "i": 0,
                "free": [None] * n, "sem": [P.newsem(name, kind="sw") for _ in range(n)]}

    def ffn(name, tiles, hT_src, wg_, wu_, wd_, ada_groups=()):
        with ExitStack() as ph:
            arena.off = arena_base[0]

            def sb(nm, shape, dt):
                return arena.alloc(list(shape), dt)
            hT = sb("hT", [128, KC, T], BF16)
            actT = sb("actT", [128, FC, T], BF16)
            ring = GRing(ph, name + "ring", 5)
            dring = make_dring(ph, name + "dring")
            sgt = sb("sgt", [128, 2, T], F32)
            ybuf = sb("ybuf", [128, 4, DPW], F32)
            yb_state = {"i": 0, "free": [None] * 4}
            s_h = P.newsem(name + "h")
            s_y = [P.newsem(name + "y") for _ in range(4)]
            hT_free = None
            sgt_free = [None, None]
            t_h = None

            def load_h(t0, tw):
                return P.dma("sp", lambda e, t0=t0, tw=tw: e.dma_start(
                    out=hT[:, :, 0:tw], in_=hT_src[:, :, t0:t0 + tw].rearrange("c p t -> p c t")), s_h,
                    deps=[hT_free])

            t_h = load_h(*tiles[0])
            plan = ada_plan(list(ada_groups), len(tiles))
            for ti, (t0, tw) in enumerate(tiles):
                ev_toks = []
                tk = None
                pend = list(plan[ti])
                for j in range(FC):
                    pb = j % 2
                    npump = -(-len(pend) // (FC - j)) if j % 2 == 0 or len(pend) >= (FC - j) else 0
                    for _ in range(npump):
                        ada.chunk(ring, *pend.pop(0))
                    slot_g, tw_g = ring.load(wg_[j])
                    slot_u, tw_u = ring.load(wu_[j])
                    tkg = gemm_g(ring, slot_g, tw_g, hT, tw, 2 * pb, [t_h])
                    tk = gemm_g(ring, slot_u, tw_u, hT, tw, 2 * pb + 1, [t_h])
                    t_s = P.op("act", lambda e, pb=pb, tw=tw: e.activation(
                        out=sgt[:, pb, 0:tw], in_=ps[:, 2 * pb, 0:tw], func=AF.Silu), deps=[tkg, sgt_free[pb]])
                    t_m = P.op("dve", lambda e, pb=pb, j=j, tw=tw: e.tensor_tensor(
                        out=actT[:, j, 0:tw], in0=sgt[:, pb, 0:tw], in1=ps[:, 2 * pb + 1, 0:tw], op=ALU.mult),
                        deps=[t_s, tk])
                    sgt_free[pb] = t_m
                    bank_free[2 * pb] = t_m
                    bank_free[2 * pb + 1] = t_m
                    ev_toks = [t_m] if j == FC - 1 else ev_toks
                hT_free = tk
                if ti + 1 < len(tiles):
                    t_h = load_h(*tiles[ti + 1])
                gemm_d(ph, name, actT, FC, wd_, tw, t0, dring, ybuf, yb_state, s_y, ev_toks + [bank_free[2], bank_free[0]])
            phase_barrier()

    def mixer():
        with ExitStack() as ph:
            arena.off = arena_base[0]

            def sb(nm, shape, dt):
                return arena.alloc(list(shape), dt)
            KT = sb("KT", [128, NKV, NTA], BF16)
            V = sb("V", [128, NSA, NKV * 128], BF16)
            hT = sb("hT", [128, KC, T], BF16)
            uT = sb("uT", [128, NG, T], BF16)
            vn = sb("vn", [128, 4, NG * 128], BF16)
            qT = sb("qT", [128, NQ, T], BF16)
            ring = GRing(ph, "mxring", 3)
            dring = make_dring(ph, "mxdring", 3)
            ybuf = sb("ybuf", [128, 2, DPW], F32)
            yb_state = {"i": 0, "free": [None] * 2}
            tmp = sb("tmp", [128, 4, 512], F32)
            PT = sb("PT", [128, 3, 512], BF16)
            sq = sb("sq", [128, 2, 512], BF16)
            cosb = sb("cosb", [128, 512], F32)
            sinb = sb("sinb", [128, 512], F32)
            masks = sb("masks", [128, 4, 512], BF16)
            bsp = sb("bsp", [128, NG * 128], F32)
            wsT = sb("wsT", [128, NG * 128], BF16)
            esink = sb("esink", [128, NQ], F32)
            gmixT = sb("gmixT", [128, NG + NQ], F32)
            ones_b = sb("ones_b", [128, 128], BF16)
            lsum = sb("lsum", [128, 512], F32)
            ssum = sb("ssum", [128, 2, 128], F32)
            rs = sb("rs", [128, 2, 128], F32)
            NVC = max(1, NG * 128 // 512)
            vst = sb("vst", [128, 4, NVC * 6], F32)
            vmv = sb("vmv", [128, 4, 2], F32)
            vrs = sb("vrs", [128, 4, 1], F32)
            s_c = P.newsem("mxc")
            s_h = P.newsem("mxh")
            s_y = [P.newsem("mxy") for _ in range(4)]
            s_t = P.newsem("mxt")
            fns = [(lambda e, k_=k_: e.dma_start(out=masks[:, k_, :], in_=masks_in[k_])) for k_ in range(4)]
            fns.append(lambda e: e.dma_start(out=bsp[:], in_=bsp_in))
            fns.append(lambda e: e.dma_start(out=esink[:], in_=sink_in))
            fns.append(lambda e: e.dma_start(out=gmixT[:], in_=gmixT_in))
            tc_ = [P.dma_group("sp", fns, s_c)]
            t_k1 = P.op("act", lambda e: e.activation(out=esink[:], in_=esink[:], func=AF.Exp), deps=tc_)
            s_ws = P.newsem("mxws", kind="sw")
            t_k2 = P.dma("pool", lambda e: e.dma_start(out=wsT[:], in_=wsT_in), s_ws)
            t_k3 = P.op("dve", lambda e: e.memset(ones_b[:], 1.0))
            consts = tc_ + [t_k1, t_k2, t_k3]
            chk(10)
            SC = 1.0 / float(np.sqrt(128.0))
            tmp_free = [None] * 4
            tmp_i = [0]

            def tmp_slot():
                i = tmp_i[0] % 4
                tmp_i[0] += 1
                return i

            hT_free = [None]

            def load_h(t0, tw):
                return P.dma("sp", lambda e, t0=t0, tw=tw: e.dma_start(
                    out=hT[:, :, 0:tw], in_=hTs[1][:, :, t0:t0 + tw].rearrange("c p t -> p c t")), s_h,
                    deps=hT_free[0])

            rope_free = [None]

            def load_rope(t0, tw):
                d = rope_free[0]
                a = P.dma_group("sp", [lambda e, t0=t0, tw=tw: e.dma_start(out=cosb[:, 0:tw], in_=cos_in[:, t0:t0 + tw]),
                                       lambda e, t0=t0, tw=tw: e.dma_start(out=sinb[:, 0:tw], in_=sin_in[:, t0:t0 + tw])],
                                s_t, deps=d)
                return [a]

            MXVAR = int(_os.environ.get("MXVAR", "0"))

            def rope_evac(bank, tw_r, dst, t_mm, t_rope):
                if MXVAR == 1:
                    t = P.op("act", lambda e: e.activation(out=dst, in_=ps[:, bank, 0:tw_r], func=AF.Copy), deps=[t_mm])
                    return t, [t]
                if MXVAR == 2:
                    i0 = tmp_slot()
                    ta = P.op("act", lambda e: e.activation(out=tmp[0:64, i0, 0:tw_r], in_=ps[64:128, bank, 0:tw_r], func=AF.Copy),
                              deps=[t_mm, tmp_free[i0]])
                    tb = P.op("act", lambda e: e.activation(out=tmp[64:128, i0, 0:tw_r], in_=ps[0:64, bank, 0:tw_r], func=AF.Copy),
                              deps=[t_mm])
                    t = P.op("act", lambda e: e.activation(out=dst, in_=tmp[:, i0, 0:tw_r], func=AF.Copy), deps=[ta, tb])
                    tmp_free[i0] = t
                    return t, [t]
                if MXVAR == 3:
                    i0 = tmp_slot()
                    ta = P.op("dve", lambda e: e.tensor_tensor(out=tmp[:, i0, 0:tw_r], in0=ps[:, bank, 0:tw_r], in1=cosb[:, 0:tw_r],
                                                             op=ALU.mult), deps=[t_mm, tmp_free[i0]] + t_rope)
                    t = P.op("pool", lambda e: e.tensor_tensor(out=tmp[:, i0, 0:tw_r], in0=tmp[:, i0, 0:tw_r], in1=sinb[:, 0:tw_r],
                                                            op=ALU.mult), deps=[ta] + t_rope)
                    t2 = P.op("dve", lambda e: e.tensor_copy(out=dst, in_=tmp[:, i0, 0:tw_r]), deps=[t])
                    tmp_free[i0] = t2
                    return t2, [ta]
                i0 = tmp_slot()
                i1 = tmp_slot()
                ta = P.op("act", lambda e: e.activation(out=tmp[0:64, i0, 0:tw_r], in_=ps[64:128, bank, 0:tw_r], func=AF.Copy),
                          deps=[t_mm, tmp_free[i0]])
                tb = P.op("act", lambda e: e.activation(out=tmp[64:128, i0, 0:tw_r], in_=ps[0:64, bank, 0:tw_r], func=AF.Copy),
                          deps=[t_mm])
                tc2 = P.op("dve", lambda e: e.tensor_tensor(out=tmp[:, i1, 0:tw_r], in0=ps[:, bank, 0:tw_r], in1=cosb[:, 0:tw_r],
                                                             op=ALU.mult), deps=[t_mm, tb, tmp_free[i1]] + t_rope)
                td = P.op("dve", lambda e: e.tensor_tensor(out=tmp[:, i0, 0:tw_r], in0=tmp[:, i0, 0:tw_r], in1=sinb[:, 0:tw_r],
                                                            op=ALU.mult), deps=[ta, tb] + t_rope + ([tc2] if MXVAR == 5 else []))
                if MXVAR == 6:
                    te0 = P.op("dve", lambda e: e.tensor_tensor(out=tmp[:, i1, 0:tw_r], in0=tmp[:, i1, 0:tw_r], in1=tmp[:, i0, 0:tw_r], op=ALU.add),
                               deps=[tc2, td])
                    te = P.op("dve", lambda e: e.tensor_copy(out=dst, in_=tmp[:, i1, 0:tw_r]), deps=[te0])
                elif MXVAR == 7:
                    te = P.op("dve", lambda e: e.tensor_copy(out=dst, in_=tmp[:, i1, 0:tw_r]), deps=[tc2, td])
                else:
                    te = P.op("dve", lambda e: e.tensor_tensor(out=dst, in0=tmp[:, i1, 0:tw_r], in1=tmp[:, i0, 0:tw_r], op=ALU.add),
                              deps=[tc2, td])
                tmp_free[i0] = te
                tmp_free[i1] = te
                return te, [tb, tc2]

            def transpose_to(src_f32, nsub_, dsts, t_src, bank):
                t_tp = None
                for s in range(nsub_):
                    t_tp = P.op("pe", lambda e, s=s: e.transpose(out=ps[:, bank, s * 128:(s + 1) * 128],
                                                                 in_=src_f32[:, s * 128:(s + 1) * 128], identity=ident[:]),
                                deps=[t_src, t_ident, bank_free[bank] if s == 0 else None], sig=(s == nsub_ - 1))
                t_ev = None
                for s in range(nsub_):
                    t_ev = P.op("dve", lambda e, s=s: e.tensor_copy(out=dsts[s], in_=ps[:, bank, s * 128:(s + 1) * 128]),
                                deps=[t_tp])
                bank_free[bank] = t_ev
                return t_tp, t_ev

            tiles_all = [(t0, min(T, NTA - t0)) for t0 in range(0, NTA, T)]
            t_h = load_h(*tiles_all[0])
            kv_ready = []
            for ti, (t0, tw) in enumerate(tiles_all):
                tw_r = max(0, min(tw, NR - t0))
                t_rope = load_rope(t0, tw_r) if tw_r > 0 else []
                tk = None
                readers = []
                for jj in range(2 * NKV):
                    j = 2 * NG + NQ + jj
                    bank = jj % 4
                    slot, t_w = ring.load(win[j])
                    tk = gemm_g(ring, slot, t_w, hT, tw, bank, [t_h])
                    if pend_tp[0] is not None:
                        pend_tp[0]()
                        pend_tp[0] = None
                    if jj == 1:
                        chk(11)
                    if jj < NKV:
                        toks = []
                        if tw_r < tw:
                            tcp = P.op("act", lambda e, jj=jj, t0=t0, tw=tw, tw_r=tw_r, bank=bank: e.activation(
                                out=KT[:, jj, t0 + tw_r:t0 + tw], in_=ps[:, bank, tw_r:tw], func=AF.Copy), deps=[tk])
                            toks.append(tcp)
                        if tw_r > 0:
                            te, rd = rope_evac(bank, tw_r, KT[:, jj, t0:t0 + tw_r], tk, t_rope)
                            toks += [te] + rd
                            readers += rd
                        bank_free[bank] = toks
                        kv_ready.append(toks)
                    else:
                        h = jj - NKV
                        i0 = tmp_slot()
                        t_c = P.op("act", lambda e, i0=i0, bank=bank, tw=tw: e.activation(
                            out=tmp[:, i0, 0:tw], in_=ps[:, bank, 0:tw], func=AF.Copy), deps=[tk, tmp_free[i0]])
                        bank_free[bank] = t_c
                        nsub_ = tw // 128
                        dsts = [V[:, t0 // 128 + s, h * 128:(h + 1) * 128] for s in range(nsub_)]

                        def do_tp0(i0=i0, dsts=dsts, t_c=t_c, jj=jj, nsub_=nsub_):
                            t_tp, t_ev = transpose_to(tmp[:, i0, :], nsub_, dsts, t_c, 6 + (jj % 2))
                            tmp_free[i0] = t_tp
                            kv_ready.append(t_ev)
                        pend_tp[0] = do_tp0
                if pend_tp[0] is not None:
                    pend_tp[0]()
                    pend_tp[0] = None
                chk(12)
                rope_free[0] = readers + [bank_free[b_] for b_ in range(4)]
                hT_free[0] = [tk]
                if ti + 1 < len(tiles_all):
                    t_h = load_h(*tiles_all[ti + 1])

            chk(0)
            tiles_own = [(t0, min(T, OWN - t0)) for t0 in range(0, OWN, T)]
            t_h = load_h(*tiles_own[0])
            yT = hT
            plan_mx = ada_plan([7], len(tiles_own))
            for ti, (t0, tw) in enumerate(tiles_own):
                S = tw // 128
                t_rope = load_rope(t0, tw)
                pend = list(plan_mx[ti])
                readers = []
                vn_ready = []
                u_ready = []
                q_ready = []
                tk = None
                NJ = 2 * NG + NQ
                for j in range(NJ):
                    bank = j % 4
                    for _ in range(-(-len(pend) // (NJ - j))):
                        ada.chunk(ring, *pend.pop(0))
                    slot, t_w = ring.load(win[j])
                    tk = gemm_g(ring, slot, t_w, hT, tw, bank, [t_h])
                    if pend_tp[0] is not None:
                        pend_tp[0]()
                        pend_tp[0] = None
                    if j < NG:
                        t_e = P.op("act", lambda e, j=j, bank=bank, tw=tw: e.activation(
                            out=uT[:, j, 0:tw], in_=ps[:, bank, 0:tw], func=AF.Gelu), deps=[tk])
                        bank_free[bank] = t_e
                        u_ready.append(t_e)
                    elif j < 2 * NG:
                        g = j - NG
                        i0 = tmp_slot()
                        t_c = P.op("act", lambda e, i0=i0, bank=bank, tw=tw: e.activation(
                            out=tmp[:, i0, 0:tw], in_=ps[:, bank, 0:tw], func=AF.Gelu), deps=[tk, tmp_free[i0]])
                        bank_free[bank] = t_c
                        dsts = [vn[:, s, g * 128:(g + 1) * 128] for s in range(S)]

                        def do_tp(i0=i0, dsts=dsts, t_c=t_c, j=j, S=S):
                            t_tp, t_ev = transpose_to(tmp[:, i0, :], S, dsts, t_c, 6 + (j % 2))
                            tmp_free[i0] = t_tp
                            vn_ready.append(t_ev)
                        pend_tp[0] = do_tp
                    else:
                        hq = j - 2 * NG
                        te, rd = rope_evac(bank, tw, qT[:, hq, 0:tw], tk, t_rope)
                        readers += rd
                        bank_free[bank] = [te] + rd
                        q_ready.append(te)
                if pend_tp[0] is not None:
                    pend_tp[0]()
                    pend_tp[0] = None
                rope_free[0] = readers + [bank_free[b_] for b_ in range(4)]
                chk(1)
                for s in range(S):
                    t1 = None
                    VW = NG * 128 // NVC
                    for ch in range(NVC):
                        t1 = P.op("dve", lambda e, s=s, ch=ch: e.bn_stats(
                            vst[:, s, ch * 6:(ch + 1) * 6], vn[:, s, ch * VW:(ch + 1) * VW]), deps=vn_ready)
                    t2 = P.op("dve", lambda e, s=s: e.bn_aggr(vmv[:, s, :], vst[:, s, :]), deps=[t1])
                    t3 = rsqrt_chain(vrs[:, s, :], vmv[:, s, 1:2], 1.0, [t2])
                    t4 = P.op("dve", lambda e, s=s: e.tensor_scalar(
                        out=vn[:, s, :], in0=vn[:, s, :], scalar1=vmv[:, s, 0:1], scalar2=vrs[:, s, 0:1],
                        op0=ALU.subtract, op1=ALU.mult), deps=[t3])
                    vn_ready.append(t4)
                chk(2)
                y_wr = [tk]
                SSB = 6
                n4 = (NG + 3) // 4
                msteps = [(s_, g4) for s_ in range(S) for g4 in range(n4)]
                ypart = {}

                def mlp_final(s_, t_ssm):
                    wl = min(4, NG)
                    tr = P.op("dve", lambda e, wl=wl: e.tensor_reduce(
                        out=ssum[:, 0, :], in_=ps[:, SSB, 0:wl * 128].rearrange("p (a b) -> p b a", b=128),
                        axis=mybir.AxisListType.X, op=ALU.add), deps=[t_ssm, ss_free[0]])
                    bank_free[SSB] = tr
                    tr3 = rsqrt_chain(rs[:, 0, :], ssum[:, 0, :], 1.0 / (NG * 128), [tr, rs_free[0]])
                    ss_free[0] = tr3
                    tl = None
                    for g in range(NG):
                        tl = P.op("dve", lambda e, g=g, s_=s_: e.scalar_tensor_tensor(
                            out=yT[:, g, s_ * 128:(s_ + 1) * 128], in0=yT[:, g, s_ * 128:(s_ + 1) * 128],
                            scalar=gmixT[:, g:g + 1], in1=rs[:, 0, :], op0=ALU.mult, op1=ALU.mult),
                            deps=[tr3] + ypart[("m", s_)] + consts)
                    rs_free[0] = tl
                    y_done.append(tl)

                def mlp_mixed(i):
                    s_, g4 = msteps[i]
                    g0 = g4 * 4
                    ng_ = min(4, NG - g0)
                    bank = 2 + (i % 2)
                    t_mm = None
                    for gg in range(ng_):
                        g = g0 + gg
                        t_mm = P.op("pe", lambda e, g=g, gg=gg, s_=s_, bank=bank: e.matmul(
                            ps[:, bank, gg * 128:(gg + 1) * 128], lhsT=vn[:, s_, g * 128:(g + 1) * 128],
                            rhs=wsT[:, g * 128:(g + 1) * 128], start=True, stop=True),
                            deps=(vn_ready + consts + [bank_free[bank]]) if gg == 0 else None, sig=(gg == ng_ - 1))
                    ti0 = tmp_slot()
                    w_ = ng_ * 128
                    ta = P.op("dve", lambda e, ti0=ti0, bank=bank, g0=g0, w_=w_: e.tensor_tensor(
                        out=tmp[:, ti0, 0:w_], in0=ps[:, bank, 0:w_], in1=bsp[:, g0 * 128:g0 * 128 + w_], op=ALU.add),
                        deps=[t_mm, tmp_free[ti0]])
                    bank_free[bank] = ta
                    tb = P.op("dve", lambda e, ti0=ti0, g0=g0, ng_=ng_, s_=s_: e.tensor_tensor(
                        out=tmp[:, ti0, 0:ng_ * 128].rearrange("p (a b) -> p a b", b=128),
                        in0=tmp[:, ti0, 0:ng_ * 128].rearrange("p (a b) -> p a b", b=128),
                        in1=uT[:, g0:g0 + ng_, s_ * 128:(s_ + 1) * 128], op=ALU.mult), deps=[ta] + u_ready)
                    sqi = sq_i[0] % 2
                    sq_i[0] += 1
                    tcq = P.op("act", lambda e, ti0=ti0, sqi=sqi, w_=w_: e.activation(
                        out=sq[:, sqi, 0:w_], in_=tmp[:, ti0, 0:w_], func=AF.Square), deps=[tb, sq_free[sqi]])
                    tdq = P.op("pool", lambda e, ti0=ti0, g0=g0, ng_=ng_, s_=s_: e.tensor_copy(
                        out=yT[:, g0:g0 + ng_, s_ * 128:(s_ + 1) * 128],
                        in_=tmp[:, ti0, 0:ng_ * 128].rearrange("p (a b) -> p a b", b=128)), deps=[tb] + y_wr)
                    tmp_free[ti0] = tdq
                    ypart.setdefault(("m", s_), []).append(tdq)
                    return (sqi, w_, tcq)

                def mlp_ss(i, st_):
                    s_, g4 = msteps[i]
                    sqi, w_, tcq = st_
                    t_ssm = P.op("pe", lambda e, sqi=sqi, w_=w_, g4=g4: e.matmul(
                        ps[:, SSB, 0:w_], lhsT=ones_b[:], rhs=sq[:, sqi, 0:w_], start=(g4 == 0), stop=(g4 == n4 - 1)),
                        deps=[tcq, consts[-1], bank_free[SSB] if g4 == 0 else None])
                    sq_free[sqi] = t_ssm
                    if g4 == n4 - 1:
                        mlp_final(s_, t_ssm)

                prev = None
                for i in range(len(msteps)):
                    cur = mlp_mixed(i)
                    if prev is not None:
                        mlp_ss(i - 1, prev)
                    prev = cur
                mlp_ss(len(msteps) - 1, prev)
                chk(3)
                def kbs_of(s_):
                    qb = t0 // 128 + s_
                    kbs = [(NR // 128 + cb, None) for cb in range(cfg.CTX // 128)]
                    kbs.append((OWN // 128, 2) if qb == 0 else (qb - 1, 0))
                    kbs.append((qb, None))
                    kbs.append((OWN // 128, 3) if qb == NSO - 1 else (qb + 1, 1))
                    return kbs

                groups = [(s_, h, kbs_of(s_)) for s_ in range(S) for h in range(NKV)]
                asteps = [(gi, ki) for gi, g_ in enumerate(groups) for ki in range(len(g_[2]))]
                SBK = [0, 1, 6]
                LA = 2
                exp_tok = {}
                pend_ss = [None]

                def att_S(i):
                    gi, ki = asteps[i]
                    s_, h, kbs = groups[gi]
                    kb, mk = kbs[ki]
                    bank = SBK[i % 3]
                    t_s = P.op("pe", lambda e, h=h, kb=kb, s_=s_, bank=bank: e.matmul(
                        ps[:, bank, :].rearrange("p (a b) -> p a b", b=128),
                        lhsT=KT[:, h, kb * 128:(kb + 1) * 128], rhs=qT[:, 4 * h:4 * h + 4, s_ * 128:(s_ + 1) * 128],
                        start=True, stop=True), deps=kv_ready + q_ready + [bank_free[bank]])
                    pi = pt_i[0] % 3
                    pt_i[0] += 1
                    t_e = P.op("act", lambda e, pi=pi, bank=bank: e.activation(
                        out=PT[:, pi, :], in_=ps[:, bank, :], func=AF.Exp, scale=SC), deps=[t_s, pt_free[pi]])
                    bank_free[bank] = t_e
                    if mk is not None:
                        t_e = P.op("pool", lambda e, pi=pi, mk=mk: e.tensor_tensor(
                            out=PT[:, pi, :], in0=PT[:, pi, :], in1=masks[:, mk, :], op=ALU.mult), deps=[t_e] + consts)
                    exp_tok[i] = (t_e, pi)

                def att_ss(gi, sqi, tcq):
                    s_, h, kbs = groups[gi]
                    t_ssa = P.op("pe", lambda e, sqi=sqi, h=h: e.matmul(
                        ps[:, 7, :], lhsT=ones_b[:], rhs=sq[:, sqi, :], start=(h == 0), stop=(h == NKV - 1)),
                        deps=[tcq, bank_free[7] if h == 0 else None])
                    sq_free[sqi] = t_ssa
                    if h == NKV - 1:
                        tr = P.op("dve", lambda e: e.tensor_reduce(
                            out=ssum[:, 1, :], in_=ps[:, 7, :].rearrange("p (a b) -> p b a", b=128),
                            axis=mybir.AxisListType.X, op=ALU.add), deps=[t_ssa, ss_free[1]])
                        bank_free[7] = tr
                        tr3 = rsqrt_chain(rs[:, 1, :], ssum[:, 1, :], 1.0 / (NQ * 128), [tr, rs_free[1]])
                        ss_free[1] = tr3
                        tl = None
                        for hq in range(NQ):
                            tl = P.op("dve", lambda e, hq=hq, s_=s_: e.scalar_tensor_tensor(
                                out=yT[:, NG + hq, s_ * 128:(s_ + 1) * 128], in0=yT[:, NG + hq, s_ * 128:(s_ + 1) * 128],
                                scalar=gmixT[:, NG + hq:NG + hq + 1], in1=rs[:, 1, :], op0=ALU.mult, op1=ALU.mult),
                                deps=[tr3] + ypart[("a", s_)] + consts)
                        rs_free[1] = tl
                        y_done.append(tl)

                def att_PV(i):
                    gi, ki = asteps[i]
                    s_, h, kbs = groups[gi]
                    kb, mk = kbs[ki]
                    OB, LB = (4, 5) if gi % 2 == 0 else (2, 3)
                    t_e, pi = exp_tok.pop(i)
                    first = ki == 0
                    last = ki == len(kbs) - 1
                    P.op("pe", lambda e, pi=pi, kb=kb, h=h, first=first, last=last: e.matmul(
                        ps[:, OB, :], lhsT=V[:, kb, h * 128:(h + 1) * 128], rhs=PT[:, pi, :], start=first, stop=last),
                        deps=[t_e, bank_free[OB] if first else None], sig=False)
                    t_l = P.op("pe", lambda e, pi=pi, first=first, last=last: e.matmul(
                        ps[:, LB, :], lhsT=ones_b[:], rhs=PT[:, pi, :], start=first, stop=last),
                        deps=[bank_free[LB] if first else None])
                    pt_free[pi] = t_l
                    if not last:
                        return
                    tn = None
                    for g in range(4):
                        tn = P.op("dve", lambda e, g=g, h=h, LB=LB: e.tensor_scalar(
                            out=lsum[:, g * 128:(g + 1) * 128], in0=ps[:, LB, g * 128:(g + 1) * 128],
                            scalar1=esink[:, 4 * h + g:4 * h + g + 1], scalar2=None, op0=ALU.add),
                            deps=[t_l, ls_free[0]] + consts)
                    bank_free[LB] = tn
                    tn2 = P.op("dve", lambda e: e.reciprocal(out=lsum[:], in_=lsum[:]), deps=[tn])
                    ti0 = tmp_slot()
                    tn3 = P.op("dve", lambda e, ti0=ti0, OB=OB: e.tensor_tensor(
                        out=tmp[:, ti0, :], in0=ps[:, OB, :], in1=lsum[:], op=ALU.mult), deps=[tn2, tmp_free[ti0]])
                    bank_free[OB] = tn3
                    ls_free[0] = tn3
                    sqi = sq_i[0] % 2
                    sq_i[0] += 1
                    tcq = P.op("act", lambda e, ti0=ti0, sqi=sqi: e.activation(
                        out=sq[:, sqi, :], in_=tmp[:, ti0, :], func=AF.Square), deps=[tn3, sq_free[sqi]])
                    tdq = P.op("pool", lambda e, ti0=ti0, h=h, s_=s_: e.tensor_copy(
                        out=yT[:, NG + 4 * h:NG + 4 * h + 4, s_ * 128:(s_ + 1) * 128],
                        in_=tmp[:, ti0, :].rearrange("p (a b) -> p a b", b=128)), deps=[tn3] + y_wr)
                    tmp_free[ti0] = tdq
                    ypart.setdefault(("a", s_), []).append(tdq)
                    if pend_ss[0] is not None:
                        att_ss(*pend_ss[0])
                    pend_ss[0] = (gi, sqi, tcq)

                na = len(asteps)
                for i in range(min(LA, na)):
                    att_S(i)
                for i in range(na):
                    if i + LA < na:
                        att_S(i + LA)
                    att_PV(i)
                if pend_ss[0] is not None:
                    att_ss(*pend_ss[0])
                    pend_ss[0] = None
                chk(4)
                gemm_d(ph, "mx", yT, KC, wout, tw, t0, dring, ybuf, yb_state, s_y, list(y_done) + [bank_free[b_] for b_ in range(8)])
                del y_done[:]
                hT_free[0] = [(P.prog["pe"], P.prog["pe"].n)]
                if ti + 1 < len(tiles_own):
                    t_h = load_h(*tiles_own[ti + 1])
            phase_barrier()

    def mixer_wrap():
        try:
            mixer()
        except _Stop:
            phase_barrier()

    MX = int(_os.environ.get("MXSTOP", "99"))

    def chk(n):
        if MX == n:
            raise _Stop()

    sq_free = [None, None]
    ss_free = [None, None]
    rs_free = [None, None]
    ls_free = [None]
    pt_free = [None, None, None]
    pt_i = [0]
    pend_tp = [None]
    sq_i = [0]
    y_parts = []
    y_done = []

    tiles_a = [(t0, min(T, NTA - t0)) for t0 in range(0, NTA, T)]
    tiles_c = [(t0, min(T, OWN - t0)) for t0 in range(0, OWN, T)]
    ctx_st0 = (OWN + cfg.HALO) // 128
    v_of_a = lambda st: 1 if st >= ctx_st0 else 0

    import os
    kstop = int(os.environ.get("KSTOP", "99"))
    steps = [
        lambda: phase0(),
        lambda: lpass("p1", NSA, None, xin, None, None, None, hTs[0], 0, v_of_a),
        lambda: ffn("fa", tiles_a, hTs[0], wg[0], wu[0], wd[0], ada_groups=(2, 3, 4, 5, 6)),
        lambda: lpass("p3", NSA, Ys, xin, lambda st: v_of_a(st), 0, xa, hTs[1], 1, v_of_a, gate_make=(0, 1)),
        lambda: mixer_wrap(),
        lambda: lpass("p6", NSO, Ys, xa, lambda st: 2, 1, xmid, hTs[2], 2, lambda st: 0, gate_make=(2,)),
        lambda: ffn("fb", tiles_c, hTs[2], wg[1], wu[1], wd[1], ada_groups=(8,)),
        lambda: lpass("p8", NSO, Ys, xmid, lambda st: 3, 2, out, None, None, None, gate_make=(3,)),
    ]
    for i_, st_ in enumerate(steps):
        if i_ <= kstop:
            st_()

    with nc.Block() as block:
        P.emit(block)
    es.close()
    return nc


def _rope_tables(cfg, pos):
    n_freq = 128 // 4
    inv_freq = (10000.0 ** (-np.arange(n_freq, dtype=np.float32) / n_freq)).astype(np.float32)
    row = (pos // cfg.GRID_W).astype(np.float32)
    col = (pos % cfg.GRID_W).astype(np.float32)
    ang = np.concatenate([row[:, None] * inv_freq, col[:, None] * inv_freq], axis=-1).astype(np.float32)
    cos = np.cos(ang).astype(np.float32).T
    sin = np.sin(ang).astype(np.float32).T
    cos2 = np.concatenate([cos, cos], 0)
    sin2 = np.concatenate([-sin, sin], 0)
    return np.ascontiguousarray(cos2), np.ascontiguousarray(sin2)


def _g_layout(w, KC):
    K, F = w.shape
    return np.ascontiguousarray(w.reshape(KC, 128, F // 128, 128).transpose(2, 1, 0, 3)).reshape(F // 128, 128, KC * 128)


def _d_layout(w, DPW):
    Cn, Dm = w.shape[0] // 128, w.shape[1]
    return np.ascontiguousarray(w.reshape(Cn, 128, Dm // DPW, DPW).transpose(2, 0, 1, 3))


def prepare_inputs(cfg, x, c, ctx, c_ctx, w_ada, b_ada, w_ffn_gate, w_ffn_up, w_ffn_down, w_in, w_spatial,
                   b_spatial, sink_logit, g_mix, w_out, ln_gain, ln_bias):
    D, KC = cfg.D, cfg.KC
    f32 = np.float32
    shared = {}
    shared["wada"] = _g_layout(np.asarray(w_ada[0], f32), KC)
    shared["badaT"] = np.ascontiguousarray(np.asarray(b_ada[0], f32).reshape(9 * KC, 128).T)
    for i in range(2):
        shared["wg%d" % i] = _g_layout(np.asarray(w_ffn_gate[0, i], f32), KC)
        shared["wu%d" % i] = _g_layout(np.asarray(w_ffn_up[0, i], f32), KC)
        shared["wd%d" % i] = _d_layout(np.asarray(w_ffn_down[0, i], f32), cfg.DPW)
    shared["win"] = _g_layout(np.asarray(w_in[0], f32), KC)
    shared["wout"] = _d_layout(np.asarray(w_out[0], f32), cfg.DPW)
    ws = np.asarray(w_spatial[0], f32)
    shared["wsT"] = np.ascontiguousarray(ws.transpose(2, 0, 1)).reshape(128, cfg.NG * 128)
    shared["bsp"] = np.ascontiguousarray(np.broadcast_to(np.asarray(b_spatial[0], f32).reshape(1, -1), (128, cfg.NG * 128)))
    shared["sinkb"] = np.ascontiguousarray(np.broadcast_to(np.asarray(sink_logit[0], f32).reshape(1, -1), (128, cfg.NQ)))
    shared["gmixT"] = np.ascontiguousarray(np.asarray(g_mix[0], f32).reshape(cfg.NG + cfg.NQ, 128).T)
    shared["lng"] = np.ascontiguousarray(np.broadcast_to(np.asarray(ln_gain[0], f32)[:, None, :], (3, 128, D)))
    shared["lnb"] = np.ascontiguousarray(np.broadcast_to(np.asarray(ln_bias[0], f32)[:, None, :], (3, 128, D)))
    shared["ident"] = np.eye(128, dtype=f32)
    kk = np.arange(128)[:, None]
    qq = np.arange(128)[None, :]
    m_prev = np.tile((kk >= qq).astype(f32), (1, 4))
    m_next = np.tile((kk <= qq).astype(f32), (1, 4))
    zero = np.zeros_like(m_prev)
    x = np.asarray(x, f32)
    ctx = np.asarray(ctx, f32)
    c = np.asarray(c, f32)
    c_ctx = np.asarray(c_ctx, f32)
    in_maps = []
    cps = cfg.cores_per_seq
    for core in range(cfg.n_cores):
        b, half = core // cps, core % cps
        o0 = half * cfg.OWN
        m = dict(shared)
        first = half == 0
        lastc = half == cps - 1
        if first:
            h0 = o0 + cfg.OWN
        else:
            h0 = o0 - 128
        m["xin"] = np.ascontiguousarray(np.concatenate([x[b, o0:o0 + cfg.OWN], x[b, h0:h0 + 128], ctx[b]], 0))
        cv = np.stack([c[b], c_ctx], -1)
        m["cT"] = np.ascontiguousarray(cv.reshape(KC, 128, 2).transpose(1, 0, 2))
        pos = np.concatenate([np.arange(o0, o0 + cfg.OWN), np.arange(h0, h0 + 128)])
        m["cos2"], m["sin2"] = _rope_tables(cfg, pos)
        fp = zero if first else m_prev
        ln_ = m_next if first else zero
        m["masks"] = np.stack([m_prev, m_next, fp, ln_], 0).astype(ml_dtypes.bfloat16)
        in_maps.append(m)
    return in_maps


_CACHE = {}


def run(cfg, inputs):
    assert cfg.cores_per_seq == 2, "kernel assumes two cores per sequence"
    key = (cfg.D, cfg.SEQ, cfg.BATCH, cfg.n_cores)
    if key not in _CACHE:
        _CACHE[key] = build(cfg)
    nc = _CACHE[key]
    in_maps = prepare_inputs(cfg, **inputs)
    res = run_bass_kernel_spmd(nc, in_maps, core_ids=list(range(cfg.n_cores)))
    outs = [np.asarray(r["out"], np.float32) for r in res.results]
    y = np.zeros((cfg.BATCH, cfg.SEQ, cfg.D), np.float32)
    cps = cfg.cores_per_seq
    for core in range(cfg.n_cores):
        b, half = core // cps, core % cps
        y[b, half * cfg.OWN:(half + 1) * cfg.OWN] = outs[core]
    return y


def kernel(**inputs):
    cfg = Cfg()
    return run(cfg, inputs)
```

```python
import numpy as np
import ml_dtypes
from contextlib import ExitStack
import concourse.bass as bass
import concourse.mybir as mybir
from concourse.bass_utils import run_bass_kernel_spmd

F32 = mybir.dt.float32
BF16 = mybir.dt.bfloat16
AF = mybir.ActivationFunctionType
ALU = mybir.AluOpType
EPS = 1e-6


class Cfg:
    def __init__(self, D=4096, SEQ=4096, BATCH=4, CTX=256, GRID_W=64, n_cores=8):
        self.D = D
        self.SEQ = SEQ
        self.BATCH = BATCH
        self.CTX = CTX
        self.GRID_W = GRID_W
        self.n_cores = n_cores
        self.HD = 128
        self.NG = (D // 2) // 128
        self.NQ = (D // 2) // 128
        self.NKV = self.NQ // 4
        self.DFF = ((8 * D // 3 + 255) // 256) * 256
        self.KC = D // 128
        self.FC = self.DFF // 128
        self.NIN = 2 * self.NG + self.NQ + 2 * self.NKV
        self.cores_per_seq = n_cores // BATCH
        self.OWN = SEQ // self.cores_per_seq
        self.HALO = 128
        self.NTA = self.OWN + self.HALO + CTX
        self.NSO = self.OWN // 128
        self.NSA = self.NTA // 128
        self.T = 512
        self.DPW = 1024
        self.NDP = D // self.DPW
        self.ALPHA = (2.0 * 1) ** 0.25


class _Stop(Exception):
    pass


class Sem:
    def __init__(self, h, name):
        self.h = h
        self.name = name
        self.n = 0


class Arena:
    def __init__(self, nc, es, nbytes):
        self.t = es.enter_context(nc.sbuf_tensor("arena", [128, nbytes // 2], BF16))
        self.nbytes = nbytes
        self.off = 0

    def alloc(self, shape, dt):
        n = 1
        for d in shape[1:]:
            n *= d
        size = n * (4 if dt == F32 else 2)
        size = (size + 63) // 64 * 64
        assert self.off + size <= self.nbytes, ("SBUF arena overflow", self.off, size, self.nbytes)
        ap = self.t[0:shape[0], self.off // 2:self.off // 2 + n * (2 if dt == F32 else 1)]
        self.off += size
        if dt == F32:
            ap = ap.bitcast(F32)
        if len(shape) == 3:
            ap = ap.rearrange("p (a b) -> p a b", b=shape[2])
        elif len(shape) == 4:
            ap = ap.rearrange("p (a b c) -> p a b c", b=shape[2], c=shape[3])
        return ap


class Prog:
    ENG = ("pe", "act", "dve", "pool", "sp")

    def __init__(self, nc, es):
        self.nc = nc
        self.es = es
        self.ops = {e: [] for e in self.ENG}
        self.waited = {e: {} for e in self.ENG}
        self.nsem = 0
        self.pool = {"hw": [], "sw": []}
        self.phase_sems = []
        self.prog = {e: self.newsem("prog_" + e, persistent=True) for e in ("pe", "act", "dve", "pool")}

    def newsem(self, name, persistent=False, kind="hw"):
        if not persistent and self.pool[kind]:
            sm = self.pool[kind].pop()
            self.phase_sems.append((kind, sm))
            return sm
        self.nsem += 1
        h = self.es.enter_context(self.nc.semaphore(name + "_%d" % self.nsem))
        sm = Sem(h, name + "_%d" % self.nsem)
        if not persistent:
            self.phase_sems.append((kind, sm))
        return sm

    def release_phase_sems(self):
        for kind, sm in self.phase_sems:
            self.pool[kind].append(sm)
        self.phase_sems = []

    def _deps(self, eng, deps):
        for d in deps or ():
            if d is None:
                continue
            if isinstance(d, list):
                self._deps(eng, d)
                continue
            sem, val = d
            if self.waited[eng].get(sem.name, 0) >= val:
                continue
            self.waited[eng][sem.name] = val
            self.ops[eng].append(("wait", sem, val))

    def op(self, eng, fn, deps=None, sig=True):
        self._deps(eng, deps)
        if sig:
            sem = self.prog[eng]
            sem.n += 1
            self.ops[eng].append(("op", fn, sem, 1))
            return (sem, sem.n)
        self.ops[eng].append(("op", fn, None, 0))
        return None

    def dma(self, eng, fn, sem, deps=None):
        self._deps(eng, deps)
        sem.n += 16
        self.ops[eng].append(("op", fn, sem, 16))
        return (sem, sem.n)

    def dma_group(self, eng, fns, sem, deps=None):
        self._deps(eng, deps)
        for fn in fns:
            sem.n += 16
            self.ops[eng].append(("op", fn, sem, 16))
        return (sem, sem.n)

    def wait(self, eng, deps):
        self._deps(eng, deps)

    def emit(self, block):
        def run(e, lst):
            for it in lst:
                if it[0] == "wait":
                    e.wait_ge(it[1].h, it[2])
                else:
                    ins = it[1](e)
                    if it[2] is not None:
                        ins.then_inc(it[2].h, it[3])

        ops = self.ops

        @block.tensor
        def _(e):
            run(e, ops["pe"])

        @block.scalar
        def _(e):
            run(e, ops["act"])

        @block.vector
        def _(e):
            run(e, ops["dve"])

        @block.gpsimd
        def _(e):
            run(e, ops["pool"])

        @block.sync
        def _(e):
            run(e, ops["sp"])


def build(cfg):
    nc = bass.Bass("TRN2", target_bir_lowering=False)
    D, KC, FC, T, NTA, OWN = cfg.D, cfg.KC, cfg.FC, cfg.T, cfg.NTA, cfg.OWN
    NG, NQ, NKV, NIN = cfg.NG, cfg.NQ, cfg.NKV, cfg.NIN
    NDP, DPW = cfg.NDP, cfg.DPW
    NSA, NSO = cfg.NSA, cfg.NSO
    ALPHA = cfg.ALPHA
    NR = OWN + cfg.HALO

    def din(name, shape, dt=F32):
        return nc.dram_tensor(name, list(shape), dt, kind="ExternalInput").ap()

    def dscr(name, shape, dt=F32):
        return nc.dram_tensor(name, list(shape), dt, kind="Internal").ap()

    xin = din("xin", [NTA, D])
    cT_in = din("cT", [128, KC, 2])
    wada = din("wada", [9 * KC, 128, KC * 128])
    badaT_in = din("badaT", [128, 9 * KC])
    wg = [din("wg%d" % i, [FC, 128, KC * 128]) for i in range(2)]
    wu = [din("wu%d" % i, [FC, 128, KC * 128]) for i in range(2)]
    wd = [din("wd%d" % i, [NDP, FC, 128, DPW]) for i in range(2)]
    win = din("win", [NIN, 128, KC * 128])
    wout = din("wout", [NDP, KC, 128, DPW])
    wsT_in = din("wsT", [128, NG * 128])
    bsp_in = din("bsp", [128, NG * 128])
    sink_in = din("sinkb", [128, NQ])
    gmixT_in = din("gmixT", [128, NG + NQ])
    lng_in = din("lng", [3, 128, D])
    lnb_in = din("lnb", [3, 128, D])
    cos_in = din("cos2", [128, NR])
    sin_in = din("sin2", [128, NR])
    masks_in = din("masks", [4, 128, 512], BF16)
    ident_in = din("ident", [128, 128])
    out = nc.dram_tensor("out", [OWN, D], F32, kind="ExternalOutput").ap()

    hTs = [dscr("hT%d" % i, [KC, 128, NTA], BF16) for i in range(3)]
    Ys = dscr("Yscr", [NTA, D])
    xa = dscr("xa", [NTA, D])
    xmid = dscr("xmid", [OWN, D])
    gbc = dscr("gbc", [4, 128, D])

    import os as _os
    es = ExitStack()
    P = Prog(nc, es)

    arena = Arena(nc, es, 212800)

    def gsb(name, shape, dt):
        return arena.alloc(list(shape), dt)

    ps = es.enter_context(nc.psum_tensor("ps", [128, 8, 512], F32))
    ident = gsb("ident", [128, 128], F32)
    modT = gsb("modT", [128, 9 * KC, 2], F32)
    csT = gsb("csT", [128, KC, 2], BF16)
    badaT = gsb("badaT", [128, 9 * KC], F32)
    cT_f = gsb("cT_f", [128, KC, 2], F32)
    arena_base = [arena.off]
    s_misc = P.newsem("misc", persistent=True)
    t_ident = P.dma("sp", lambda e: e.dma_start(out=ident[:], in_=ident_in), s_misc)
    s_c0 = P.newsem("c0", persistent=True)
    t_c0 = P.dma_group("sp", [lambda e: e.dma_start(out=cT_f[:], in_=cT_in),
                              lambda e: e.dma_start(out=badaT[:], in_=badaT_in)], s_c0)
    t_cs = P.op("act", lambda e: e.activation(out=csT[:], in_=cT_f[:], func=AF.Silu), deps=[t_c0])

    bank_free = [None] * 8
    state = {"all": []}

    arena_peak = [0]

    def phase_barrier():
        arena_peak[0] = max(arena_peak[0], arena.off)
        if _os.environ.get("KVERBOSE"):
            print("phase end: arena off", arena.off, "ops", {k: len(v) for k, v in P.ops.items()})
        toks = [(P.prog[e], P.prog[e].n) for e in ("pe", "act", "dve", "pool") if P.prog[e].n > 0]
        best = {}
        for (sem, val) in state["all"]:
            if sem.name not in best or best[sem.name][1] < val:
                best[sem.name] = (sem, val)
        toks += list(best.values())
        for e in Prog.ENG:
            P.wait(e, toks)
        state["all"] = []
        P.release_phase_sems()

    def rsqrt_chain(out_ap, in_ap, scale, deps):
        t1 = P.op("dve", lambda e: e.tensor_scalar(out=out_ap, in0=in_ap, scalar1=float(scale), scalar2=EPS,
                                                   op0=ALU.mult, op1=ALU.add), deps=deps)
        t2 = P.op("act", lambda e: e.activation(out=out_ap, in_=out_ap, func=AF.Sqrt), deps=[t1])
        t3 = P.op("dve", lambda e: e.reciprocal(out=out_ap, in_=out_ap), deps=[t2])
        return t3

    def track(tok):
        state["all"].append(tok)
        return tok

    class AdaStream:
        def __init__(self):
            self.grp_tok = {}

        def chunk(self, ring, grp, c):
            bank = 4 + (grp % 2)
            cc = grp * KC + c
            slot, t_w = ring.load(wada[cc])
            tk = None
            for k in range(KC):
                last = k == KC - 1
                tk = P.op("pe", lambda e, bank=bank, c=c, slot=slot, k=k, last=last: e.matmul(
                    ps[:, bank, 2 * c:2 * c + 2], lhsT=ring.buf[:, slot, k * 128:(k + 1) * 128],
                    rhs=csT[:, k, :], start=(k == 0), stop=last),
                    deps=[t_w, t_cs, bank_free[bank] if c == 0 else None] if k == 0 else None, sig=last)
            ring.free[slot] = tk
            if c == KC - 1:
                psv = ps[:, bank, 0:2 * KC].rearrange("p (c v) -> p c v", v=2)
                t_mod = None
                for v in range(2):
                    t_mod = P.op("dve", lambda e, grp=grp, v=v, psv=psv: e.tensor_tensor(
                        out=modT[:, grp * KC:(grp + 1) * KC, v], in0=psv[:, :, v],
                        in1=badaT[:, grp * KC:(grp + 1) * KC], op=ALU.add), deps=[tk, t_cs])
                bank_free[bank] = t_mod
                if grp % 3 == 1:
                    g0 = grp * KC
                    t_mod = P.op("dve", lambda e, g0=g0: e.tensor_scalar(
                        out=modT[:, g0:g0 + KC, :], in0=modT[:, g0:g0 + KC, :], scalar1=1.0, scalar2=None,
                        op0=ALU.add), deps=[t_mod])
                self.grp_tok[grp] = t_mod

    ada = AdaStream()

    def ada_plan(groups, ntiles):
        per = [[] for _ in range(ntiles)]
        for i, g in enumerate(groups):
            per[min(ntiles - 1, i * ntiles // max(1, len(groups)))] += [(g, c) for c in range(KC)]
        return per

    def phase0():
        arena.off = arena_base[0]
        ring = GRing(None, "adaring", 4)
        for grp in (0, 1):
            for c in range(KC):
                ada.chunk(ring, grp, c)
        phase_barrier()

    def make_gates(sb, gis):
        gates = [(0, 0, 0.5), (0, 1, 0.5), (1, 0, 1.0), (2, 0, 0.5)]
        ones_f = sb("ones_f", [128, 128], F32)
        dm = sb("dm", [128, 8, 128], F32)
        gs = sb("gs", [128, D], F32)
        s_st = P.newsem("gst")
        t_ones = P.op("dve", lambda e: e.memset(ones_f[:], 1.0))
        dm_free = [None] * 8
        gs_free = None
        di = 0
        toks = []
        for gi in gis:
            s_, v, fac = gates[gi]
            t_ev = None
            for c in range(KC):
                q = c % 4
                bank = 2 + ((c // 4) % 2)
                col = (s_ * 3 + 2) * KC + c
                dmi = di % 8
                di += 1
                t_dm = P.op("dve", lambda e, dmi=dmi, col=col, v=v: e.tensor_scalar(
                    out=dm[:, dmi, :], in0=ident[:], scalar1=modT[:, col, v:v + 1], scalar2=None,
                    op0=ALU.mult), deps=[t_ident, dm_free[dmi]])
                t_mm = P.op("pe", lambda e, bank=bank, q=q, dmi=dmi: e.matmul(
                    ps[:, bank, q * 128:(q + 1) * 128], lhsT=ones_f[:], rhs=dm[:, dmi, :],
                    start=True, stop=True), deps=[t_dm, t_ones, bank_free[bank] if q == 0 else None])
                dm_free[dmi] = t_mm
                if q == 3 or c == KC - 1:
                    c0 = c - q
                    t_ev = P.op("act", lambda e, bank=bank, c0=c0, c=c, q=q, fac=fac: e.activation(
                        out=gs[:, c0 * 128:(c + 1) * 128], in_=ps[:, bank, 0:(q + 1) * 128],
                        func=AF.Identity, scale=fac), deps=[t_mm, gs_free])
                    bank_free[bank] = t_ev
            gs_free = track(P.dma("sp", lambda e, gi=gi: e.dma_start(out=gbc[gi], in_=gs[:]), s_st, deps=[t_ev]))
            toks.append(gs_free)
        return toks

    def lpass(name, nsub, src_y, src_res, gate_of, ln_idx, x_dst, hT_dst, mod_s, v_of, gate_make=()):
        with ExitStack() as ph:
            arena.off = arena_base[0]

            def sb(nm, shape, dt):
                return arena.alloc(list(shape), dt)
            NB = 3
            do_ln = src_y is not None
            t_gates = make_gates(sb, list(gate_make)) if gate_make else []
            rb = sb("rb", [128, NB, D], F32)
            s_ld = P.newsem(name + "ld")
            s_rb = [P.newsem(name + "rb") for _ in range(NB)]
            s_yb = [P.newsem(name + "yb") for _ in range(NB)]
            s_xs = [P.newsem(name + "xs", kind="sw") for _ in range(NB)]
            s_hs = [P.newsem(name + "hs") for _ in range(2)]
            NCH = D // 512
            if do_ln:
                yb = sb("yb", [128, NB, D], F32)
                gidx = sorted(set(gate_of(st) for st in range(nsub)))
                gt = sb("gt", [128, len(gidx), D], F32)
                gn = sb("gn", [128, D], F32)
                bs = sb("bs", [128, D], F32)
                stats = sb("stats", [128, NB, NCH * 6], F32)
                mv = sb("mv", [128, NB, 2], F32)
                rstd = sb("rstd", [128, NB, 1], F32)
                nmr = sb("nmr", [128, NB, 1], F32)
                fns = [(lambda e, i=i, g=g: e.dma_start(out=gt[:, i, :], in_=gbc[g])) for i, g in enumerate(gidx)]
                fns.append(lambda e: e.dma_start(out=gn[:], in_=lng_in[ln_idx]))
                fns.append(lambda e: e.dma_start(out=bs[:], in_=lnb_in[ln_idx]))
                t_consts = [P.dma_group("sp", fns, s_ld, deps=t_gates)]
            if hT_dst is not None:
                ho = sb("ho", [128, 2, KC, 128], BF16)
            rb_free = [None] * NB
            yb_free = [None] * NB
            ho_free = [None] * 2
            loads = {}

            def issue_load(st):
                b = st % NB
                t1 = P.dma("sp", lambda e, st=st, b=b: e.dma_start(out=rb[:, b, :], in_=src_res[st * 128:(st + 1) * 128, :]),
                           s_rb[b], deps=rb_free[b])
                t2 = None
                if do_ln:
                    t2 = P.dma("sp", lambda e, st=st, b=b: e.dma_start(out=yb[:, b, :], in_=src_y[st * 128:(st + 1) * 128, :]),
                               s_yb[b], deps=yb_free[b])
                loads[st] = (t1, t2)

            issue_load(0)
            if nsub > 1:
                issue_load(1)
            for st in range(nsub):
                b = st % NB
                if st + 2 < nsub:
                    issue_load(st + 2)
                t1, t2 = loads[st]
                t_x = t1
                if do_ln:
                    gi = gidx.index(gate_of(st))
                    ta = P.op("dve", lambda e, b=b, gi=gi: e.tensor_tensor(
                        out=yb[:, b, :], in0=yb[:, b, :], in1=gt[:, gi, :], op=ALU.mult), deps=[t2] + t_consts)
                    tb = P.op("dve", lambda e, b=b: e.scalar_tensor_tensor(
                        out=rb[:, b, :], in0=rb[:, b, :], scalar=ALPHA, in1=yb[:, b, :], op0=ALU.mult, op1=ALU.add),
                        deps=[ta, t1])
                    yb_free[b] = [tb]
                    tc = None
                    for ch in range(NCH):
                        tc = P.op("dve", lambda e, b=b, ch=ch: e.bn_stats(
                            stats[:, b, ch * 6:(ch + 1) * 6], rb[:, b, ch * 512:(ch + 1) * 512]), deps=[tb])
                    td = P.op("dve", lambda e, b=b: e.bn_aggr(mv[:, b, :], stats[:, b, :]), deps=[tc])
                    te = rsqrt_chain(rstd[:, b, :], mv[:, b, 1:2], 1.0, [td])
                    tf0 = P.op("dve", lambda e, b=b: e.tensor_scalar(
                        out=nmr[:, b, :], in0=mv[:, b, 0:1], scalar1=rstd[:, b, 0:1], scalar2=-1.0,
                        op0=ALU.mult, op1=ALU.mult), deps=[te])
                    tf = P.op("act", lambda e, b=b: e.activation(
                        out=rb[:, b, :], in_=rb[:, b, :], func=AF.Identity, bias=nmr[:, b, 0:1], scale=rstd[:, b, 0:1]),
                        deps=[tf0])
                    tg = P.op("dve", lambda e, b=b: e.tensor_tensor(
                        out=rb[:, b, :], in0=rb[:, b, :], in1=gn[:], op=ALU.mult), deps=[tf] + t_consts)
                    t_x = P.op("dve", lambda e, b=b: e.tensor_tensor(
                        out=rb[:, b, :], in0=rb[:, b, :], in1=bs[:], op=ALU.add), deps=[tg])
                frees = []
                if x_dst is not None:
                    frees.append(track(P.dma("pool", lambda e, st=st, b=b: e.dma_start(
                        out=x_dst[st * 128:(st + 1) * 128, :], in_=rb[:, b, :]), s_xs[b], deps=[t_x])))
                if hT_dst is not None:
                    v = v_of(st)
                    bh = st % 2
                    t_tp = None
                    t_ev = None
                    for c in range(KC):
                        q = c % 4
                        bank = (c // 4) % 8
                        t_tp = P.op("pe", lambda e, b=b, c=c, q=q, bank=bank: e.transpose(
                            out=ps[:, bank, q * 128:(q + 1) * 128], in_=rb[:, b, c * 128:(c + 1) * 128],
                            identity=ident[:]), deps=[t_x, t_ident, bank_free[bank] if q == 0 else None],
                            sig=(q == 3))
                        if q == 3:
                            for qq in range(4):
                                cc = c - 3 + qq
                                t_ev = P.op("act", lambda e, bh=bh, cc=cc, qq=qq, bank=bank, v=v: e.activation(
                                    out=ho[:, bh, cc, :], in_=ps[:, bank, qq * 128:(qq + 1) * 128], func=AF.Identity,
                                    bias=modT[:, (mod_s * 3 + 0) * KC + cc, v:v + 1],
                                    scale=modT[:, (mod_s * 3 + 1) * KC + cc, v:v + 1]),
                                    deps=[t_tp] + (ho_free[bh] or []))
                            bank_free[bank] = t_ev
                    frees.append(t_tp)
                    t_hs = track(P.dma("sp", lambda e, st=st, bh=bh: e.dma_start(
                        out=hT_dst[:, :, st * 128:(st + 1) * 128].rearrange("c p t -> p c t"), in_=ho[:, bh, :, :]),
                        s_hs[bh], deps=[t_ev]))
                    ho_free[bh] = [t_hs]
                rb_free[b] = frees
            phase_barrier()

    class GRing:
        def __init__(self, ph, name, nslots):
            self.n = nslots
            self.buf = arena.alloc([128, nslots, KC * 128], BF16)
            self.free = [None] * nslots
            self.i = 0
            self.sem = [P.newsem(name, kind="sw") for _ in range(nslots)]

        def load(self, src):
            slot = self.i % self.n
            self.i += 1
            tok = P.dma("pool", lambda e, slot=slot, src=src: e.dma_start(out=self.buf[:, slot, :], in_=src),
                        self.sem[slot], deps=[self.free[slot]])
            return slot, tok

    def gemm_g(ring, slot, t_w, hT, tw, bank, extra_deps):
        tk = None
        for k in range(KC):
            last = k == KC - 1
            tk = P.op("pe", lambda e, slot=slot, k=k, bank=bank, last=last: e.matmul(
                ps[:, bank, 0:tw], lhsT=ring.buf[:, slot, k * 128:(k + 1) * 128], rhs=hT[:, k, 0:tw],
                start=(k == 0), stop=last), deps=([t_w, bank_free[bank]] + extra_deps) if k == 0 else None, sig=last)
        ring.free[slot] = tk
        return tk

    def gemm_d(ph, name, aT, nchunks, wsrc, tw, y_dst_rows, dring, ybuf, yb_state, s_y, a_ready):
        S = tw // 128
        GC = 2
        for dp in range(NDP):
            tk = None
            for c0 in range(0, nchunks, GC):
                gc = min(GC, nchunks - c0)
                slot = dring["i"] % dring["n"]
                dring["i"] += 1
                t_w = P.dma("pool", lambda e, slot=slot, dp=dp, c0=c0, gc=gc: e.dma_start(
                    out=dring["buf"][:, slot, 0:gc, :], in_=wsrc[dp, c0:c0 + gc].rearrange("g p n -> p g n")),
                    dring["sem"][slot], deps=[dring["free"][slot]])
                for cl in range(gc):
                    c = c0 + cl
                    for s in range(S):
                        for hf in range(2):
                            bank = s * 2 + hf
                            first = c == 0
                            last = c == nchunks - 1
                            deps = None
                            if cl == 0 and s == 0 and hf == 0:
                                deps = [t_w] + a_ready
                            if first:
                                deps = (deps or []) + [bank_free[bank]]
                            sig = (cl == gc - 1 and s == S - 1 and hf == 1)
                            tk = P.op("pe", lambda e, bank=bank, c=c, s=s, slot=slot, cl=cl, hf=hf, first=first, last=last: e.matmul(
                                ps[:, bank, :], lhsT=aT[:, c, s * 128:(s + 1) * 128],
                                rhs=dring["buf"][:, slot, cl, hf * 512:(hf + 1) * 512], start=first, stop=last),
                                deps=deps, sig=sig)
                dring["free"][slot] = tk
            for s in range(S):
                yi = yb_state["i"] % len(yb_state["free"])
                yb_state["i"] += 1
                eng = "act" if s % 2 == 0 else "dve"
                if eng == "act":
                    t_ev = P.op("act", lambda e, yi=yi, s=s: e.activation(
                        out=ybuf[:, yi, :], in_=ps[:, 2 * s:2 * s + 2, :].rearrange("p a b -> p (a b)"), func=AF.Copy),
                        deps=[tk, yb_state["free"][yi]])
                else:
                    t_ev = P.op("dve", lambda e, yi=yi, s=s: e.tensor_copy(
                        out=ybuf[:, yi, :], in_=ps[:, 2 * s:2 * s + 2, :].rearrange("p a b -> p (a b)")),
                        deps=[tk, yb_state["free"][yi]])
                bank_free[2 * s] = t_ev
                bank_free[2 * s + 1] = t_ev
                r0 = y_dst_rows + s * 128
                yb_state["free"][yi] = track(P.dma("sp", lambda e, yi=yi, r0=r0, dp=dp: e.dma_start(
                    out=Ys[r0:r0 + 128, dp * DPW:(dp + 1) * DPW], in_=ybuf[:, yi, :]), s_y[yi], deps=[t_ev]))

    def make_dring(ph, name, n=4):
        return {"buf": arena.alloc([128, n, 2, DPW], BF16), "n": n, "i": 0,
                "free": [None] * n, "sem": [P.newsem(name, kind="sw") for _ in range(n)]}

    def ffn(name, tiles, hT_src, wg_, wu_, wd_, ada_groups=()):
        with ExitStack() as ph:
            arena.off = arena_base[0]

            def sb(nm, shape, dt):
                return arena.alloc(list(shape), dt)
            hT = sb("hT", [128, KC, T], BF16)
            actT = sb("actT", [128, FC, T], BF16)
            ring = GRing(ph, name + "ring", 5)
            dring = make_dring(ph, name + "dring")
            sgt = sb("sgt", [128, 2, T], F32)
            ybuf = sb("ybuf", [128, 4, DPW], F32)
            yb_state = {"i": 0, "free": [None] * 4}
            s_h = P.newsem(name + "h")
            s_y = [P.newsem(name + "y") for _ in range(4)]
            hT_free = None
            sgt_free = [None, None]
            t_h = None

            def load_h(t0, tw):
                return P.dma("sp", lambda e, t0=t0, tw=tw: e.dma_start(
                    out=hT[:, :, 0:tw], in_=hT_src[:, :, t0:t0 + tw].rearrange("c p t -> p c t")), s_h,
                    deps=[hT_free])

            t_h = load_h(*tiles[0])
            plan = ada_plan(list(ada_groups), len(tiles))
            for ti, (t0, tw) in enumerate(tiles):
                ev_toks = []
                tk = None
                pend = list(plan[ti])
                for j in range(FC):
                    pb = j % 2
                    npump = -(-len(pend) // (FC - j)) if j % 2 == 0 or len(pend) >= (FC - j) else 0
                    for _ in range(npump):
                        ada.chunk(ring, *pend.pop(0))
                    slot_g, tw_g = ring.load(wg_[j])
                    slot_u, tw_u = ring.load(wu_[j])
                    tkg = gemm_g(ring, slot_g, tw_g, hT, tw, 2 * pb, [t_h])
                    tk = gemm_g(ring, slot_u, tw_u, hT, tw, 2 * pb + 1, [t_h])
                    t_s = P.op("act", lambda e, pb=pb, tw=tw: e.activation(
                        out=sgt[:, pb, 0:tw], in_=ps[:, 2 * pb, 0:tw], func=AF.Silu), deps=[tkg, sgt_free[pb]])
                    t_m = P.op("dve", lambda e, pb=pb, j=j, tw=tw: e.tensor_tensor(
                        out=actT[:, j, 0:tw], in0=sgt[:, pb, 0:tw], in1=ps[:, 2 * pb + 1, 0:tw], op=ALU.mult),
                        deps=[t_s, tk])
                    sgt_free[pb] = t_m
                    bank_free[2 * pb] = t_m
                    bank_free[2 * pb + 1] = t_m
                    ev_toks = [t_m] if j == FC - 1 else ev_toks
                hT_free = tk
                if ti + 1 < len(tiles):
                    t_h = load_h(*tiles[ti + 1])
                gemm_d(ph, name, actT, FC, wd_, tw, t0, dring, ybuf, yb_state, s_y, ev_toks + [bank_free[2], bank_free[0]])
            phase_barrier()

    def mixer():
        with ExitStack() as ph:
            arena.off = arena_base[0]

            def sb(nm, shape, dt):
                return arena.alloc(list(shape), dt)
            KT = sb("KT", [128, NKV, NTA], BF16)
            V = sb("V", [128, NSA, NKV * 128], BF16)
            hT = sb("hT", [128, KC, T], BF16)
            uT = sb("uT", [128, NG, T], BF16)
            vn = sb("vn", [128, 4, NG * 128], BF16)
            qT = sb("qT", [128, NQ, T], BF16)
            ring = GRing(ph, "mxring", 3)
            dring = make_dring(ph, "mxdring", 3)
            ybuf = sb("ybuf", [128, 2, DPW], F32)
            yb_state = {"i": 0, "free": [None] * 2}
            tmp = sb("tmp", [128, 4, 512], F32)
            PT = sb("PT", [128, 3, 512], BF16)
            sq = sb("sq", [128, 2, 512], BF16)
            cosb = sb("cosb", [128, 512], F32)
            sinb = sb("sinb", [128, 512], F32)
            masks = sb("masks", [128, 4, 512], BF16)
            bsp = sb("bsp", [128, NG * 128], F32)
            wsT = sb("wsT", [128, NG * 128], BF16)
            esink = sb("esink", [128, NQ], F32)
            gmixT = sb("gmixT", [128, NG + NQ], F32)
            ones_b = sb("ones_b", [128, 128], BF16)
            lsum = sb("lsum", [128, 512], F32)
            ssum = sb("ssum", [128, 2, 128], F32)
            rs = sb("rs", [128, 2, 128], F32)
            NVC = max(1, NG * 128 // 512)
            vst = sb("vst", [128, 4, NVC * 6], F32)
            vmv = sb("vmv", [128, 4, 2], F32)
            vrs = sb("vrs", [128, 4, 1], F32)
            s_c = P.newsem("mxc")
            s_h = P.newsem("mxh")
            s_y = [P.newsem("mxy") for _ in range(4)]
            s_t = P.newsem("mxt")
            fns = [(lambda e, k_=k_: e.dma_start(out=masks[:, k_, :], in_=masks_in[k_])) for k_ in range(4)]
            fns.append(lambda e: e.dma_start(out=bsp[:], in_=bsp_in))
            fns.append(lambda e: e.dma_start(out=esink[:], in_=sink_in))
            fns.append(lambda e: e.dma_start(out=gmixT[:], in_=gmixT_in))
            tc_ = [P.dma_group("sp", fns, s_c)]
            t_k1 = P.op("act", lambda e: e.activation(out=esink[:], in_=esink[:], func=AF.Exp), deps=tc_)
            s_ws = P.newsem("mxws", kind="sw")
            t_k2 = P.dma("pool", lambda e: e.dma_start(out=wsT[:], in_=wsT_in), s_ws)
            t_k3 = P.op("dve", lambda e: e.memset(ones_b[:], 1.0))
            consts = tc_ + [t_k1, t_k2, t_k3]
            chk(10)
            SC = 1.0 / float(np.sqrt(128.0))
            tmp_free = [None] * 4
            tmp_i = [0]

            def tmp_slot():
                i = tmp_i[0] % 4
                tmp_i[0] += 1
                return i

            hT_free = [None]

            def load_h(t0, tw):
                return P.dma("sp", lambda e, t0=t0, tw=tw: e.dma_start(
                    out=hT[:, :, 0:tw], in_=hTs[1][:, :, t0:t0 + tw].rearrange("c p t -> p c t")), s_h,
                    deps=hT_free[0])

            rope_free = [None]

            def load_rope(t0, tw):
                d = rope_free[0]
                a = P.dma_group("sp", [lambda e, t0=t0, tw=tw: e.dma_start(out=cosb[:, 0:tw], in_=cos_in[:, t0:t0 + tw]),
                                       lambda e, t0=t0, tw=tw: e.dma_start(out=sinb[:, 0:tw], in_=sin_in[:, t0:t0 + tw])],
                                s_t, deps=d)
                return [a]

            MXVAR = int(_os.environ.get("MXVAR", "0"))

            def rope_evac(bank, tw_r, dst, t_mm, t_rope):
                if MXVAR == 1:
                    t = P.op("act", lambda e: e.activation(out=dst, in_=ps[:, bank, 0:tw_r], func=AF.Copy), deps=[t_mm])
                    return t, [t]
                if MXVAR == 2:
                    i0 = tmp_slot()
                    ta = P.op("act", lambda e: e.activation(out=tmp[0:64, i0, 0:tw_r], in_=ps[64:128, bank, 0:tw_r], func=AF.Copy),
                              deps=[t_mm, tmp_free[i0]])
                    tb = P.op("act", lambda e: e.activation(out=tmp[64:128, i0, 0:tw_r], in_=ps[0:64, bank, 0:tw_r], func=AF.Copy),
                              deps=[t_mm])
                    t = P.op("act", lambda e: e.activation(out=dst, in_=tmp[:, i0, 0:tw_r], func=AF.Copy), deps=[ta, tb])
                    tmp_free[i0] = t
                    return t, [t]
                if MXVAR == 3:
                    i0 = tmp_slot()
                    ta = P.op("dve", lambda e: e.tensor_tensor(out=tmp[:, i0, 0:tw_r], in0=ps[:, bank, 0:tw_r], in1=cosb[:, 0:tw_r],
                                                             op=ALU.mult), deps=[t_mm, tmp_free[i0]] + t_rope)
                    t = P.op("pool", lambda e: e.tensor_tensor(out=tmp[:, i0, 0:tw_r], in0=tmp[:, i0, 0:tw_r], in1=sinb[:, 0:tw_r],
                                                            op=ALU.mult), deps=[ta] + t_rope)
                    t2 = P.op("dve", lambda e: e.tensor_copy(out=dst, in_=tmp[:, i0, 0:tw_r]), deps=[t])
                    tmp_free[i0] = t2
                    return t2, [ta]
                i0 = tmp_slot()
                i1 = tmp_slot()
                ta = P.op("act", lambda e: e.activation(out=tmp[0:64, i0, 0:tw_r], in_=ps[64:128, bank, 0:tw_r], func=AF.Copy),
                          deps=[t_mm, tmp_free[i0]])
                tb = P.op("act", lambda e: e.activation(out=tmp[64:128, i0, 0:tw_r], in_=ps[0:64, bank, 0:tw_r], func=AF.Copy),
                          deps=[t_mm])
                tc2 = P.op("dve", lambda e: e.tensor_tensor(out=tmp[:, i1, 0:tw_r], in0=ps[:, bank, 0:tw_r], in1=cosb[:, 0:tw_r],
                                                             op=ALU.mult), deps=[t_mm, tb, tmp_free[i1]] + t_rope)
                td = P.op("dve", lambda e: e.tensor_tensor(out=tmp[:, i0, 0:tw_r], in0=tmp[:, i0, 0:tw_r], in1=sinb[:, 0:tw_r],
                                                            op=ALU.mult), deps=[ta, tb] + t_rope + ([tc2] if MXVAR == 5 else []))
                if MXVAR == 6:
                    te0 = P.op("dve", lambda e: e.tensor_tensor(out=tmp[:, i1, 0:tw_r], in0=tmp[:, i1, 0:tw_r], in1=tmp[:, i0, 0:tw_r], op=ALU.add),
                               deps=[tc2, td])
                    te = P.op("dve", lambda e: e.tensor_copy(out=dst, in_=tmp[:, i1, 0:tw_r]), deps=[te0])
                elif MXVAR == 7:
                    te = P.op("dve", lambda e: e.tensor_copy(out=dst, in_=tmp[:, i1, 0:tw_r]), deps=[tc2, td])
                else:
                    te = P.op("dve", lambda e: e.tensor_tensor(out=dst, in0=tmp[:, i1, 0:tw_r], in1=tmp[:, i0, 0:tw_r], op=ALU.add),
                              deps=[tc2, td])
                tmp_free[i0] = te
                tmp_free[i1] = te
                return te, [tb, tc2]

            def transpose_to(src_f32, nsub_, dsts, t_src, bank):
                t_tp = None
                for s in range(nsub_):
                    t_tp = P.op("pe", lambda e, s=s: e.transpose(out=ps[:, bank, s * 128:(s + 1) * 128],
                                                                 in_=src_f32[:, s * 128:(s + 1) * 128], identity=ident[:]),
                                deps=[t_src, t_ident, bank_free[bank] if s == 0 else None], sig=(s == nsub_ - 1))
                t_ev = None
                for s in range(nsub_):
                    t_ev = P.op("dve", lambda e, s=s: e.tensor_copy(out=dsts[s], in_=ps[:, bank, s * 128:(s + 1) * 128]),
                                deps=[t_tp])
                bank_free[bank] = t_ev
                return t_tp, t_ev

            tiles_all = [(t0, min(T, NTA - t0)) for t0 in range(0, NTA, T)]
            t_h = load_h(*tiles_all[0])
            kv_ready = []
            for ti, (t0, tw) in enumerate(tiles_all):
                tw_r = max(0, min(tw, NR - t0))
                t_rope = load_rope(t0, tw_r) if tw_r > 0 else []
                tk = None
                readers = []
                for jj in range(2 * NKV):
                    j = 2 * NG + NQ + jj
                    bank = jj % 4
                    slot, t_w = ring.load(win[j])
                    tk = gemm_g(ring, slot, t_w, hT, tw, bank, [t_h])
                    if pend_tp[0] is not None:
                        pend_tp[0]()
                        pend_tp[0] = None
                    if jj == 1:
                        chk(11)
                    if jj < NKV:
                        toks = []
                        if tw_r < tw:
                            tcp = P.op("act", lambda e, jj=jj, t0=t0, tw=tw, tw_r=tw_r, bank=bank: e.activation(
                                out=KT[:, jj, t0 + tw_r:t0 + tw], in_=ps[:, bank, tw_r:tw], func=AF.Copy), deps=[tk])
                            toks.append(tcp)
                        if tw_r > 0:
                            te, rd = rope_evac(bank, tw_r, KT[:, jj, t0:t0 + tw_r], tk, t_rope)
                            toks += [te] + rd
                            readers += rd
                        bank_free[bank] = toks
                        kv_ready.append(toks)
                    else:
                        h = jj - NKV
                        i0 = tmp_slot()
                        t_c = P.op("act", lambda e, i0=i0, bank=bank, tw=tw: e.activation(
                            out=tmp[:, i0, 0:tw], in_=ps[:, bank, 0:tw], func=AF.Copy), deps=[tk, tmp_free[i0]])
                        bank_free[bank] = t_c
                        nsub_ = tw // 128
                        dsts = [V[:, t0 // 128 + s, h * 128:(h + 1) * 128] for s in range(nsub_)]

                        def do_tp0(i0=i0, dsts=dsts, t_c=t_c, jj=jj, nsub_=nsub_):
                            t_tp, t_ev = transpose_to(tmp[:, i0, :], nsub_, dsts, t_c, 6 + (jj % 2))
                            tmp_free[i0] = t_tp
                            kv_ready.append(t_ev)
                        pend_tp[0] = do_tp0
                if pend_tp[0] is not None:
                    pend_tp[0]()
                    pend_tp[0] = None
                chk(12)
                rope_free[0] = readers + [bank_free[b_] for b_ in range(4)]
                hT_free[0] = [tk]
                if ti + 1 < len(tiles_all):
                    t_h = load_h(*tiles_all[ti + 1])

            chk(0)
            tiles_own = [(t0, min(T, OWN - t0)) for t0 in range(0, OWN, T)]
            t_h = load_h(*tiles_own[0])
            yT = hT
            plan_mx = ada_plan([7], len(tiles_own))
            for ti, (t0, tw) in enumerate(tiles_own):
                S = tw // 128
                t_rope = load_rope(t0, tw)
                pend = list(plan_mx[ti])
                readers = []
                vn_ready = []
                u_ready = []
                q_ready = []
                tk = None
                NJ = 2 * NG + NQ
                for j in range(NJ):
                    bank = j % 4
                    for _ in range(-(-len(pend) // (NJ - j))):
                        ada.chunk(ring, *pend.pop(0))
                    slot, t_w = ring.load(win[j])
                    tk = gemm_g(ring, slot, t_w, hT, tw, bank, [t_h])
                    if pend_tp[0] is not None:
                        pend_tp[0]()
                        pend_tp[0] = None
                    if j < NG:
                        t_e = P.op("act", lambda e, j=j, bank=bank, tw=tw: e.activation(
                            out=uT[:, j, 0:tw], in_=ps[:, bank, 0:tw], func=AF.Gelu), deps=[tk])
                        bank_free[bank] = t_e
                        u_ready.append(t_e)
                    elif j < 2 * NG:
                        g = j - NG
                        i0 = tmp_slot()
                        t_c = P.op("act", lambda e, i0=i0, bank=bank, tw=tw: e.activation(
                            out=tmp[:, i0, 0:tw], in_=ps[:, bank, 0:tw], func=AF.Gelu), deps=[tk, tmp_free[i0]])
                        bank_free[bank] = t_c
                        dsts = [vn[:, s, g * 128:(g + 1) * 128] for s in range(S)]

                        def do_tp(i0=i0, dsts=dsts, t_c=t_c, j=j, S=S):
                            t_tp, t_ev = transpose_to(tmp[:, i0, :], S, dsts, t_c, 6 + (j % 2))
                            tmp_free[i0] = t_tp
                            vn_ready.append(t_ev)
                        pend_tp[0] = do_tp
                    else:
                        hq = j - 2 * NG
                        te, rd = rope_evac(bank, tw, qT[:, hq, 0:tw], tk, t_rope)
                        readers += rd
                        bank_free[bank] = [te] + rd
                        q_ready.append(te)
                if pend_tp[0] is not None:
                    pend_tp[0]()
                    pend_tp[0] = None
                rope_free[0] = readers + [bank_free[b_] for b_ in range(4)]
                chk(1)
                for s in range(S):
                    t1 = None
                    VW = NG * 128 // NVC
                    for ch in range(NVC):
                        t1 = P.op("dve", lambda e, s=s, ch=ch: e.bn_stats(
                            vst[:, s, ch * 6:(ch + 1) * 6], vn[:, s, ch * VW:(ch + 1) * VW]), deps=vn_ready)
                    t2 = P.op("dve", lambda e, s=s: e.bn_aggr(vmv[:, s, :], vst[:, s, :]), deps=[t1])
                    t3 = rsqrt_chain(vrs[:, s, :], vmv[:, s, 1:2], 1.0, [t2])
                    t4 = P.op("dve", lambda e, s=s: e.tensor_scalar(
                        out=vn[:, s, :], in0=vn[:, s, :], scalar1=vmv[:, s, 0:1], scalar2=vrs[:, s, 0:1],
                        op0=ALU.subtract, op1=ALU.mult), deps=[t3])
                    vn_ready.append(t4)
                chk(2)
                y_wr = [tk]
                SSB = 6
                n4 = (NG + 3) // 4
                msteps = [(s_, g4) for s_ in range(S) for g4 in range(n4)]
                ypart = {}

                def mlp_final(s_, t_ssm):
                    wl = min(4, NG)
                    tr = P.op("dve", lambda e, wl=wl: e.tensor_reduce(
                        out=ssum[:, 0, :], in_=ps[:, SSB, 0:wl * 128].rearrange("p (a b) -> p b a", b=128),
                        axis=mybir.AxisListType.X, op=ALU.add), deps=[t_ssm, ss_free[0]])
                    bank_free[SSB] = tr
                    tr3 = rsqrt_chain(rs[:, 0, :], ssum[:, 0, :], 1.0 / (NG * 128), [tr, rs_free[0]])
                    ss_free[0] = tr3
                    tl = None
                    for g in range(NG):
                        tl = P.op("dve", lambda e, g=g, s_=s_: e.scalar_tensor_tensor(
                            out=yT[:, g, s_ * 128:(s_ + 1) * 128], in0=yT[:, g, s_ * 128:(s_ + 1) * 128],
                            scalar=gmixT[:, g:g + 1], in1=rs[:, 0, :], op0=ALU.mult, op1=ALU.mult),
                            deps=[tr3] + ypart[("m", s_)] + consts)
                    rs_free[0] = tl
                    y_done.append(tl)

                def mlp_mixed(i):
                    s_, g4 = msteps[i]
                    g0 = g4 * 4
                    ng_ = min(4, NG - g0)
                    bank = 2 + (i % 2)
                    t_mm = None
                    for gg in range(ng_):
                        g = g0 + gg
                        t_mm = P.op("pe", lambda e, g=g, gg=gg, s_=s_, bank=bank: e.matmul(
                            ps[:, bank, gg * 128:(gg + 1) * 128], lhsT=vn[:, s_, g * 128:(g + 1) * 128],
                            rhs=wsT[:, g * 128:(g + 1) * 128], start=True, stop=True),
                            deps=(vn_ready + consts + [bank_free[bank]]) if gg == 0 else None, sig=(gg == ng_ - 1))
                    ti0 = tmp_slot()
                    w_ = ng_ * 128
                    ta = P.op("dve", lambda e, ti0=ti0, bank=bank, g0=g0, w_=w_: e.tensor_tensor(
                        out=tmp[:, ti0, 0:w_], in0=ps[:, bank, 0:w_], in1=bsp[:, g0 * 128:g0 * 128 + w_], op=ALU.add),
                        deps=[t_mm, tmp_free[ti0]])
                    bank_free[bank] = ta
                    tb = P.op("dve", lambda e, ti0=ti0, g0=g0, ng_=ng_, s_=s_: e.tensor_tensor(
                        out=tmp[:, ti0, 0:ng_ * 128].rearrange("p (a b) -> p a b", b=128),
                        in0=tmp[:, ti0, 0:ng_ * 128].rearrange("p (a b) -> p a b", b=128),
                        in1=uT[:, g0:g0 + ng_, s_ * 128:(s_ + 1) * 128], op=ALU.mult), deps=[ta] + u_ready)
                    sqi = sq_i[0] % 2
                    sq_i[0] += 1
                    tcq = P.op("act", lambda e, ti0=ti0, sqi=sqi, w_=w_: e.activation(
                        out=sq[:, sqi, 0:w_], in_=tmp[:, ti0, 0:w_], func=AF.Square), deps=[tb, sq_free[sqi]])
                    tdq = P.op("pool", lambda e, ti0=ti0, g0=g0, ng_=ng_, s_=s_: e.tensor_copy(
                        out=yT[:, g0:g0 + ng_, s_ * 128:(s_ + 1) * 128],
                        in_=tmp[:, ti0, 0:ng_ * 128].rearrange("p (a b) -> p a b", b=128)), deps=[tb] + y_wr)
                    tmp_free[ti0] = tdq
                    ypart.setdefault(("m", s_), []).append(tdq)
                    return (sqi, w_, tcq)

                def mlp_ss(i, st_):
                    s_, g4 = msteps[i]
                    sqi, w_, tcq = st_
                    t_ssm = P.op("pe", lambda e, sqi=sqi, w_=w_, g4=g4: e.matmul(
                        ps[:, SSB, 0:w_], lhsT=ones_b[:], rhs=sq[:, sqi, 0:w_], start=(g4 == 0), stop=(g4 == n4 - 1)),
                        deps=[tcq, consts[-1], bank_free[SSB] if g4 == 0 else None])
                    sq_free[sqi] = t_ssm
                    if g4 == n4 - 1:
                        mlp_final(s_, t_ssm)

                prev = None
                for i in range(len(msteps)):
                    cur = mlp_mixed(i)
                    if prev is not None:
                        mlp_ss(i - 1, prev)
                    prev = cur
                mlp_ss(len(msteps) - 1, prev)
                chk(3)
                def kbs_of(s_):
                    qb = t0 // 128 + s_
                    kbs = [(NR // 128 + cb, None) for cb in range(cfg.CTX // 128)]
                    kbs.append((OWN // 128, 2) if qb == 0 else (qb - 1, 0))
                    kbs.append((qb, None))
                    kbs.append((OWN // 128, 3) if qb == NSO - 1 else (qb + 1, 1))
                    return kbs

                groups = [(s_, h, kbs_of(s_)) for s_ in range(S) for h in range(NKV)]
                asteps = [(gi, ki) for gi, g_ in enumerate(groups) for ki in range(len(g_[2]))]
                SBK = [0, 1, 6]
                LA = 2
                exp_tok = {}
                pend_ss = [None]

                def att_S(i):
                    gi, ki = asteps[i]
                    s_, h, kbs = groups[gi]
                    kb, mk = kbs[ki]
                    bank = SBK[i % 3]
                    t_s = P.op("pe", lambda e, h=h, kb=kb, s_=s_, bank=bank: e.matmul(
                        ps[:, bank, :].rearrange("p (a b) -> p a b", b=128),
                        lhsT=KT[:, h, kb * 128:(kb + 1) * 128], rhs=qT[:, 4 * h:4 * h + 4, s_ * 128:(s_ + 1) * 128],
                        start=True, stop=True), deps=kv_ready + q_ready + [bank_free[bank]])
                    pi = pt_i[0] % 3
                    pt_i[0] += 1
                    t_e = P.op("act", lambda e, pi=pi, bank=bank: e.activation(
                        out=PT[:, pi, :], in_=ps[:, bank, :], func=AF.Exp, scale=SC), deps=[t_s, pt_free[pi]])
                    bank_free[bank] = t_e
                    if mk is not None:
                        t_e = P.op("pool", lambda e, pi=pi, mk=mk: e.tensor_tensor(
                            out=PT[:, pi, :], in0=PT[:, pi, :], in1=masks[:, mk, :], op=ALU.mult), deps=[t_e] + consts)
                    exp_tok[i] = (t_e, pi)

                def att_ss(gi, sqi, tcq):
                    s_, h, kbs = groups[gi]
                    t_ssa = P.op("pe", lambda e, sqi=sqi, h=h: e.matmul(
                        ps[:, 7, :], lhsT=ones_b[:], rhs=sq[:, sqi, :], start=(h == 0), stop=(h == NKV - 1)),
                        deps=[tcq, bank_free[7] if h == 0 else None])
                    sq_free[sqi] = t_ssa
                    if h == NKV - 1:
                        tr = P.op("dve", lambda e: e.tensor_reduce(
                            out=ssum[:, 1, :], in_=ps[:, 7, :].rearrange("p (a b) -> p b a", b=128),
                            axis=mybir.AxisListType.X, op=ALU.add), deps=[t_ssa, ss_free[1]])
                        bank_free[7] = tr
                        tr3 = rsqrt_chain(rs[:, 1, :], ssum[:, 1, :], 1.0 / (NQ * 128), [tr, rs_free[1]])
                        ss_free[1] = tr3
                        tl = None
                        for hq in range(NQ):
                            tl = P.op("dve", lambda e, hq=hq, s_=s_: e.scalar_tensor_tensor(
                                out=yT[:, NG + hq, s_ * 128:(s_ + 1) * 128], in0=yT[:, NG + hq, s_ * 128:(s_ + 1) * 128],
                                scalar=gmixT[:, NG + hq:NG + hq + 1], in1=rs[:, 1, :], op0=ALU.mult, op1=ALU.mult),
                                deps=[tr3] + ypart[("a", s_)] + consts)
                        rs_free[1] = tl
                        y_done.append(tl)

                def att_PV(i):
                    gi, ki = asteps[i]
                    s_, h, kbs = groups[gi]
                    kb, mk = kbs[ki]
                    OB, LB = (4, 5) if gi % 2 == 0 else (2, 3)
                    t_e, pi = exp_tok.pop(i)
                    first = ki == 0
                    last = ki == len(kbs) - 1
                    P.op("pe", lambda e, pi=pi, kb=kb, h=h, first=first, last=last: e.matmul(
                        ps[:, OB, :], lhsT=V[:, kb, h * 128:(h + 1) * 128], rhs=PT[:, pi, :], start=first, stop=last),
                        deps=[t_e, bank_free[OB] if first else None], sig=False)
                    t_l = P.op("pe", lambda e, pi=pi, first=first, last=last: e.matmul(
                        ps[:, LB, :], lhsT=ones_b[:], rhs=PT[:, pi, :], start=first, stop=last),
                        deps=[bank_free[LB] if first else None])
                    pt_free[pi] = t_l
                    if not last:
                        return
                    tn = None
                    for g in range(4):
                        tn = P.op("dve", lambda e, g=g, h=h, LB=LB: e.tensor_scalar(
                            out=lsum[:, g * 128:(g + 1) * 128], in0=ps[:, LB, g * 128:(g + 1) * 128],
                            scalar1=esink[:, 4 * h + g:4 * h + g + 1], scalar2=None, op0=ALU.add),
                            deps=[t_l, ls_free[0]] + consts)
                    bank_free[LB] = tn
                    tn2 = P.op("dve", lambda e: e.reciprocal(out=lsum[:], in_=lsum[:]), deps=[tn])
                    ti0 = tmp_slot()
                    tn3 = P.op("dve", lambda e, ti0=ti0, OB=OB: e.tensor_tensor(
                        out=tmp[:, ti0, :], in0=ps[:, OB, :], in1=lsum[:], op=ALU.mult), deps=[tn2, tmp_free[ti0]])
                    bank_free[OB] = tn3
                    ls_free[0] = tn3
                    sqi = sq_i[0] % 2
                    sq_i[0] += 1
                    tcq = P.op("act", lambda e, ti0=ti0, sqi=sqi: e.activation(
                        out=sq[:, sqi, :], in_=tmp[:, ti0, :], func=AF.Square), deps=[tn3, sq_free[sqi]])
                    tdq = P.op("pool", lambda e, ti0=ti0, h=h, s_=s_: e.tensor_copy(
                        out=yT[:, NG + 4 * h:NG + 4 * h + 4, s_ * 128:(s_ + 1) * 128],
                        in_=tmp[:, ti0, :].rearrange("p (a b) -> p a b", b=128)), deps=[tn3] + y_wr)
                    tmp_free[ti0] = tdq
                    ypart.setdefault(("a", s_), []).append(tdq)
                    if pend_ss[0] is not None:
                        att_ss(*pend_ss[0])
                    pend_ss[0] = (gi, sqi, tcq)

                na = len(asteps)
                for i in range(min(LA, na)):
                    att_S(i)
                for i in range(na):
                    if i + LA < na:
                        att_S(i + LA)
                    att_PV(i)
                if pend_ss[0] is not None:
                    att_ss(*pend_ss[0])
                    pend_ss[0] = None
                chk(4)
                gemm_d(ph, "mx", yT, KC, wout, tw, t0, dring, ybuf, yb_state, s_y, list(y_done) + [bank_free[b_] for b_ in range(8)])
                del y_done[:]
                hT_free[0] = [(P.prog["pe"], P.prog["pe"].n)]
                if ti + 1 < len(tiles_own):
                    t_h = load_h(*tiles_own[ti + 1])
            phase_barrier()

    def mixer_wrap():
        try:
            mixer()
        except _Stop:
            phase_barrier()

    MX = int(_os.environ.get("MXSTOP", "99"))

    def chk(n):
        if MX == n:
            raise _Stop()

    sq_free = [None, None]
    ss_free = [None, None]
    rs_free = [None, None]
    ls_free = [None]
    pt_free = [None, None, None]
    pt_i = [0]
    pend_tp = [None]
    sq_i = [0]
    y_parts = []
    y_done = []

    tiles_a = [(t0, min(T, NTA - t0)) for t0 in range(0, NTA, T)]
    tiles_c = [(t0, min(T, OWN - t0)) for t0 in range(0, OWN, T)]
    ctx_st0 = (OWN + cfg.HALO) // 128
    v_of_a = lambda st: 1 if st >= ctx_st0 else 0

    import os
    kstop = int(os.environ.get("KSTOP", "99"))
    steps = [
        lambda: phase0(),
        lambda: lpass("p1", NSA, None, xin, None, None, None, hTs[0], 0, v_of_a),
        lambda: ffn("fa", tiles_a, hTs[0], wg[0], wu[0], wd[0], ada_groups=(2, 3, 4, 5, 6)),
        lambda: lpass("p3", NSA, Ys, xin, lambda st: v_of_a(st), 0, xa, hTs[1], 1, v_of_a, gate_make=(0, 1)),
        lambda: mixer_wrap(),
        lambda: lpass("p6", NSO, Ys, xa, lambda st: 2, 1, xmid, hTs[2], 2, lambda st: 0, gate_make=(2,)),
        lambda: ffn("fb", tiles_c, hTs[2], wg[1], wu[1], wd[1], ada_groups=(8,)),
        lambda: lpass("p8", NSO, Ys, xmid, lambda st: 3, 2, out, None, None, None, gate_make=(3,)),
    ]
    for i_, st_ in enumerate(steps):
        if i_ <= kstop:
            st_()

    with nc.Block() as block:
        P.emit(block)
    es.close()
    return nc


def _rope_tables(cfg, pos):
    n_freq = 128 // 4
    inv_freq = (10000.0 ** (-np.arange(n_freq, dtype=np.float32) / n_freq)).astype(np.float32)
    row = (pos // cfg.GRID_W).astype(np.float32)
    col = (pos % cfg.GRID_W).astype(np.float32)
    ang = np.concatenate([row[:, None] * inv_freq, col[:, None] * inv_freq], axis=-1).astype(np.float32)
    cos = np.cos(ang).astype(np.float32).T
    sin = np.sin(ang).astype(np.float32).T
    cos2 = np.concatenate([cos, cos], 0)
    sin2 = np.concatenate([-sin, sin], 0)
    return np.ascontiguousarray(cos2), np.ascontiguousarray(sin2)


def _g_layout(w, KC):
    K, F = w.shape
    return np.ascontiguousarray(w.reshape(KC, 128, F // 128, 128).transpose(2, 1, 0, 3)).reshape(F // 128, 128, KC * 128)


def _d_layout(w, DPW):
    Cn, Dm = w.shape[0] // 128, w.shape[1]
    return np.ascontiguousarray(w.reshape(Cn, 128, Dm // DPW, DPW).transpose(2, 0, 1, 3))


def prepare_inputs(cfg, x, c, ctx, c_ctx, w_ada, b_ada, w_ffn_gate, w_ffn_up, w_ffn_down, w_in, w_spatial,
                   b_spatial, sink_logit, g_mix, w_out, ln_gain, ln_bias):
    D, KC = cfg.D, cfg.KC
    f32 = np.float32
    shared = {}
    shared["wada"] = _g_layout(np.asarray(w_ada[0], f32), KC)
    shared["badaT"] = np.ascontiguousarray(np.asarray(b_ada[0], f32).reshape(9 * KC, 128).T)
    for i in range(2):
        shared["wg%d" % i] = _g_layout(np.asarray(w_ffn_gate[0, i], f32), KC)
        shared["wu%d" % i] = _g_layout(np.asarray(w_ffn_up[0, i], f32), KC)
        shared["wd%d" % i] = _d_layout(np.asarray(w_ffn_down[0, i], f32), cfg.DPW)
    shared["win"] = _g_layout(np.asarray(w_in[0], f32), KC)
    shared["wout"] = _d_layout(np.asarray(w_out[0], f32), cfg.DPW)
    ws = np.asarray(w_spatial[0], f32)
    shared["wsT"] = np.ascontiguousarray(ws.transpose(2, 0, 1)).reshape(128, cfg.NG * 128)
    shared["bsp"] = np.ascontiguousarray(np.broadcast_to(np.asarray(b_spatial[0], f32).reshape(1, -1), (128, cfg.NG * 128)))
    shared["sinkb"] = np.ascontiguousarray(np.broadcast_to(np.asarray(sink_logit[0], f32).reshape(1, -1), (128, cfg.NQ)))
    shared["gmixT"] = np.ascontiguousarray(np.asarray(g_mix[0], f32).reshape(cfg.NG + cfg.NQ, 128).T)
    shared["lng"] = np.ascontiguousarray(np.broadcast_to(np.asarray(ln_gain[0], f32)[:, None, :], (3, 128, D)))
    shared["lnb"] = np.ascontiguousarray(np.broadcast_to(np.asarray(ln_bias[0], f32)[:, None, :], (3, 128, D)))
    shared["ident"] = np.eye(128, dtype=f32)
    kk = np.arange(128)[:, None]
    qq = np.arange(128)[None, :]
    m_prev = np.tile((kk >= qq).astype(f32), (1, 4))
    m_next = np.tile((kk <= qq).astype(f32), (1, 4))
    zero = np.zeros_like(m_prev)
    x = np.asarray(x, f32)
    ctx = np.asarray(ctx, f32)
    c = np.asarray(c, f32)
    c_ctx = np.asarray(c_ctx, f32)
    in_maps = []
    cps = cfg.cores_per_seq
    for core in range(cfg.n_cores):
        b, half = core // cps, core % cps
        o0 = half * cfg.OWN
        m = dict(shared)
        first = half == 0
        lastc = half == cps - 1
        if first:
            h0 = o0 + cfg.OWN
        else:
            h0 = o0 - 128
        m["xin"] = np.ascontiguousarray(np.concatenate([x[b, o0:o0 + cfg.OWN], x[b, h0:h0 + 128], ctx[b]], 0))
        cv = np.stack([c[b], c_ctx], -1)
        m["cT"] = np.ascontiguousarray(cv.reshape(KC, 128, 2).transpose(1, 0, 2))
        pos = np.concatenate([np.arange(o0, o0 + cfg.OWN), np.arange(h0, h0 + 128)])
        m["cos2"], m["sin2"] = _rope_tables(cfg, pos)
        fp = zero if first else m_prev
        ln_ = m_next if first else zero
        m["masks"] = np.stack([m_prev, m_next, fp, ln_], 0).astype(ml_dtypes.bfloat16)
        in_maps.append(m)
    return in_maps


_CACHE = {}


def run(cfg, inputs):
    assert cfg.cores_per_seq == 2, "kernel assumes two cores per sequence"
    key = (cfg.D, cfg.SEQ, cfg.BATCH, cfg.n_cores)
    if key not in _CACHE:
        _CACHE[key] = build(cfg)
    nc = _CACHE[key]
    in_maps = prepare_inputs(cfg, **inputs)
    res = run_bass_kernel_spmd(nc, in_maps, core_ids=list(range(cfg.n_cores)))
    outs = [np.asarray(r["out"], np.float32) for r in res.results]
    y = np.zeros((cfg.BATCH, cfg.SEQ, cfg.D), np.float32)
    cps = cfg.cores_per_seq
    for core in range(cfg.n_cores):
        b, half = core // cps, core % cps
        y[b, half * cfg.OWN:(half + 1) * cfg.OWN] = outs[core]
    return y


def kernel(**inputs):
    cfg = Cfg()
    return run(cfg, inputs)
```
